# Optimizing a Trainium2 kernel written in Bass

```python
import math
import jax
import jax.numpy as jnp
from jax import lax
import numpy as np

D_MODEL = 1024
BATCH = 4
SEQ = 4096
DEPTH = 4
DEC_BATCH = 32
DEC_SEQ = 64
PAST_LEN = 1024

CHUNK = 64
EPS = 1e-6
N_EVEN = (DEPTH + 1) // 2
N_ODD = DEPTH // 2
GDN_HEADS = 4
GDN_DK = 128
GDN_DV = 128
GDN_CONV = 4
GDN_QKV = GDN_HEADS * (2 * GDN_DK + GDN_DV)
GDN_WIDTH = GDN_HEADS * GDN_DV
SC_WIDTH = D_MODEL - GDN_WIDTH
SC_CONV = 3
EVEN_IN = GDN_QKV + GDN_WIDTH + 2 * GDN_HEADS + 3 * SC_WIDTH
FOX_HEADS = 8
FOX_DH = 128
FOX_WIDTH = FOX_HEADS * FOX_DH
FOX_QBLK = 128
ODD_IN = 3 * FOX_WIDTH + FOX_HEADS
D_FF = 256 * ((8 * D_MODEL + 3 * 256 - 1) // (3 * 256))
INIT_FORGET_BIAS = 3.0
INIT_FORGET_W_SCALE = 0.1

kernel_name = 'hybrid_gdn_shortconv_fox_stream_step'


def rms_norm(x, g):
    xf = x.astype(jnp.float32)
    y = xf * lax.rsqrt(jnp.mean(xf * xf, axis=-1, keepdims=True) + EPS)
    return (y * g.astype(jnp.float32)).astype(x.dtype)


def l2_norm(x):
    xf = x.astype(jnp.float32)
    return xf * lax.rsqrt(jnp.sum(xf * xf, axis=-1, keepdims=True) + EPS)


def causal_dwconv(x, buf, w):
    width = w.shape[0]
    T = x.shape[1]
    xp = jnp.concatenate([buf.astype(x.dtype), x], axis=1)
    y = xp[:, 0:T] * w[0]
    for i in range(1, width):
        y = y + xp[:, i:i + T] * w[i]
    return y, xp[:, T:]


def gated_delta_rule(q, k, v, g, beta, S0, chunk):
    B, T, H, DK = q.shape
    DV = v.shape[-1]
    n = T // chunk

    def blocks(a):
        a = a.reshape((B, n, chunk, H) + a.shape[3:])
        return jnp.moveaxis(a, (1, 3), (0, 2))

    qc, kc, vc = blocks(q), blocks(k), blocks(v)
    gc, bc = blocks(g), blocks(beta)
    G = jnp.cumsum(gc, axis=-1)
    idx = jnp.arange(chunk)
    incl = idx[:, None] >= idx[None, :]
    strict = idx[:, None] > idx[None, :]
    decay = jnp.exp(jnp.where(incl, G[..., :, None] - G[..., None, :], -jnp.inf))
    kb = kc * bc[..., None]
    A = jnp.einsum('nbhid,nbhjd->nbhij', kb, kc) * jnp.where(strict, decay, 0.0)
    rhs = jnp.concatenate([vc * bc[..., None], kb * jnp.exp(G)[..., None]], axis=-1)
    sol = lax.linalg.triangular_solve(A + jnp.eye(chunk, dtype=A.dtype), rhs,
                                      left_side=True, lower=True, unit_diagonal=True)
    u_pre, w = sol[..., :DV], sol[..., DV:]
    P = jnp.einsum('nbhid,nbhjd->nbhij', qc, kc) * decay
    qg = qc * jnp.exp(G)[..., None]
    kd = kc * jnp.exp(G[..., -1:] - G)[..., None]
    gC = jnp.exp(G[..., -1])

    def step(S, xs):
        u_pre_i, w_i, P_i, qg_i, kd_i, gC_i = xs
        u = u_pre_i - jnp.einsum('bhck,bhkv->bhcv', w_i, S)
        o = jnp.einsum('bhck,bhkv->bhcv', qg_i, S) + jnp.einsum('bhij,bhjv->bhiv', P_i, u)
        S = S * gC_i[..., None, None] + jnp.einsum('bhck,bhcv->bhkv', kd_i, u)
        return S, o

    S_fin, o = lax.scan(step, S0, (u_pre, w, P, qg, kd, gC))
    o = jnp.moveaxis(o, (0, 2), (1, 3)).reshape(B, T, H, DV)
    return o, S_fin


def gdn_sconv_mixer(h, conv_buf, S0, sc_buf, w_in, conv_w, a_log, dt_bias, norm_g, sc_w, w_out):
    B, T, _ = h.shape
    p = h @ w_in
    o1 = GDN_QKV
    o2 = o1 + GDN_WIDTH
    o3 = o2 + GDN_HEADS
    o4 = o3 + GDN_HEADS
    o5 = o4 + SC_WIDTH
    o6 = o5 + SC_WIDTH
    qkv, z, a, b = p[..., :o1], p[..., o1:o2], p[..., o2:o3], p[..., o3:o4]
    gate_b, gate_c, h_in = p[..., o4:o5], p[..., o5:o6], p[..., o6:]
    if conv_buf is None:
        conv_buf = jnp.zeros((B, GDN_CONV - 1, GDN_QKV), p.dtype)
    if S0 is None:
        S0 = jnp.zeros((B, GDN_HEADS, GDN_DK, GDN_DV), jnp.float32)
    if sc_buf is None:
        sc_buf = jnp.zeros((B, SC_CONV - 1, SC_WIDTH), p.dtype)
    qkv, conv_buf_new = causal_dwconv(qkv, conv_buf, conv_w)
    qkv = jax.nn.silu(qkv)
    nk = GDN_HEADS * GDN_DK
    q = l2_norm(qkv[..., :nk].reshape(B, T, GDN_HEADS, GDN_DK)) * (GDN_DK ** -0.5)
    k = l2_norm(qkv[..., nk:2 * nk].reshape(B, T, GDN_HEADS, GDN_DK))
    v = qkv[..., 2 * nk:].reshape(B, T, GDN_HEADS, GDN_DV).astype(jnp.float32)
    g = -jnp.exp(a_log.astype(jnp.float32)) * jax.nn.softplus(a.astype(jnp.float32) + dt_bias.astype(jnp.float32))
    beta = jax.nn.sigmoid(b.astype(jnp.float32))
    o, S_new = gated_delta_rule(q, k, v, g, beta, S0.astype(jnp.float32), min(CHUNK, T))
    o = rms_norm(o, norm_g) * jax.nn.silu(z.reshape(B, T, GDN_HEADS, GDN_DV).astype(jnp.float32))
    o = o.reshape(B, T, GDN_WIDTH).astype(h.dtype)
    yc, sc_buf_new = causal_dwconv(gate_c * h_in, sc_buf, sc_w)
    ob = gate_b * yc
    out = jnp.concatenate([o, ob], axis=-1) @ w_out
    return out, conv_buf_new, S_new, sc_buf_new


def fox_project(h, w_in, b_f, qn_g, kn_g):
    B, T, _ = h.shape
    p = h @ w_in
    shp = (B, T, FOX_HEADS, FOX_DH)
    q = rms_norm(p[..., :FOX_WIDTH].reshape(shp), qn_g)
    k = rms_norm(p[..., FOX_WIDTH:2 * FOX_WIDTH].reshape(shp), kn_g)
    v = p[..., 2 * FOX_WIDTH:3 * FOX_WIDTH].reshape(shp)
    logf = jax.nn.log_sigmoid(p[..., 3 * FOX_WIDTH:].astype(jnp.float32) + b_f.astype(jnp.float32))
    return q, k, v, logf


def fox_attend_prompt(q, k, v, logf):
    B, S, H, Dh = q.shape
    nblk = S // FOX_QBLK
    c = jnp.cumsum(logf, axis=1).transpose(0, 2, 1)
    qb = q.reshape(B, nblk, FOX_QBLK, H, Dh).transpose(1, 0, 2, 3, 4)
    cq = c.reshape(B, H, nblk, FOX_QBLK).transpose(2, 0, 1, 3)
    kpos = jnp.arange(S)
    scale = Dh ** -0.5

    def block(args):
        q_i, c_i, start = args
        s = jnp.einsum('bqhd,bkhd->bhqk', q_i, k, preferred_element_type=jnp.float32) * scale
        s = s + c_i[..., :, None] - c[:, :, None, :]
        qpos = start + jnp.arange(FOX_QBLK)
        s = jnp.where(kpos[None, :] <= qpos[:, None], s, -jnp.inf)
        p = jax.nn.softmax(s, axis=-1)
        return jnp.einsum('bhqk,bkhd->bqhd', p.astype(v.dtype), v)

    o = lax.map(block, (qb, cq, jnp.arange(nblk) * FOX_QBLK))
    return o.transpose(1, 0, 2, 3, 4).reshape(B, S, H, Dh)


def fox_attend_sample(q, k, v, logf, k_cache, v_cache, logf_cache):
    B, T, H, Dh = q.shape
    P = k_cache.shape[1]
    k_all = jnp.concatenate([k_cache, k], axis=1)
    v_all = jnp.concatenate([v_cache, v], axis=1)
    lf = jnp.concatenate([logf_cache.astype(jnp.float32), logf], axis=1)
    c = jnp.cumsum(lf, axis=1).transpose(0, 2, 1)
    s = jnp.einsum('bqhd,bkhd->bhqk', q, k_all, preferred_element_type=jnp.float32) * (Dh ** -0.5)
    s = s + c[:, :, P:, None] - c[:, :, None, :]
    qpos = P + jnp.arange(T)
    kpos = jnp.arange(P + T)
    s = jnp.where(kpos[None, :] <= qpos[:, None], s, -jnp.inf)
    p = jax.nn.softmax(s, axis=-1)
    return jnp.einsum('bhqk,bkhd->bqhd', p.astype(v_all.dtype), v_all)


def swiglu(h, w_gate, w_up, w_down):
    return (jax.nn.silu(h @ w_gate) * (h @ w_up)) @ w_down


def trunk(x, gdn_conv, gdn_S, sconv, fox_k, fox_v, fox_logf, pw):
    B, T, _ = x.shape
    has_past = fox_k is not None
    new_k, new_v, new_lf, new_S, new_conv, new_sc = [], [], [], [], [], []
    for layer in range(DEPTH):
        i = layer // 2
        h = rms_norm(x, pw['norm_mix_g'][layer])
        if layer % 2 == 0:
            out, cb, S, sb = gdn_sconv_mixer(
                h,
                gdn_conv[i] if has_past else None,
                gdn_S[i] if has_past else None,
                sconv[i] if has_past else None,
                pw['w_in_even'][i], pw['gdn_conv_w'][i], pw['gdn_a_log'][i], pw['gdn_dt_bias'][i],
                pw['gdn_norm_g'][i], pw['sconv_w'][i], pw['w_out_even'][i])
            new_conv.append(cb)
            new_S.append(S)
            new_sc.append(sb)
        else:
            q, k, v, lf = fox_project(h, pw['w_in_odd'][i], pw['fox_b_f'][i],
                                      pw['fox_q_norm_g'][i], pw['fox_k_norm_g'][i])
            if has_past:
                o = fox_attend_sample(q, k, v, lf, fox_k[i], fox_v[i], fox_logf[i])
            else:
                o = fox_attend_prompt(q, k, v, lf)
            out = o.reshape(B, T, FOX_WIDTH) @ pw['w_out_odd'][i]
            new_k.append(k)
            new_v.append(v)
            new_lf.append(lf)
        x = x + out
        x = x + swiglu(rms_norm(x, pw['norm_ffn_g'][layer]), pw['ffn_w_gate'][layer],
                       pw['ffn_w_up'][layer], pw['ffn_w_down'][layer])
    return (x, jnp.stack(new_k), jnp.stack(new_v), jnp.stack(new_lf),
            jnp.stack(new_S), jnp.stack(new_conv), jnp.stack(new_sc))


def setup_inputs(seed: int = 0) -> dict:
    key = jax.random.key(seed)
    ks = jax.random.split(key, 25)
    f32 = jnp.float32

    def nrm(k, shape, s=1.0):
        return s * jax.random.normal(k, shape, f32)

    x_prompt = nrm(ks[0], (BATCH, SEQ, D_MODEL))
    x_sample = nrm(ks[1], (DEC_BATCH, DEC_SEQ, D_MODEL))
    cache_fox_k = nrm(ks[2], (N_ODD, DEC_BATCH, PAST_LEN, FOX_HEADS, FOX_DH))
    cache_fox_v = nrm(ks[3], (N_ODD, DEC_BATCH, PAST_LEN, FOX_HEADS, FOX_DH))
    cache_fox_logf = jax.nn.log_sigmoid(INIT_FORGET_BIAS + nrm(ks[4], (N_ODD, DEC_BATCH, PAST_LEN, FOX_HEADS), 0.3))
    state_gdn_S = nrm(ks[5], (N_EVEN, DEC_BATCH, GDN_HEADS, GDN_DK, GDN_DV), 0.1)
    state_gdn_conv = nrm(ks[6], (N_EVEN, DEC_BATCH, GDN_CONV - 1, GDN_QKV))
    state_sconv = nrm(ks[7], (N_EVEN, DEC_BATCH, SC_CONV - 1, SC_WIDTH))
    norm_mix_g = 1.0 + nrm(ks[8], (DEPTH, D_MODEL), 0.02)
    norm_ffn_g = 1.0 + nrm(ks[9], (DEPTH, D_MODEL), 0.02)
    w_in_even = nrm(ks[10], (N_EVEN, D_MODEL, EVEN_IN), D_MODEL ** -0.5)
    gdn_conv_w = nrm(ks[11], (N_EVEN, GDN_CONV, GDN_QKV), GDN_CONV ** -0.5)
    gdn_a_log = jnp.log(jax.random.uniform(ks[12], (N_EVEN, GDN_HEADS), f32, 1.0, 16.0))
    dt = jnp.exp(jax.random.uniform(ks[13], (N_EVEN, GDN_HEADS), f32, math.log(1e-3), math.log(1e-1)))
    gdn_dt_bias = dt + jnp.log(-jnp.expm1(-dt))
    gdn_norm_g = 1.0 + nrm(ks[14], (N_EVEN, GDN_DV), 0.02)
    sconv_w = nrm(ks[15], (N_EVEN, SC_CONV, SC_WIDTH), SC_CONV ** -0.5)
    w_out_even = nrm(ks[16], (N_EVEN, D_MODEL, D_MODEL), D_MODEL ** -0.5)
    w_in_odd = nrm(ks[17], (N_ODD, D_MODEL, ODD_IN), D_MODEL ** -0.5)
    w_in_odd = w_in_odd.at[..., 3 * FOX_WIDTH:].multiply(INIT_FORGET_W_SCALE)
    fox_b_f = INIT_FORGET_BIAS + nrm(ks[18], (N_ODD, FOX_HEADS), 0.1)
    fox_q_norm_g = 1.0 + nrm(ks[19], (N_ODD, FOX_DH), 0.02)
    fox_k_norm_g = 1.0 + nrm(ks[20], (N_ODD, FOX_DH), 0.02)
    w_out_odd = nrm(ks[21], (N_ODD, FOX_WIDTH, D_MODEL), FOX_WIDTH ** -0.5)
    ffn_w_gate = nrm(ks[22], (DEPTH, D_MODEL, D_FF), D_MODEL ** -0.5)
    ffn_w_up = nrm(ks[23], (DEPTH, D_MODEL, D_FF), D_MODEL ** -0.5)
    ffn_w_down = nrm(ks[24], (DEPTH, D_FF, D_MODEL), D_FF ** -0.5)
    return {
        'x_prompt': x_prompt, 'x_sample': x_sample,
        'cache_fox_k': cache_fox_k, 'cache_fox_v': cache_fox_v, 'cache_fox_logf': cache_fox_logf,
        'state_gdn_S': state_gdn_S, 'state_gdn_conv': state_gdn_conv, 'state_sconv': state_sconv,
        'norm_mix_g': norm_mix_g, 'norm_ffn_g': norm_ffn_g,
        'w_in_even': w_in_even, 'gdn_conv_w': gdn_conv_w, 'gdn_a_log': gdn_a_log,
        'gdn_dt_bias': gdn_dt_bias, 'gdn_norm_g': gdn_norm_g, 'sconv_w': sconv_w,
        'w_out_even': w_out_even,
        'w_in_odd': w_in_odd, 'fox_b_f': fox_b_f, 'fox_q_norm_g': fox_q_norm_g,
        'fox_k_norm_g': fox_k_norm_g, 'w_out_odd': w_out_odd,
        'ffn_w_gate': ffn_w_gate, 'ffn_w_up': ffn_w_up, 'ffn_w_down': ffn_w_down,
    }


def reference(x_prompt, x_sample, cache_fox_k, cache_fox_v, cache_fox_logf, state_gdn_S,
              state_gdn_conv, state_sconv, norm_mix_g, norm_ffn_g, w_in_even, gdn_conv_w,
              gdn_a_log, gdn_dt_bias, gdn_norm_g, sconv_w, w_out_even, w_in_odd, fox_b_f,
              fox_q_norm_g, fox_k_norm_g, w_out_odd, ffn_w_gate, ffn_w_up, ffn_w_down):
    pw = dict(norm_mix_g=norm_mix_g, norm_ffn_g=norm_ffn_g, w_in_even=w_in_even,
              gdn_conv_w=gdn_conv_w, gdn_a_log=gdn_a_log, gdn_dt_bias=gdn_dt_bias,
              gdn_norm_g=gdn_norm_g, sconv_w=sconv_w, w_out_even=w_out_even, w_in_odd=w_in_odd,
              fox_b_f=fox_b_f, fox_q_norm_g=fox_q_norm_g, fox_k_norm_g=fox_k_norm_g,
              w_out_odd=w_out_odd, ffn_w_gate=ffn_w_gate, ffn_w_up=ffn_w_up, ffn_w_down=ffn_w_down)
    (y_prompt, p_fox_k, p_fox_v, p_fox_logf,
     p_gdn_S, p_gdn_conv, p_sconv) = trunk(x_prompt, None, None, None, None, None, None, pw)
    (y_sample, s_fox_k, s_fox_v, s_fox_logf,
     s_gdn_S, s_gdn_conv, s_sconv) = trunk(x_sample, state_gdn_conv, state_gdn_S, state_sconv,
                                           cache_fox_k, cache_fox_v, cache_fox_logf, pw)
    return (y_prompt, y_sample, p_fox_k, p_fox_v, p_fox_logf, p_gdn_S, p_gdn_conv, p_sconv,
            s_fox_k, s_fox_v, s_fox_logf, s_gdn_S, s_gdn_conv, s_sconv)
```

```python
import numpy as np
from contextlib import ExitStack
import concourse.bass as bass
import concourse.mybir as mybir
from concourse.bass_utils import run_bass_kernel_spmd

F32 = mybir.dt.float32
BF16 = mybir.dt.bfloat16
AF = mybir.ActivationFunctionType
ALU = mybir.AluOpType

D = 1024
TP = 4096
NSS = 4
TS = 64
R = TP + NSS * TS
PAST = 1024
DFF = 2816
EVEN_IN = 3592
ODD_IN = 3080
EPS = 1e-6
NCORES = 8

TILES = [(r * 128, 128) for r in range(32)] + [(TP + TS * s, TS) for s in range(NSS)]
GROUPS = [(g * 512, 512, [4 * g + i for i in range(4)]) for g in range(8)] + [(TP, 256, [32, 33, 34, 35])]
SEQS = [(0, TP)] + [(TP + TS * s, TS) for s in range(NSS)]


class Tl:
    def __init__(self, t, name=""):
        self.t = t
        self.w = None
        self.rd = {}
        self.sem = None
        self.cnt = 0
        self.name = name
        self.psum = False

    def __getitem__(self, k):
        return self.t[k]


class Eng:
    def __init__(self, name):
        self.name = name
        self.ops = []
        self.cnt = 0
        self.sem = None
        self.waited = {}


class Prog:
    def __init__(self, nc):
        self.nc = nc
        self.stack = ExitStack()
        self.E = {n: Eng(n) for n in ("pe", "act", "dve", "pool", "sp")}
        for e in self.E.values():
            e.sem = self.stack.enter_context(nc.semaphore("s_" + e.name))
        self.pending = []
        self.dsems = []
        self.free = {"sp": [], "pool": []}
        self.phase_tiles = []

    def _deps(self, eng, reads, writes):
        deps = {}

        def add(st, raw):
            if st is None:
                return
            s, v = st
            if s is eng.sem and (eng.name == "pe" or not raw):
                return
            k = id(s)
            if deps.get(k, (None, 0))[1] < v:
                deps[k] = (s, v)

        for t in reads:
            add(t.w, True)
            if t.psum:
                for st in t.rd.values():
                    add(st, False)
        for t in writes:
            add(t.w, False)
            for st in t.rd.values():
                add(st, False)
        out = []
        for k, (s, v) in deps.items():
            if eng.waited.get(k, 0) >= v:
                continue
            eng.waited[k] = v
            out.append((s, v))
        return out

    def op(self, en, fn, reads=(), writes=(), inc=True):
        eng = self.E[en]
        waits = self._deps(eng, reads, writes)
        if inc:
            eng.cnt += 1
            st = (eng.sem, eng.cnt)
        else:
            st = (eng.sem, eng.cnt + 1)
        for t in writes:
            t.w = st
            t.rd = {}
        for t in reads:
            if t in writes:
                continue
            t.rd[id(eng.sem)] = st
        eng.ops.append((waits, fn, eng.sem if inc else None, 1))

    def _dsem(self, t, qn):
        if t.sem is None:
            t.sem = {}
        if qn not in t.sem:
            if self.free[qn]:
                t.sem[qn] = self.free[qn].pop()
            else:
                ds = Eng("dsem")
                ds.sem = self.stack.enter_context(self.nc.semaphore())
                self.dsems.append(ds)
                t.sem[qn] = ds
            self.phase_tiles.append((t, qn))
        return t.sem[qn]

    def dma(self, qn, out_ap, in_ap, reads=(), writes=(), **kw):
        eng = self.E[qn]
        waits = self._deps(eng, reads, writes)
        t = writes[0] if writes else reads[0]
        ds = self._dsem(t, qn)
        ds.cnt += 16
        st = (ds.sem, ds.cnt)
        for x in writes:
            x.w = st
            x.rd = {}
        for x in reads:
            x.rd[id(ds.sem)] = st
        eng.ops.append((waits, lambda e: e.dma_start(out=out_ap, in_=in_ap, **kw), ds.sem, 16))
        self.pending.append(st)

    def flush(self):
        eng = self.E["sp"]
        best = {}
        for s, v in self.pending:
            if best.get(id(s), (s, 0))[1] < v:
                best[id(s)] = (s, v)
        waits = [(s, v) for k, (s, v) in best.items() if eng.waited.get(k, 0) < v]
        eng.ops.append((waits, None, None, 0))
        self.pending = []
        with self.nc.Block() as blk:
            for name, deco in (("pe", blk.tensor), ("act", blk.scalar), ("dve", blk.vector),
                               ("pool", blk.gpsimd), ("sp", blk.sync)):
                ops = self.E[name].ops
                if not ops:
                    continue

                def body(e, ops=ops):
                    for waits, fn, sem, amt in ops:
                        if fn is None:
                            for s, v in waits:
                                e.wait_ge(s, v)
                            continue
                        for s, v in waits[:-1]:
                            e.wait_ge(s, v)
                        ins = fn(e)
                        if waits:
                            ins._wait_ge(*waits[-1])
                        if sem is not None:
                            ins.then_inc(sem, amt)

                deco(body)
        for e in self.E.values():
            e.ops = []
        for e in self.E.values():
            for e2 in self.E.values():
                e.waited[id(e2.sem)] = e2.cnt
            for ds in self.dsems:
                e.waited[id(ds.sem)] = ds.cnt
        for t, qn in self.phase_tiles:
            self.free[qn].append(t.sem.pop(qn))
        self.phase_tiles = []


class Pool_:
    def __init__(self, tiles):
        self.tiles = tiles
        self.i = 0

    def get(self):
        t = self.tiles[self.i % len(self.tiles)]
        self.i += 1
        return t


OPTS = {}


def build(upto=None, dbg=False):
    nc = bass.Bass("TRN2", target_bir_lowering=False)

    def din(name, shape):
        return nc.dram_tensor(name, list(shape), F32, kind="ExternalInput").ap()

    def dout(name, shape):
        return nc.dram_tensor(name, list(shape), F32, kind="ExternalOutput").ap()

    def dscr(name, shape, dt):
        return nc.dram_tensor(name, list(shape), dt, kind=("ExternalOutput" if dbg else "Internal")).ap()

    x_all = din("x_all", [R, D])
    ck = din("ck", [2, NSS, PAST, D])
    cv = din("cv", [2, NSS, PAST, D])
    clf = din("clf", [2, NSS, PAST, 8])
    S0 = din("S0", [2, 5, 4, 128, 128])
    conv0 = din("conv0", [2, 5, 3, 1536])
    sc0 = din("sc0", [2, 5, 2, 512])
    norm_mix_g = din("norm_mix_g", [4, D])
    norm_ffn_g = din("norm_ffn_g", [4, D])
    w_in_even = din("w_in_even", [2, D, EVEN_IN])
    gdn_conv_w = din("gdn_conv_w", [2, 4, 1536])
    gdn_a_log = din("gdn_a_log", [2, 4])
    gdn_dt_bias = din("gdn_dt_bias", [2, 4])
    gdn_norm_g = din("gdn_norm_g", [2, 128])
    sconv_w = din("sconv_w", [2, 3, 512])
    w_out_even = din("w_out_even", [2, D, D])
    w_in_odd = din("w_in_odd", [2, D, ODD_IN])
    fox_b_f = din("fox_b_f", [2, 8])
    fox_q_norm_g = din("fox_q_norm_g", [2, 128])
    fox_k_norm_g = din("fox_k_norm_g", [2, 128])
    w_out_odd = din("w_out_odd", [2, D, D])
    ffn_w_gate = din("ffn_w_gate", [4, D, DFF])
    ffn_w_up = din("ffn_w_up", [4, D, DFF])
    ffn_w_down = din("ffn_w_down", [4, DFF, D])
    c_ident = din("c_ident", [128, 128])
    c_U = din("c_U", [128, 128])
    c_ones = din("c_ones", [128, 128])
    c_negls = din("c_negls", [128, 128])
    c_negu = din("c_negu", [128, 128])

    y_all = dout("y_all", [R, D])
    fox_k = dout("fox_k", [2, R, D])
    fox_v = dout("fox_v", [2, R, D])
    fox_logf = dout("fox_logf", [2, R, 8])
    gdn_S = dout("gdn_S", [2, 5, 4, 128, 128])
    gdn_conv = dout("gdn_conv", [2, 5, 3, 1536])
    sconv_o = dout("sconv_o", [2, 5, 2, 512])

    X = dscr("X", [R, D], F32)
    QKVT = dscr("QKVT", [1536, R], BF16)
    Z = dscr("Z", [R, 512], BF16)
    OT = dscr("OT", [D, R], BF16)
    KT = dscr("KT", [D, R], BF16)
    QT = dscr("QT", [D, R], BF16)
    V = dscr("V", [R, D], BF16)
    ACTS = dscr("ACTS", [DFF, R], BF16)

    P = Prog(nc)
    GS = P.stack

    uid = [0]

    def sb(st, name, shape, dt):
        uid[0] += 1
        name = f"{name}_{uid[0]}"
        return Tl(st.enter_context(nc.sbuf_tensor(name, list(shape), dt)), name)

    def pst(st, name, dt):
        shape = [128, 512] if dt == F32 else [128, 1024]
        uid[0] += 1
        name = f"{name}_{uid[0]}"
        t = Tl(st.enter_context(nc.psum_tensor(name, shape, dt)), name)
        t.psum = True
        return t

    def bcast(ap2d_row, nparts, ncols):
        return bass.AP(ap2d_row.tensor, ap2d_row.offset, [[0, nparts], [1, ncols]])

    def mm(out_t, out_ap, lhsT, rhs, reads, start=True, stop=True, inc=True):
        P.op("pe", lambda e: e.matmul(out_ap, lhsT, rhs, start=start, stop=stop), reads=reads, writes=[out_t], inc=inc)

    def tr(out_t, out_ap, in_ap, ident_ap, reads, inc=True):
        P.op("pe", lambda e: e.transpose(out_ap, in_ap, ident_ap), reads=reads, writes=[out_t], inc=inc)

    def act(out_t, out_ap, in_ap, func, reads, bias=None, scale=None, accum=None):
        kw = {}
        if bias is not None:
            kw["bias"] = bias
        if scale is not None:
            kw["scale"] = scale
        writes = [out_t]
        if accum is not None:
            kw["accum_out"] = accum[1]
            if accum[0] is not out_t:
                writes.append(accum[0])
        P.op("act", lambda e: e.activation(out=out_ap, in_=in_ap, func=func, **kw), reads=reads, writes=writes)

    def ts(en, out_t, out_ap, in0, s1, s2, op0, op1, reads):
        if op1 is None:
            P.op(en, lambda e: e.tensor_scalar(out_ap, in0, s1, None, op0), reads=reads, writes=[out_t])
        else:
            P.op(en, lambda e: e.tensor_scalar(out_ap, in0, s1, s2, op0, op1), reads=reads, writes=[out_t])

    def tt(en, out_t, out_ap, in0, in1, op, reads):
        P.op(en, lambda e: e.tensor_tensor(out_ap, in0, in1, op), reads=reads, writes=[out_t])

    def stt(en, out_t, out_ap, in0, scalar, in1, op0, op1, reads):
        P.op(en, lambda e: e.scalar_tensor_tensor(out_ap, in0, scalar, in1, op0, op1), reads=reads, writes=[out_t])

    def cp(en, out_t, out_ap, in_ap, reads):
        if en == "act":
            P.op("act", lambda e: e.copy(out_ap, in_ap), reads=reads, writes=[out_t])
        else:
            P.op(en, lambda e: e.tensor_copy(out_ap, in_ap), reads=reads, writes=[out_t])

    def rsqrt_(en, t, ap, mult, add, reads=()):
        n = ap.shape[0]
        act(t, ap, ap, AF.Sqrt, list(reads) + [t, epsT], bias=epsT[:n, 0:1], scale=mult)
        P.op("dve", lambda e: e.reciprocal(ap, ap), reads=[t], writes=[t])

    def abs_(t, ap, src_t, src_ap):
        ts("dve", t, ap, src_ap, -1.0, None, ALU.mult, None, [src_t])
        tt("dve", t, ap, ap, src_ap, ALU.max, [t, src_t])

    HT = GS.enter_context(nc.sbuf_tensor("HT", [128, 8, R], BF16))
    HTt = [Tl(HT, f"HT{i}") for i in range(len(TILES))]
    identf = sb(GS, "identf", [128, 128], F32)
    identb = sb(GS, "identb", [128, 128], BF16)
    Uf = sb(GS, "Uf", [128, 128], F32)
    onesf = sb(GS, "onesf", [128, 128], F32)
    onesb = sb(GS, "onesb", [128, 128], BF16)
    negls = sb(GS, "negls", [128, 128], F32)
    negu = sb(GS, "negu", [128, 128], F32)
    maskb = sb(GS, "maskb", [128, 128], BF16)
    epsT = sb(GS, "epsT", [128, 1], F32)
    GB = sb(GS, "GB", [128, 36, 8], F32)
    CCp = sb(GS, "CCp", [128, 32, 8], F32)
    CEp = sb(GS, "CEp", [128, 8, 8], F32)
    CCs = [sb(GS, f"CCs{s}", [128, 9, 8], F32) for s in range(NSS)]
    CEs = [sb(GS, f"CEs{s}", [128, 8], F32) for s in range(NSS)]

    def norm_a(ti, xt, gB, pools):
        r0, n = TILES[ti]
        junk, ssp, hnp, psb = pools
        jk = junk.get()
        ss = ssp.get()
        hn = hnp.get()
        act(jk, jk[:n, :], xt[:n, :], AF.Square, [xt], accum=(ss, ss[:n, 0:1]))
        rsqrt_("dve", ss, ss[:n, 0:1], 1.0 / D, EPS)
        stt("dve", hn, hn[:n, :], xt[:n, :], ss[:n, 0:1], gB[:n, :], ALU.mult, ALU.mult, [xt, ss, gB])
        return hn

    def norm_b(ti, hn, pools):
        r0, n = TILES[ti]
        psb = pools[3]
        pb = psb.get()
        for kc in range(8):
            tr(pb, pb[:, kc * 128:kc * 128 + n], hn[:n, kc * 128:(kc + 1) * 128], identb[:n, :n], [hn, identb], inc=(kc == 7))
        src = pb.t[:, :].rearrange("p (k m) -> p k m", k=8)[:, :, 0:n]
        cp("act", HTt[ti], HT[:, :, r0:r0 + n], src, [pb])

    def norm_tile(ti, xt, gB, pools):
        norm_b(ti, norm_a(ti, xt, gB, pools), pools)

    def norm_pools(st, psb):
        return (Pool_([sb(st, f"njk{i}", [128, D], BF16) for i in range(2)]),
                Pool_([sb(st, f"nss{i}", [128, 1], F32) for i in range(3)]),
                Pool_([sb(st, f"nhn{i}", [128, D], BF16) for i in range(3)]),
                psb)

    def load_gain(st, name, g_ap_row):
        t = sb(st, name, [128, D], F32)
        P.dma("sp", t[:, :], bcast(g_ap_row, 128, D), writes=[t])
        return t

    def phase0():
        with ExitStack() as st:
            psb = Pool_([pst(st, f"p0b{i}", BF16) for i in range(2)])
            P.dma("sp", identf[:, :], c_ident, writes=[identf])
            P.dma("sp", Uf[:, :], c_U, writes=[Uf])
            P.dma("sp", onesf[:, :], c_ones, writes=[onesf])
            P.dma("sp", negls[:, :], c_negls, writes=[negls])
            P.dma("sp", negu[:, :], c_negu, writes=[negu])
            P.op("dve", lambda e: e.memset(epsT[:, :], EPS), writes=[epsT])
            P.dma("pool", identb[:, :], c_ident, writes=[identb])
            P.dma("pool", onesb[:, :], c_ones, writes=[onesb])
            P.dma("pool", maskb[:, :], c_U, writes=[maskb])
            if OPTS.get('p0_consts_only'):
                P.flush()
                return
            gB = load_gain(st, "g0", norm_mix_g[0, :])
            xp = Pool_([sb(st, f"x0_{i}", [128, D], F32) for i in range(3)])
            pools = norm_pools(st, psb)
            for ti, (r0, n) in enumerate(TILES):
                xt = xp.get()
                P.dma("sp", xt[:n, :], x_all[r0:r0 + n, :], writes=[xt])
                norm_tile(ti, xt, gB, pools)
            P.flush()

    def load_w(wb, w2d, c0, ncol, dst0=0, q="pool"):
        src = w2d.rearrange("(k p) c -> p k c", p=128)[:, :, c0:c0 + ncol]
        P.dma(q, wb[:, :, dst0:dst0 + ncol], src, writes=[wb])

    def phase_proj_even(l2):
        W = w_in_even[l2]
        with ExitStack() as st:
            psf = Pool_([pst(st, f"pef{i}", F32) for i in range(8)])
            wbs = Pool_([sb(st, f"wb{i}", [128, 8, 520], BF16) for i in range(3)])
            cw = sb(st, "cw", [128, 12, 4], F32)
            scw = sb(st, "scw", [128, 4, 3], F32)
            for c in range(12):
                P.dma("sp", cw[:, c, :], gdn_conv_w[l2, :, c * 128:(c + 1) * 128].rearrange("t c -> c t"),
                      writes=[cw], allow_slow_non_contiguous=True)
            for c in range(4):
                P.dma("sp", scw[:, c, :], sconv_w[l2, :, c * 128:(c + 1) * 128].rearrange("t c -> c t"),
                      writes=[scw], allow_slow_non_contiguous=True)
            rb = [[sb(st, f"rb{c}_{p}", [128, 516], F32) for p in range(2)] for c in range(4)]
            rbs = Pool_([sb(st, f"rbs{i}", [128, 68], F32) for i in range(3)])
            yp = Pool_([sb(st, f"cy{i}", [128, 512], F32) for i in range(3)])
            sop = Pool_([sb(st, f"so{i}", [128, 512], F32) for i in range(5)])
            sqp = Pool_([sb(st, f"sq{i}", [128, 512], F32) for i in range(3)])
            rsp = Pool_([sb(st, f"rs{i}", [128, 512], F32) for i in range(3)])
            obp = Pool_([sb(st, f"ob{i}", [128, 512], BF16) for i in range(3)])

            o1 = 2056

            def wload(i, wb):
                if i < 3:
                    load_w(wb, W, i * 512, 512)
                elif i < 7:
                    c = i - 3
                    for j in range(3):
                        load_w(wb, W, o1 + j * 512 + c * 128, 128, dst0=j * 128)
                else:
                    load_w(wb, W, 1536, 520)

            wq = [wbs.get()]
            wload(0, wq[0])

            def next_w(i):
                if i + 1 < 8:
                    wq.append(wbs.get())
                    wload(i + 1, wq[i + 1])

            def conv_taps(cur, off0, n, chunk, y, yoff, ntap, wt):
                ts("dve", y, y[:, yoff:yoff + n], cur[:, off0:off0 + n], wt[:, chunk, 0:1], None, ALU.mult, None, [cur, wt])
                for i in range(1, ntap):
                    stt("dve", y, y[:, yoff:yoff + n], cur[:, off0 + i:off0 + i + n], wt[:, chunk, i:i + 1], y[:, yoff:yoff + n],
                        ALU.mult, ALU.add, [cur, wt, y])

            for b in range(3):
                wb = wq[b]
                next_w(b)
                items = [(gi, c) for gi in range(len(GROUPS)) for c in range(4)]
                stt_ = {}

                def stage0(k, wb=wb):
                    gi, c = items[k]
                    g0, gn, gtiles = GROUPS[gi]
                    hts = [HTt[i] for i in gtiles]
                    ps = psf.get()
                    for kc in range(8):
                        mm(ps, ps[:, 0:gn], wb[:, kc, c * 128:(c + 1) * 128], HT[:, kc, g0:g0 + gn], [wb] + hts,
                           start=(kc == 0), stop=(kc == 7), inc=(kc == 7))
                    stt_[k] = {"ps": ps}

                def stage1(k, b=b):
                    gi, c = items[k]
                    g0, gn, gtiles = GROUPS[gi]
                    chunk = b * 4 + c
                    ps = stt_[k]["ps"]
                    y = yp.get()
                    if gi < 8:
                        cur = rb[c][gi % 2]
                        prev = rb[c][(gi - 1) % 2]
                        cp("act", cur, cur[:, 3:3 + gn], ps[:, 0:gn], [ps])
                        if gi == 0:
                            P.dma("sp", cur[:, 0:3], conv0[l2, 0, :, chunk * 128:(chunk + 1) * 128].rearrange("t c -> c t"),
                                  writes=[cur], allow_slow_non_contiguous=True)
                        else:
                            cp("pool", cur, cur[:, 0:3], prev[:, 512:515], [prev])
                        conv_taps(cur, 0, gn, chunk, y, 0, 4, cw)
                        if gi == 7:
                            P.dma("sp", gdn_conv[l2, 0, :, chunk * 128:(chunk + 1) * 128].rearrange("t c -> c t"),
                                  cur[:, 512:515], reads=[cur], allow_slow_non_contiguous=True)
                    else:
                        for s_ in range(NSS):
                            cur = rbs.get()
                            cp("act", cur, cur[:, 3:3 + TS], ps[:, s_ * TS:(s_ + 1) * TS], [ps])
                            P.dma("sp", cur[:, 0:3], conv0[l2, 1 + s_, :, chunk * 128:(chunk + 1) * 128].rearrange("t c -> c t"),
                                  writes=[cur], allow_slow_non_contiguous=True)
                            conv_taps(cur, 0, TS, chunk, y, s_ * TS, 4, cw)
                            P.dma("sp", gdn_conv[l2, 1 + s_, :, chunk * 128:(chunk + 1) * 128].rearrange("t c -> c t"),
                                  cur[:, TS:TS + 3], reads=[cur], allow_slow_non_contiguous=True)
                    stt_[k]["y"] = y

                def stage2(k, b=b):
                    gi, c = items[k]
                    g0, gn, gtiles = GROUPS[gi]
                    chunk = b * 4 + c
                    y = stt_[k]["y"]
                    if b == 2:
                        ob = obp.get()
                        act(ob, ob[:, 0:gn], y[:, 0:gn], AF.Silu, [y])
                        P.dma("sp", QKVT[chunk * 128:(chunk + 1) * 128, g0:g0 + gn], ob[:, 0:gn], reads=[ob])
                    else:
                        so = sop.get()
                        act(so, so[:, 0:gn], y[:, 0:gn], AF.Silu, [y])
                        sq = sqp.get()
                        act(sq, sq[:, 0:gn], so[:, 0:gn], AF.Square, [so])
                        stt_[k]["so"] = so
                        stt_[k]["sq"] = sq

                def stage3(k, b=b):
                    gi, c = items[k]
                    g0, gn, gtiles = GROUPS[gi]
                    if b == 2:
                        return
                    sq = stt_[k]["sq"]
                    ps = psf.get()
                    mm(ps, ps[:, 0:gn], onesf[:, :], sq[:, 0:gn], [onesf, sq])
                    stt_[k]["ps1"] = ps

                def stage4(k, b=b):
                    gi, c = items[k]
                    g0, gn, gtiles = GROUPS[gi]
                    if b == 2:
                        return
                    ps = stt_[k]["ps1"]
                    rs = rsp.get()
                    act(rs, rs[:, 0:gn], ps[:, 0:gn], AF.Sqrt, [ps, epsT], bias=epsT[:, 0:1])
                    stt_[k]["rs"] = rs

                def stage5(k, b=b):
                    gi, c = items[k]
                    g0, gn, gtiles = GROUPS[gi]
                    chunk = b * 4 + c
                    st_ = stt_.pop(k)
                    if b == 2:
                        return
                    so, rs = st_["so"], st_["rs"]
                    P.op("dve", lambda e, rs=rs, gn=gn: e.reciprocal(rs[:, 0:gn], rs[:, 0:gn]), reads=[rs], writes=[rs])
                    sc = (128.0 ** -0.5) if b == 0 else 1.0
                    ob = obp.get()
                    stt("dve", ob, ob[:, 0:gn], so[:, 0:gn], sc, rs[:, 0:gn], ALU.mult, ALU.mult, [so, rs])
                    P.dma("sp", QKVT[chunk * 128:(chunk + 1) * 128, g0:g0 + gn], ob[:, 0:gn], reads=[ob])

                NI = len(items)
                stages = [stage0, stage1, stage2, stage3, stage4, stage5]
                for t in range(NI + 5):
                    for lag in (5, 4, 3, 2, 1, 0):
                        if 0 <= t - lag < NI:
                            stages[lag](t - lag)

            mb = [sb(st, f"mb{p}", [128, 516], F32) for p in range(2)]
            mbs = Pool_([sb(st, f"mbs{i}", [128, 68], F32) for i in range(3)])
            gcp = Pool_([sb(st, f"gc{i}", [128, 512], F32) for i in range(2)])
            for c in range(4):
                wb = wq[3 + c]
                next_w(3 + c)
                stt_ = {}

                def sstage0(gi, wb=wb):
                    g0, gn, gtiles = GROUPS[gi]
                    hts = [HTt[i] for i in gtiles]
                    pss = []
                    for j in range(3):
                        ps = psf.get()
                        for kc in range(8):
                            mm(ps, ps[:, 0:gn], wb[:, kc, j * 128:(j + 1) * 128], HT[:, kc, g0:g0 + gn], [wb] + hts,
                               start=(kc == 0), stop=(kc == 7), inc=(kc == 7))
                        pss.append(ps)
                    stt_[gi] = pss

                def sstage1(gi, c=c):
                    g0, gn, gtiles = GROUPS[gi]
                    pgb, pgc, phin = stt_.pop(gi)
                    gct = gcp.get()
                    cp("act", gct, gct[:, 0:gn], pgc[:, 0:gn], [pgc])
                    y = yp.get()
                    if gi < 8:
                        cur = mb[gi % 2]
                        prev = mb[(gi - 1) % 2]
                        tt("dve", cur, cur[:, 2:2 + gn], phin[:, 0:gn], gct[:, 0:gn], ALU.mult, [phin, gct])
                        if gi == 0:
                            P.dma("sp", cur[:, 0:2], sc0[l2, 0, :, c * 128:(c + 1) * 128].rearrange("t c -> c t"),
                                  writes=[cur], allow_slow_non_contiguous=True)
                        else:
                            cp("pool", cur, cur[:, 0:2], prev[:, 512:514], [prev])
                        conv_taps(cur, 0, gn, c, y, 0, 3, scw)
                        if gi == 7:
                            P.dma("sp", sconv_o[l2, 0, :, c * 128:(c + 1) * 128].rearrange("t c -> c t"),
                                  cur[:, 512:514], reads=[cur], allow_slow_non_contiguous=True)
                    else:
                        for s_ in range(NSS):
                            cur = mbs.get()
                            sl = slice(s_ * TS, (s_ + 1) * TS)
                            tt("dve", cur, cur[:, 2:2 + TS], phin[:, sl], gct[:, sl], ALU.mult, [phin, gct])
                            P.dma("sp", cur[:, 0:2], sc0[l2, 1 + s_, :, c * 128:(c + 1) * 128].rearrange("t c -> c t"),
                                  writes=[cur], allow_slow_non_contiguous=True)
                            conv_taps(cur, 0, TS, c, y, s_ * TS, 3, scw)
                            P.dma("sp", sconv_o[l2, 1 + s_, :, c * 128:(c + 1) * 128].rearrange("t c -> c t"),
                                  cur[:, TS:TS + 2], reads=[cur], allow_slow_non_contiguous=True)
                    ob = obp.get()
                    tt("dve", ob, ob[:, 0:gn], pgb[:, 0:gn], y[:, 0:gn], ALU.mult, [pgb, y])
                    P.dma("sp", OT[512 + c * 128:512 + (c + 1) * 128, g0:g0 + gn], ob[:, 0:gn], reads=[ob])

                NG_ = len(GROUPS)
                for t in range(NG_ + 1):
                    if t < NG_:
                        sstage0(t)
                    if 0 <= t - 1 < NG_:
                        sstage1(t - 1)

            wb = wq[7]
            AB = sb(st, "AB", [128, 36, 8], F32)
            P.op("pool", lambda e: e.memset(AB[:, :, :], 0.0), writes=[AB])
            zsp = Pool_([sb(st, f"zs{i}", [128, 512], BF16) for i in range(3)])
            for ti, (r0, n) in enumerate(TILES):
                ps = psf.get()
                for kc in range(8):
                    mm(ps, ps[:n, 0:512], HT[:, kc, r0:r0 + n], wb[:, kc, 0:512], [wb, HTt[ti]],
                       start=(kc == 0), stop=(kc == 7), inc=(kc == 7))
                ps2 = psf.get()
                for kc in range(8):
                    mm(ps2, ps2[:n, 0:8], HT[:, kc, r0:r0 + n], wb[:, kc, 512:520], [wb, HTt[ti]],
                       start=(kc == 0), stop=(kc == 7), inc=(kc == 7))
                zs = zsp.get()
                act(zs, zs[:n, :], ps[:n, 0:512], AF.Silu, [ps])
                P.dma("sp", Z[r0:r0 + n, :], zs[:n, :], reads=[zs])
                cp("dve", AB, AB[:n, ti, :], ps2[:n, 0:8], [ps2])
            dtb = sb(st, "dtb", [128, 4], F32)
            nex = sb(st, "nex", [128, 4], F32)
            P.dma("sp", dtb[:, :], bcast(gdn_dt_bias[l2, :], 128, 4), writes=[dtb])
            P.dma("sp", nex[:, :], bcast(gdn_a_log[l2, :], 128, 4), writes=[nex])
            act(nex, nex[:, :], nex[:, :], AF.Exp, [nex])
            ts("dve", nex, nex[:, :], nex[:, :], -1.0, None, ALU.mult, None, [nex])
            t1 = sb(st, "gt1", [128, 36], F32)
            t2 = sb(st, "gt2", [128, 36], F32)
            t3 = sb(st, "gt3", [128, 36], F32)
            for h in range(4):
                ts("dve", t1, t1[:, :], AB[:, :, h], dtb[:, h:h + 1], None, ALU.add, None, [AB, dtb])
                abs_(t2, t2[:, :], t1, t1[:, :])
                act(t2, t2[:, :], t2[:, :], AF.Exp, [t2], scale=-1.0)
                act(t2, t2[:, :], t2[:, :], AF.Ln, [t2], bias=1.0)
                ts("dve", t3, t3[:, :], t1[:, :], 0.0, None, ALU.max, None, [t1])
                tt("dve", t3, t3[:, :], t3[:, :], t2[:, :], ALU.add, [t3, t2])
                ts("dve", GB, GB[:, :, h], t3[:, :], nex[:, h:h + 1], None, ALU.mult, None, [t3, nex])
            act(GB, GB[:, :, 4:8], AB[:, :, 4:8], AF.Sigmoid, [AB])
            P.flush()

    def phase_gdn(l2):
        with ExitStack() as st:
            psf = Pool_([pst(st, f"pgf{i}", F32) for i in range(6)])
            psb = Pool_([pst(st, f"pgb{i}", BF16) for i in range(2)])
            NG = sb(st, "NG", [128, 128], F32)
            P.dma("sp", NG[:, :], bcast(gdn_norm_g[l2, :], 128, 128), writes=[NG])

            class HeadCtx:
                pass

            def mk_ctx(tag, Tmax, share):
                c = HeadCtx()
                if share is None:
                    c.qT = sb(st, f"qT{tag}", [128, Tmax], BF16)
                    c.kT = sb(st, f"kT{tag}", [128, Tmax], BF16)
                    c.vT = sb(st, f"vT{tag}", [128, Tmax], BF16)
                    c.zs = sb(st, f"zz{tag}", [128, Tmax // 128 if Tmax >= 128 else 1, 128], BF16)
                    c.OTb = sb(st, f"otb{tag}", [128, Tmax], BF16)
                    c.S = sb(st, f"S{tag}", [128, 128], F32)
                    c.Sb = sb(st, f"Sb{tag}", [128, 128], BF16)
                else:
                    for a_ in ("qT", "kT", "vT", "zs", "OTb", "S", "Sb"):
                        setattr(c, a_, getattr(share, a_))

                def f(name, dt=F32, shape=(128, 128)):
                    return sb(st, f"{name}{tag}", list(shape), dt)

                c.gBt = f("gBt")
                c.gcol = f("gcol", F32, (128, 1))
                c.glast = f("glast", F32, (128, 1))
                c.EG = f("EG")
                c.egcol = f("egcol", F32, (128, 1))
                c.ekd = f("ekd", F32, (128, 1))
                c.egC = f("egC", F32, (128, 1))
                c.bg = f("bg", F32, (128, 1))
                c.d1 = f("d1")
                c.d2 = f("d2")
                c.vb = f("vb", F32)
                c.kbg = f("kbg", F32)
                c.kd = f("kd", BF16)
                c.A = f("A", F32)
                c.PT = f("PT", BF16)
                c.M = [f("M0", F32), f("M1", F32)]
                c.MT = [f("MT0", F32), f("MT1", F32)]
                c.Rm = [f("R0", F32), f("R1", F32)]
                c.RT = [f("RT0", F32), f("RT1", F32)]
                c.nwT = f("nwT", BF16)
                c.ub = f("ub", BF16)
                c.qgT = f("qgT", BF16)
                c.jk = f("jk", BF16)
                c.ss = f("ss", F32, (128, 1))
                c.t1 = f("t1")
                c.t2 = f("t2", BF16)
                return c

            SEQ_ATTRS = ("qT", "kT", "vT", "zs", "OTb", "S", "Sb")

            def mk_pair(tag, Tmax):
                c0 = mk_ctx(tag + "0", Tmax, None)
                c1 = mk_ctx(tag + "1", Tmax, c0)
                return [c0, c1]

            def chunk_head(c, C, n, ti, h):
                t0 = n * C
                cs = slice(t0, t0 + C)
                gsrc = GB[:C, ti, h:h + 1]
                beta = GB[:C, ti, 4 + h:5 + h]
                ts("dve", c.gBt, c.gBt[:C, :], onesf[:C, :], gsrc, None, ALU.mult, None, [onesf, GB])
                psG = psf.get()
                mm(psG, psG[:, 0:C], c.gBt[:C, :], Uf[:C, :C], [c.gBt, Uf])
                mm(psG, psG[:C, 256:257], Uf[:C, :C], gsrc, [Uf, GB])
                cp("act", c.gcol, c.gcol[:C, :], psG[:C, 256:257], [psG])
                cp("act", c.glast, c.glast[:, :], psG[:, C - 1:C], [psG])
                act(c.EG, c.EG[:, :C], psG[:, 0:C], AF.Exp, [psG])
                act(c.egcol, c.egcol[:C, :], c.gcol[:C, :], AF.Exp, [c.gcol])
                act(c.ekd, c.ekd[:C, :], c.gcol[:C, :], AF.Exp, [c.gcol, c.glast], bias=c.glast[:C, 0:1], scale=-1.0)
                act(c.egC, c.egC[:, :], c.glast[:, :], AF.Exp, [c.glast])
                tt("dve", c.bg, c.bg[:C, :], beta, c.egcol[:C, :], ALU.mult, [GB, c.egcol])
                stt("dve", c.d1, c.d1[:C, :C], psG[:C, 0:C], c.gcol[:C, 0:1], negls[:C, :C], ALU.subtract, ALU.subtract, [psG, c.gcol, negls])
                act(c.d1, c.d1[:C, :C], c.d1[:C, :C], AF.Exp, [c.d1], scale=-1.0)
                stt("dve", c.d2, c.d2[:C, :C], psG[:C, 0:C], c.gcol[:C, 0:1], negu[:C, :C], ALU.subtract, ALU.add, [psG, c.gcol, negu])
                act(c.d2, c.d2[:C, :C], c.d2[:C, :C], AF.Exp, [c.d2])
                yield
                pb = psb.get()
                tr(pb, pb[:C, 0:128], c.kT[:, cs], identb[:, :], [c.kT, identb], inc=False)
                tr(pb, pb[:C, 128:256], c.vT[:, cs], identb[:, :], [c.vT, identb])
                ts("dve", c.vb, c.vb[:C, :], pb[:C, 128:256], beta, None, ALU.mult, None, [pb, GB])
                ts("dve", c.kbg, c.kbg[:C, :], pb[:C, 0:128], c.bg[:C, 0:1], None, ALU.mult, None, [pb, c.bg])
                ts("dve", c.kd, c.kd[:C, :], pb[:C, 0:128], c.ekd[:C, 0:1], None, ALU.mult, None, [pb, c.ekd])
                yield
                psK = psf.get()
                mm(psK, psK[:C, 0:C], c.kT[:, cs], c.kT[:, cs], [c.kT], inc=False)
                mm(psK, psK[:C, 128:128 + C], c.kT[:, cs], c.qT[:, cs], [c.kT, c.qT])
                stt("dve", c.A, c.A[:C, :C], psK[:C, 0:C], beta, c.d1[:C, :C], ALU.mult, ALU.mult, [psK, GB, c.d1])
                tt("dve", c.PT, c.PT[:C, :C], psK[:C, 128:128 + C], c.d2[:C, :C], ALU.mult, [psK, c.d2])
                yield
                pb2 = psf.get()
                tr(pb2, pb2[:C, 0:C], c.A[:C, :C], identf[:C, :C], [c.A, identf])
                tt("dve", c.RT[0], c.RT[0][:C, :C], identf[:C, :C], pb2[:C, 0:C], ALU.subtract, [identf, pb2])
                cp("act", c.MT[0], c.MT[0][:C, :C], pb2[:C, 0:C], [pb2])
                tt("pool", c.Rm[0], c.Rm[0][:C, :C], identf[:C, :C], c.A[:C, :C], ALU.subtract, [identf, c.A])
                yield
                M, MT, Rm, RT = c.A, c.MT[0], c.Rm[0], c.RT[0]
                nl = {128: 6, 64: 5}[C]
                for l in range(nl):
                    last = (l == nl - 1)
                    Mn, MTn, Rn, RTn = c.M[l % 2], c.MT[(l + 1) % 2], c.Rm[(l + 1) % 2], c.RT[(l + 1) % 2]
                    psM = psf.get()
                    mm(psM, psM[:C, 0:C], MT[:C, :C], M[:C, :C], [MT, M], inc=last)
                    if not last:
                        mm(psM, psM[:C, 128:128 + C], M[:C, :C], MT[:C, :C], [M, MT])
                    cp("act", Mn, Mn[:C, :C], psM[:C, 0:C], [psM])
                    if not last:
                        cp("dve", MTn, MTn[:C, :C], psM[:C, 128:128 + C], [psM])
                    yield
                    psR = psf.get()
                    mm(psR, psR[:C, 0:C], Mn[:C, :C], RT[:C, :C], [Mn, RT], inc=last)
                    if not last:
                        mm(psR, psR[:C, 128:128 + C], RT[:C, :C], Mn[:C, :C], [RT, Mn])
                    tt("dve", RTn, RTn[:C, :C], psR[:C, 0:C], RT[:C, :C], ALU.add, [psR, RT])
                    if not last:
                        tt("dve", Rn, Rn[:C, :C], psR[:C, 128:128 + C], Rm[:C, :C], ALU.add, [psR, Rm])
                    M, MT, Rm, RT = Mn, MTn, Rn, RTn
                    yield
                TT = RT
                c.TT = TT

            def chunk_tail(c, C, n, ti, h):
                t0 = n * C
                cs = slice(t0, t0 + C)
                TT = c.TT
                psW = psf.get()
                mm(psW, psW[:, 0:C], c.kbg[:C, :], TT[:C, :C], [c.kbg, TT])
                P.op("act", lambda e: e.mul(c.nwT[:, :C], psW[:, 0:C], -1.0), reads=[psW], writes=[c.nwT])
                psU = psf.get()
                mm(psU, psU[:C, 0:128], TT[:C, :C], c.vb[:C, :], [TT, c.vb], start=True, stop=False, inc=False)
                mm(psU, psU[:C, 0:128], c.nwT[:, :C], c.Sb[:, :], [c.nwT, c.Sb], start=False, stop=True)
                cp("dve", c.ub, c.ub[:C, :], psU[:C, 0:128], [psU])
                yield
                tt("pool", c.qgT, c.qgT[:, :C], c.qT[:, cs], c.EG[:, :C], ALU.mult, [c.qT, c.EG])
                psO = psf.get()
                mm(psO, psO[:C, 0:128], c.qgT[:, :C], c.Sb[:, :], [c.qgT, c.Sb], start=True, stop=False, inc=False)
                mm(psO, psO[:C, 0:128], c.PT[:C, :C], c.ub[:C, :], [c.PT, c.ub], start=False, stop=True)
                psS = psf.get()
                mm(psS, psS[:, 0:128], c.kd[:C, :], c.ub[:C, :], [c.kd, c.ub])
                stt("dve", c.S, c.S[:, :], c.S[:, :], c.egC[:, 0:1], psS[:, 0:128], ALU.mult, ALU.add, [c.S, c.egC, psS])
                cp("act", c.Sb, c.Sb[:, :], c.S[:, :], [c.S])
                act(c.jk, c.jk[:C, :], psO[:C, 0:128], AF.Square, [psO], accum=(c.ss, c.ss[:C, 0:1]))
                rsqrt_("dve", c.ss, c.ss[:C, 0:1], 1.0 / 128, EPS)
                stt("dve", c.t1, c.t1[:C, :], psO[:C, 0:128], c.ss[:C, 0:1], NG[:C, :], ALU.mult, ALU.mult, [psO, c.ss, NG])
                tt("pool", c.t2, c.t2[:C, :], c.t1[:C, :], c.zs[:C, n, :], ALU.mult, [c.t1, c.zs])
                pb3 = psb.get()
                tr(pb3, pb3[:, 0:C], c.t2[:C, :], identb[:C, :C], [c.t2, identb])
                cp("act", c.OTb, c.OTb[:, cs], pb3[:, 0:C], [pb3])

            def run_part(pairs, si, heads, C, tp0, Tpart, first, last_part):
                row0, T = SEQS[si]
                r0 = row0 + tp0
                for pr, h in zip(pairs, heads):
                    c = pr[0]
                    P.dma("sp", c.qT[:, 0:Tpart], QKVT[h * 128:(h + 1) * 128, r0:r0 + Tpart], writes=[c.qT])
                    P.dma("sp", c.kT[:, 0:Tpart], QKVT[(4 + h) * 128:(5 + h) * 128, r0:r0 + Tpart], writes=[c.kT])
                    P.dma("sp", c.vT[:, 0:Tpart], QKVT[(8 + h) * 128:(9 + h) * 128, r0:r0 + Tpart], writes=[c.vT])
                    P.dma("sp", c.zs[:C, 0:Tpart // C, :], Z[r0:r0 + Tpart, h * 128:(h + 1) * 128].rearrange("(n p) d -> p n d", p=C),
                          writes=[c.zs])
                    if first:
                        P.dma("sp", c.S[:, :], S0[l2, si, h], writes=[c.S])
                        cp("act", c.Sb, c.Sb[:, :], c.S[:, :], [c.S])
                nch = Tpart // C

                def tix(n):
                    return (tp0 // 128 + n) if si == 0 else 32 + (si - 1)

                def rr(gens):
                    live = list(gens)
                    while live:
                        nxt = []
                        for g in live:
                            try:
                                next(g)
                                nxt.append(g)
                            except StopIteration:
                                pass
                        live = nxt

                rr([chunk_head(pr[0], C, 0, tix(0), h) for pr, h in zip(pairs, heads)])
                for n in range(nch):
                    gens = [chunk_tail(pr[n % 2], C, n, tix(n), h) for pr, h in zip(pairs, heads)]
                    if n + 1 < nch:
                        gens += [chunk_head(pr[(n + 1) % 2], C, n + 1, tix(n + 1), h) for pr, h in zip(pairs, heads)]
                    rr(gens)
                for pr, h in zip(pairs, heads):
                    c = pr[0]
                    P.dma("sp", OT[h * 128:(h + 1) * 128, r0:r0 + Tpart], c.OTb[:, 0:Tpart], reads=[c.OTb])
                    if last_part:
                        P.dma("sp", gdn_S[l2, si, h], c.S[:, :], reads=[c.S])

            QTR = TP // 4
            pairs = [mk_pair(t, QTR) for t in "abcd"]
            if not OPTS.get('gdn_sample_only'):
                for qtr in range(4):
                    run_part(pairs, 0, [0, 1, 2, 3], 128, qtr * QTR, QTR, qtr == 0, qtr == 3)
            for s in range(OPTS.get('gdn_nss', NSS)):
                run_part(pairs, 1 + s, [0, 1, 2, 3], 64, 0, TS, True, True)
            P.flush()

    def phase_out(L, w_out2d):
        with ExitStack() as st:
            psf = Pool_([pst(st, f"pof{i}", F32) for i in range(6)])
            psb = Pool_([pst(st, f"pob{i}", BF16) for i in range(2)])
            wo = sb(st, "wo", [128, 8, D], BF16)
            load_w(wo, w_out2d, 0, 512, 0)
            load_w(wo, w_out2d, 512, 512, 512)
            gB = load_gain(st, "g2", norm_ffn_g[L, :])
            OTr = OT.rearrange("(k p) r -> p k r", p=128)
            for (g0, gn, gtiles) in GROUPS:
                P.dma("sp", HT[:, :, g0:g0 + gn], OTr[:, :, g0:g0 + gn], writes=[HTt[i] for i in gtiles])
            xp = Pool_([sb(st, f"xo{i}", [128, D], F32) for i in range(2)])
            xnp = Pool_([sb(st, f"xn{i}", [128, D], F32) for i in range(3)])
            pools = norm_pools(st, psb)
            xsrc = x_all if L == 0 else X
            pend = None
            for ti, (r0, n) in enumerate(TILES):
                xt = xp.get()
                P.dma("sp", xt[:n, :], xsrc[r0:r0 + n, :], writes=[xt])
                xn = xnp.get()
                for half in range(2):
                    ps = psf.get()
                    for kc in range(8):
                        mm(ps, ps[:n, :], HT[:, kc, r0:r0 + n], wo[:, kc, half * 512:(half + 1) * 512], [HTt[ti], wo],
                           start=(kc == 0), stop=(kc == 7), inc=(kc == 7))
                    tt("dve", xn, xn[:n, half * 512:(half + 1) * 512], ps[:n, :], xt[:n, half * 512:(half + 1) * 512], ALU.add, [ps, xt])
                P.dma("sp", X[r0:r0 + n, :], xn[:n, :], reads=[xn])
                if pend is not None:
                    norm_b(pend[0], pend[1], pools)
                pend = (ti, norm_a(ti, xn, gB, pools))
            norm_b(pend[0], pend[1], pools)
            P.flush()

    def phase_ffn1(L):
        with ExitStack() as st:
            psf = Pool_([pst(st, f"pff{i}", F32) for i in range(8)])
            wgs = Pool_([sb(st, f"wg{i}", [128, 8, 512], BF16) for i in range(2)])
            wus = Pool_([sb(st, f"wu{i}", [128, 8, 512], BF16) for i in range(2)])
            sgp = Pool_([sb(st, f"sg{i}", [128, 512], F32) for i in range(3)])
            abp = Pool_([sb(st, f"ab{i}", [128, 512], BF16) for i in range(4)])
            for fb in range(6):
                f0 = fb * 512
                nf = min(512, DFF - f0)
                wg = wgs.get()
                wu = wus.get()
                load_w(wg, ffn_w_gate[L], f0, nf)
                load_w(wu, ffn_w_up[L], f0, nf)
                for (g0, gn, gtiles) in GROUPS:
                    hts = [HTt[i] for i in gtiles]
                    for c in range(nf // 128):
                        psg = psf.get()
                        for kc in range(8):
                            mm(psg, psg[:, 0:gn], wg[:, kc, c * 128:(c + 1) * 128], HT[:, kc, g0:g0 + gn], [wg] + hts,
                               start=(kc == 0), stop=(kc == 7), inc=(kc == 7))
                        psu = psf.get()
                        for kc in range(8):
                            mm(psu, psu[:, 0:gn], wu[:, kc, c * 128:(c + 1) * 128], HT[:, kc, g0:g0 + gn], [wu] + hts,
                               start=(kc == 0), stop=(kc == 7), inc=(kc == 7))
                        sg = sgp.get()
                        act(sg, sg[:, 0:gn], psg[:, 0:gn], AF.Silu, [psg])
                        ab = abp.get()
                        tt("dve", ab, ab[:, 0:gn], psu[:, 0:gn], sg[:, 0:gn], ALU.mult, [psu, sg])
                        P.dma("sp", ACTS[f0 + c * 128:f0 + (c + 1) * 128, g0:g0 + gn], ab[:, 0:gn], reads=[ab])
            P.flush()

    def phase_ffn2(L):
        with ExitStack() as st:
            psf = Pool_([pst(st, f"pdf{i}", F32) for i in range(6)])
            psb = Pool_([pst(st, f"pdb{i}", BF16) for i in range(2)])
            wd = sb(st, "wd", [128, 22, D], BF16)
            wdr = ffn_w_down[L].rearrange("(c p) d -> p c d", p=128)
            for i in range(0, 22, 4):
                j = min(22, i + 4)
                P.dma("pool", wd[:, i:j, :], wdr[:, i:j, :], writes=[wd])
            last = (L == 3)
            gB = None if last else load_gain(st, "g3", norm_mix_g[L + 1, :])
            abp = Pool_([sb(st, f"a2{i}", [128, 22, 512], BF16) for i in range(2)])
            xp = Pool_([sb(st, f"xd{i}", [128, D], F32) for i in range(2)])
            xnp = Pool_([sb(st, f"xm{i}", [128, D], F32) for i in range(3)])
            pools = norm_pools(st, psb)
            ACr = ACTS.rearrange("(c p) r -> p c r", p=128)
            pend = None
            for (g0, gn, gtiles) in GROUPS:
                ab = abp.get()
                P.dma("sp", ab[:, 0:11, 0:gn], ACr[:, 0:11, g0:g0 + gn], writes=[ab])
                P.dma("sp", ab[:, 11:22, 0:gn], ACr[:, 11:22, g0:g0 + gn], writes=[ab])
                for ti in gtiles:
                    r0, n = TILES[ti]
                    off = r0 - g0
                    xt = xp.get()
                    P.dma("sp", xt[:n, :], X[r0:r0 + n, :], writes=[xt])
                    xn = xnp.get()
                    for half in range(2):
                        ps = psf.get()
                        for c in range(22):
                            mm(ps, ps[:n, :], ab[:, c, off:off + n], wd[:, c, half * 512:(half + 1) * 512], [ab, wd],
                               start=(c == 0), stop=(c == 21), inc=(c == 21))
                        tt("dve", xn, xn[:n, half * 512:(half + 1) * 512], ps[:n, :], xt[:n, half * 512:(half + 1) * 512], ALU.add, [ps, xt])
                    if last:
                        P.dma("sp", y_all[r0:r0 + n, :], xn[:n, :], reads=[xn])
                    else:
                        P.dma("sp", X[r0:r0 + n, :], xn[:n, :], reads=[xn])
                        if pend is not None:
                            norm_b(pend[0], pend[1], pools)
                        pend = (ti, norm_a(ti, xn, gB, pools))
            if pend is not None:
                norm_b(pend[0], pend[1], pools)
            P.flush()

    def phase_proj_odd(l2):
        W = w_in_odd[l2]
        with ExitStack() as st:
            psf = Pool_([pst(st, f"ppf{i}", F32) for i in range(6)])
            psb = Pool_([pst(st, f"ppb{i}", BF16) for i in range(2)])
            wbs = Pool_([sb(st, f"wq{i}", [128, 8, 520], BF16) for i in range(3)])
            QG = sb(st, "QG", [128, 128], F32)
            KG = sb(st, "KG", [128, 128], F32)
            bfB = sb(st, "bfB", [128, 8], F32)
            P.dma("sp", QG[:, :], bcast(fox_q_norm_g[l2, :], 128, 128), writes=[QG])
            P.dma("sp", KG[:, :], bcast(fox_k_norm_g[l2, :], 128, 128), writes=[KG])
            P.dma("sp", bfB[:, :], bcast(fox_b_f[l2, :], 128, 8), writes=[bfB])
            ts("dve", QG, QG[:, :], QG[:, :], 128.0 ** -0.5, None, ALU.mult, None, [QG])
            KTb = sb(st, "KTb", [128, 4, R], BF16)
            KTbt = [Tl(KTb.t, f"KTb{i}") for i in range(len(TILES))]
            LFR = sb(st, "LFR", [128, 36, 8], F32)
            P.op("pool", lambda e: e.memset(LFR[:, :, :], 0.0), writes=[LFR])
            LF = sb(st, "LF", [128, 36, 8], F32)
            jkp = Pool_([sb(st, f"pj{i}", [128, 128], BF16) for i in range(2)])
            ssp = Pool_([sb(st, f"pss{i}", [128, 4], F32) for i in range(3)])
            qnp = Pool_([sb(st, f"qn{i}", [128, 512], BF16) for i in range(4)])
            knp = Pool_([sb(st, f"kn{i}", [128, 512], F32) for i in range(3)])
            wq = [wbs.get()]
            load_w(wq[0], W, 0, 512)
            for blk in range(6):
                kind = blk // 2
                wb = wq[blk]
                if blk + 1 < 6:
                    wq.append(wbs.get())
                    load_w(wq[blk + 1], W, (blk + 1) * 512, 520 if blk + 1 == 5 else 512)
                stt_ = {}

                def stage0(ti, blk=blk, wb=wb):
                    r0, n = TILES[ti]
                    ps = psf.get()
                    for kc in range(8):
                        mm(ps, ps[:n, :], HT[:, kc, r0:r0 + n], wb[:, kc, 0:512], [HTt[ti], wb],
                           start=(kc == 0), stop=(kc == 7), inc=(kc == 7))
                    stt_[ti] = {"ps": ps}
                    if blk == 5:
                        ps2 = psf.get()
                        for kc in range(8):
                            mm(ps2, ps2[:n, 0:8], HT[:, kc, r0:r0 + n], wb[:, kc, 512:520], [HTt[ti], wb],
                               start=(kc == 0), stop=(kc == 7), inc=(kc == 7))
                        stt_[ti]["ps2"] = ps2

                def stage1(ti, blk=blk, kind=kind):
                    r0, n = TILES[ti]
                    ps = stt_[ti]["ps"]
                    if blk == 5:
                        ps2 = stt_[ti]["ps2"]
                        cp("act", LFR, LFR[:n, ti, :], ps2[:n, 0:8], [ps2])
                    if kind < 2:
                        ss = ssp.get()
                        for h in range(4):
                            jk = jkp.get()
                            act(jk, jk[:n, :], ps[:n, h * 128:(h + 1) * 128], AF.Square, [ps], accum=(ss, ss[:n, h:h + 1]))
                        rsqrt_("dve", ss, ss[:n, :], 1.0 / 128, EPS)
                        G = QG if kind == 0 else KG
                        if kind == 0:
                            qn = qnp.get()
                            for h in range(4):
                                stt("dve", qn, qn[:n, h * 128:(h + 1) * 128], ps[:n, h * 128:(h + 1) * 128], ss[:n, h:h + 1], G[:n, :],
                                    ALU.mult, ALU.mult, [ps, ss, G])
                        else:
                            kn = knp.get()
                            for h in range(4):
                                stt("dve", kn, kn[:n, h * 128:(h + 1) * 128], ps[:n, h * 128:(h + 1) * 128], ss[:n, h:h + 1], G[:n, :],
                                    ALU.mult, ALU.mult, [ps, ss, G])
                            P.dma("sp", fox_k[l2, r0:r0 + n, (blk - 2) * 512:(blk - 1) * 512], kn[:n, :], reads=[kn])
                            qn = qnp.get()
                            cp("pool", qn, qn[:n, :], kn[:n, :], [kn])
                        stt_[ti]["qn"] = qn
                    else:
                        vf = knp.get()
                        cp("act", vf, vf[:n, :], ps[:n, :], [ps])
                        P.dma("sp", fox_v[l2, r0:r0 + n, (blk - 4) * 512:(blk - 3) * 512], vf[:n, :], reads=[vf])
                        vb = qnp.get()
                        cp("dve", vb, vb[:n, :], ps[:n, :], [ps])
                        P.dma("sp", V[r0:r0 + n, (blk - 4) * 512:(blk - 3) * 512], vb[:n, :], reads=[vb])

                def stage2(ti, kind=kind):
                    r0, n = TILES[ti]
                    st_ = stt_.pop(ti)
                    if kind < 2:
                        qn = st_["qn"]
                        pb = psb.get()
                        for h in range(4):
                            tr(pb, pb[:, h * 128:h * 128 + n], qn[:n, h * 128:(h + 1) * 128], identb[:n, :n], [qn, identb], inc=(h == 3))
                        src = pb.t[:, 0:512].rearrange("p (k m) -> p k m", k=4)[:, :, 0:n]
                        cp("act", KTbt[ti], KTb[:, :, r0:r0 + n], src, [pb])

                NT = len(TILES)
                for t in range(NT + 2):
                    if t < NT:
                        stage0(t)
                    if 0 <= t - 2 < NT:
                        stage2(t - 2)
                    if 0 <= t - 1 < NT:
                        stage1(t - 1)
                if kind < 2:
                    dst = (QT if kind == 0 else KT)[(blk % 2) * 512:(blk % 2 + 1) * 512, :].rearrange("(h p) r -> p h r", p=128)
                    for hh in range(4):
                        P.dma("sp", dst[:, hh, :], KTb[:, hh, :], reads=KTbt)
            for h in range(8):
                ts("dve", LFR, LFR[:, :, h], LFR[:, :, h], bfB[:, h:h + 1], None, ALU.add, None, [LFR, bfB])
            t2 = sb(st, "lt2", [128, 36, 8], F32)
            abs_(t2, t2[:, :, :], LFR, LFR[:, :, :])
            act(t2, t2[:, :, :], t2[:, :, :], AF.Exp, [t2], scale=-1.0)
            act(t2, t2[:, :, :], t2[:, :, :], AF.Ln, [t2], bias=1.0)
            ts("dve", LF, LF[:, :, :], LFR[:, :, :], 0.0, None, ALU.min, None, [LFR])
            tt("dve", LF, LF[:, :, :], LF[:, :, :], t2[:, :, :], ALU.subtract, [LF, t2])
            for ti, (r0, n) in enumerate(TILES):
                P.dma("sp", fox_logf[l2, r0:r0 + n, :], LF[:n, ti, :], reads=[LF])
            CUM = sb(st, "CUM", [128, 32, 8], F32)
            TOT = sb(st, "TOT", [128, 32, 8], F32)
            carry = sb(st, "carry", [128, 8], F32)
            psc = psf.get()
            for r in range(32):
                mm(psc, psc[:, r * 16:r * 16 + 8], Uf[:, :], LF[:, r, :], [Uf, LF], inc=False)
                mm(psc, psc[:, r * 16 + 8:r * 16 + 16], onesf[:, :], LF[:, r, :], [onesf, LF], inc=(r == 31))
            pv = psc.t[:, :].rearrange("p (r c) -> p r c", c=16)
            cp("act", CUM, CUM[:, :, :], pv[:, :, 0:8], [psc])
            cp("dve", TOT, TOT[:, :, :], pv[:, :, 8:16], [psc])
            P.op("dve", lambda e: e.memset(carry[:, :], 0.0), writes=[carry])
            for r in range(32):
                tt("dve", CCp, CCp[:, r, :], CUM[:, r, :], carry[:, :], ALU.add, [CUM, carry])
                tt("dve", carry, carry[:, :], carry[:, :], TOT[:, r, :], ALU.add, [carry, TOT])
                if r % 4 == 3:
                    cp("dve", CEp, CEp[:, r // 4, :], carry[:, :], [carry])
            CL = sb(st, "CL", [128, 8, 8], F32)
            CUMs = sb(st, "CUMs", [128, 9, 8], F32)
            TOTs = sb(st, "TOTs", [128, 9, 8], F32)
            for s in range(NSS):
                P.dma("sp", CL[:, :, :], clf[l2, s].rearrange("(b p) h -> p b h", p=128), writes=[CL])
                psc = psf.get()
                for b in range(8):
                    mm(psc, psc[:, b * 16:b * 16 + 8], Uf[:, :], CL[:, b, :], [Uf, CL], inc=False)
                    mm(psc, psc[:, b * 16 + 8:b * 16 + 16], onesf[:, :], CL[:, b, :], [onesf, CL], inc=False)
                mm(psc, psc[:TS, 128:136], Uf[:TS, :TS], LF[:TS, 32 + s, :], [Uf, LF], inc=False)
                mm(psc, psc[:, 136:144], onesf[:TS, :], LF[:TS, 32 + s, :], [onesf, LF])
                pv = psc.t[:, 0:144].rearrange("p (r c) -> p r c", c=16)
                cp("act", CUMs, CUMs[:, :, :], pv[:, :, 0:8], [psc])
                cp("dve", TOTs, TOTs[:, :, :], pv[:, :, 8:16], [psc])
                P.op("dve", lambda e: e.memset(carry[:, :], 0.0), writes=[carry])
                for b in range(9):
                    tt("dve", CCs[s], CCs[s][:, b, :], CUMs[:, b, :], carry[:, :], ALU.add, [CUMs, carry])
                    tt("dve", carry, carry[:, :], carry[:, :], TOTs[:, b, :], ALU.add, [carry, TOTs])
                cp("dve", CEs[s], CEs[s][:, :], carry[:, :], [carry])
            P.flush()

    def phase_attn_prompt(l2):
        with ExitStack() as st:
            psS_ = Pool_([pst(st, f"paS{i}", F32) for i in range(4)])
            psA = [[pst(st, f"paO{i}", F32), pst(st, f"paL{i}", F32)] for i in range(2)]
            KTh = Pool_([sb(st, f"KTh{i}", [128, TP], BF16) for i in range(2)])
            QTh = Pool_([sb(st, f"QTh{i}", [128, TP], BF16) for i in range(2)])
            Vh = Pool_([sb(st, f"Vh{i}", [128, 32, 128], BF16) for i in range(2)])
            OTb = Pool_([sb(st, f"aot{i}", [128, TP], BF16) for i in range(2)])
            BIp = Pool_([sb(st, f"BI{i}", [128, 32], F32) for i in range(3)])
            ptp = Pool_([sb(st, f"pt{i}", [128, 512], BF16) for i in range(4)])
            rlp = Pool_([sb(st, f"rl{i}", [128, 512], F32) for i in range(2)])
            qi = 0
            for h in range(8):
                kt, qt, vh, ot = KTh.get(), QTh.get(), Vh.get(), OTb.get()
                P.dma("sp", kt[:, :], KT[h * 128:(h + 1) * 128, 0:TP], writes=[kt])
                P.dma("sp", qt[:, :], QT[h * 128:(h + 1) * 128, 0:TP], writes=[qt])
                P.dma("sp", vh[:, :, :], V[0:TP, h * 128:(h + 1) * 128].rearrange("(b p) d -> p b d", p=128), writes=[vh])
                items = [(Q, j) for Q in range(8) for j in range(4 * (Q + 1))]
                stA = {}
                acc = {}

                def stageA(k, kt=kt, qt=qt, h=h):
                    Q, j = items[k]
                    J = 4 * (Q + 1)
                    if j == 0:
                        BI = BIp.get()
                        ts("dve", BI, BI[:, 0:J], CCp[:, 0:J, h], CEp[:, Q, h:h + 1], -1.0, ALU.subtract, ALU.mult, [CCp, CEp])
                        stA["BI"] = BI
                    BI = stA["BI"]
                    diag = j >= 4 * Q
                    qlo = (j - 4 * Q) * 128 if diag else 0
                    ps = psS_.get()
                    mm(ps, ps[:, qlo:512], kt[:, j * 128:(j + 1) * 128], qt[:, Q * 512 + qlo:(Q + 1) * 512], [kt, qt])
                    pt = ptp.get()
                    act(pt, pt[:, qlo:512], ps[:, qlo:512], AF.Exp, [ps, BI], bias=BI[:, j:j + 1])
                    if diag:
                        tt("pool", pt, pt[:, qlo:qlo + 128], pt[:, qlo:qlo + 128], maskb[:, :], ALU.mult, [pt, maskb])
                    stA[k] = (pt, qlo)

                def stageB(k, vh=vh, ot=ot):
                    nonlocal qi
                    Q, j = items[k]
                    J = 4 * (Q + 1)
                    pt, qlo = stA.pop(k)
                    if j == 0:
                        acc["p"] = psA[qi % 2]
                        qi += 1
                    psO, psL = acc["p"]
                    mm(psO, psO[:, qlo:512], vh[:, j, :], pt[:, qlo:512], [vh, pt], start=(j == 0), stop=(j == J - 1), inc=(j == J - 1))
                    mm(psL, psL[:, qlo:512], onesb[:, :], pt[:, qlo:512], [onesb, pt], start=(j == 0), stop=(j == J - 1), inc=(j == J - 1))
                    if j == J - 1:
                        rl = rlp.get()
                        P.op("dve", lambda e, rl=rl, psL=psL: e.reciprocal(rl[:, :], psL[:, :]), reads=[psL], writes=[rl])
                        tt("dve", ot, ot[:, Q * 512:(Q + 1) * 512], psO[:, :], rl[:, :], ALU.mult, [psO, rl])

                LA = 2
                for k in range(len(items) + LA):
                    if k < len(items):
                        stageA(k)
                    if k >= LA:
                        stageB(k - LA)
                P.dma("sp", OT[h * 128:(h + 1) * 128, 0:TP], ot[:, :], reads=[ot])
            P.flush()

    def phase_attn_sample(l2):
        with ExitStack() as st:
            psf = Pool_([pst(st, f"psf{i}", F32) for i in range(6)])
            psb = Pool_([pst(st, f"psb{i}", BF16) for i in range(2)])
            KCp = Pool_([sb(st, f"KC{i}", [128, 8, D], BF16) for i in range(2)])
            VCp = Pool_([sb(st, f"VC{i}", [128, 8, D], BF16) for i in range(2)])
            KTc = Pool_([sb(st, f"KTc{i}", [128, 1024], BF16) for i in range(3)])
            KNp = Pool_([sb(st, f"KN{i}", [128, 8, TS], BF16) for i in range(2)])
            QNp = Pool_([sb(st, f"QN{i}", [128, 8, TS], BF16) for i in range(2)])
            VNp = Pool_([sb(st, f"VN{i}", [128, D], BF16) for i in range(2)])
            OTs = Pool_([sb(st, f"OTs{i}", [128, 8, TS], BF16) for i in range(2)])
            BIp = Pool_([sb(st, f"BIs{i}", [128, 9], F32) for i in range(3)])
            ptp = Pool_([sb(st, f"pts{i}", [128, 9 * TS], BF16) for i in range(3)])
            rlp = Pool_([sb(st, f"rls{i}", [128, TS], F32) for i in range(2)])
            KTr = KT.rearrange("(h p) r -> p h r", p=128)
            QTr = QT.rearrange("(h p) r -> p h r", p=128)
            for s in range(NSS):
                row0 = TP + TS * s
                kc_, vc_ = KCp.get(), VCp.get()
                for b0 in range(0, 8, 2):
                    P.dma("pool", kc_[:, b0:b0 + 2, :], ck[l2, s].rearrange("(b p) f -> p b f", p=128)[:, b0:b0 + 2, :], writes=[kc_])
                    P.dma("pool", vc_[:, b0:b0 + 2, :], cv[l2, s].rearrange("(b p) f -> p b f", p=128)[:, b0:b0 + 2, :], writes=[vc_])
                kn, qn, vn, ots = KNp.get(), QNp.get(), VNp.get(), OTs.get()
                P.dma("sp", kn[:, :, :], KTr[:, :, row0:row0 + TS], writes=[kn])
                P.dma("sp", qn[:, :, :], QTr[:, :, row0:row0 + TS], writes=[qn])
                P.dma("sp", vn[:TS, :], V[row0:row0 + TS, :], writes=[vn])
                for h in range(8):
                    pb = psb.get()
                    for b in range(8):
                        tr(pb, pb[:, b * 128:(b + 1) * 128], kc_[:, b, h * 128:(h + 1) * 128], identb[:, :], [kc_, identb], inc=(b == 7))
                    ktc = KTc.get()
                    cp("dve", ktc, ktc[:, :], pb[:, :], [pb])
                    BI = BIp.get()
                    ts("dve", BI, BI[:, 0:9], CCs[s][:, 0:9, h], CEs[s][:, h:h + 1], -1.0, ALU.subtract, ALU.mult, [CCs[s], CEs[s]])
                    ps = psf.get()
                    for b in range(8):
                        mm(ps, ps[:, b * TS:(b + 1) * TS], ktc[:, b * 128:(b + 1) * 128], qn[:, h, :], [ktc, qn], inc=(b == 7))
                    ps2 = psf.get()
                    mm(ps2, ps2[:TS, 0:TS], kn[:, h, :], qn[:, h, :], [kn, qn])
                    pt = ptp.get()
                    for b in range(8):
                        act(pt, pt[:, b * TS:(b + 1) * TS], ps[:, b * TS:(b + 1) * TS], AF.Exp, [ps, BI], bias=BI[:, b:b + 1])
                    act(pt, pt[:TS, 8 * TS:9 * TS], ps2[:TS, 0:TS], AF.Exp, [ps2, BI], bias=BI[:TS, 8:9])
                    tt("pool", pt, pt[:TS, 8 * TS:9 * TS], pt[:TS, 8 * TS:9 * TS], maskb[:TS, :TS], ALU.mult, [pt, maskb])
                    psO = psf.get()
                    psL = psf.get()
                    for b in range(8):
                        mm(psO, psO[:, 0:TS], vc_[:, b, h * 128:(h + 1) * 128], pt[:, b * TS:(b + 1) * TS], [vc_, pt],
                           start=(b == 0), stop=False, inc=False)
                    mm(psO, psO[:, 0:TS], vn[:TS, h * 128:(h + 1) * 128], pt[:TS, 8 * TS:9 * TS], [vn, pt], start=False, stop=True)
                    for b in range(8):
                        mm(psL, psL[:, 0:TS], onesb[:, :], pt[:, b * TS:(b + 1) * TS], [onesb, pt], start=(b == 0), stop=False, inc=False)
                    mm(psL, psL[:, 0:TS], onesb[:TS, :], pt[:TS, 8 * TS:9 * TS], [onesb, pt], start=False, stop=True)
                    rl = rlp.get()
                    P.op("dve", lambda e, rl=rl, psL=psL: e.reciprocal(rl[:, :], psL[:, 0:TS]), reads=[psL], writes=[rl])
                    tt("dve", ots, ots[:, h, :], psO[:, 0:TS], rl[:, :], ALU.mult, [psO, rl])
                P.dma("sp", OT.rearrange("(h p) r -> p h r", p=128)[:, :, row0:row0 + TS], ots[:, :, :], reads=[ots])
            P.flush()

    plan = [("p0", phase0)]
    for L in range(4):
        l2 = L // 2
        if L % 2 == 0:
            plan.append((f"pe{L}", lambda l2=l2: phase_proj_even(l2)))
            plan.append((f"gdn{L}", lambda l2=l2: phase_gdn(l2)))
            plan.append((f"out{L}", lambda L=L, l2=l2: phase_out(L, w_out_even[l2])))
        else:
            plan.append((f"po{L}", lambda l2=l2: phase_proj_odd(l2)))
            plan.append((f"ap{L}", lambda l2=l2: phase_attn_prompt(l2)))
            plan.append((f"as{L}", lambda l2=l2: phase_attn_sample(l2)))
            plan.append((f"out{L}", lambda L=L, l2=l2: phase_out(L, w_out_odd[l2])))
        plan.append((f"f1_{L}", lambda L=L: phase_ffn1(L)))
        plan.append((f"f2_{L}", lambda L=L: phase_ffn2(L)))
    for name, fn in plan:
        if OPTS.get('only') and name not in OPTS['only']:
            continue
        fn()
        if upto is not None and name == upto:
            break
    if dbg:
        HTd = nc.dram_tensor("HTd", [128, 8, R], BF16, kind="ExternalOutput").ap()
        P.dma("sp", HTd, HT[:, :, :], reads=HTt)
        P.flush()
    P.stack.close()
    return nc


def make_consts():
    i = np.arange(128)
    ident = np.eye(128, dtype=np.float32)
    U = (i[:, None] <= i[None, :]).astype(np.float32)
    ones = np.ones((128, 128), np.float32)
    negls = np.where(i[:, None] > i[None, :], 0.0, -30000.0).astype(np.float32)
    negu = np.where(i[None, :] >= i[:, None], 0.0, -30000.0).astype(np.float32)
    return dict(c_ident=ident, c_U=U, c_ones=ones, c_negls=negls, c_negu=negu)


def make_in_maps(inp):
    f = lambda a: np.ascontiguousarray(np.asarray(a, dtype=np.float32))
    consts = make_consts()
    shared = {k: f(inp[k]) for k in (
        "norm_mix_g", "norm_ffn_g", "w_in_even", "gdn_conv_w", "gdn_a_log", "gdn_dt_bias", "gdn_norm_g",
        "sconv_w", "w_out_even", "w_in_odd", "fox_b_f", "fox_q_norm_g", "fox_k_norm_g", "w_out_odd",
        "ffn_w_gate", "ffn_w_up", "ffn_w_down")}
    shared.update(consts)
    xp = f(inp["x_prompt"]); xs = f(inp["x_sample"])
    ckk = f(inp["cache_fox_k"]).reshape(2, 32, PAST, D)
    cvv = f(inp["cache_fox_v"]).reshape(2, 32, PAST, D)
    clf = f(inp["cache_fox_logf"])
    sS = f(inp["state_gdn_S"]); sc = f(inp["state_gdn_conv"]); ss = f(inp["state_sconv"])
    maps = []
    for c in range(NCORES):
        sl = slice(4 * c, 4 * c + 4)
        m = dict(shared)
        m["x_all"] = np.ascontiguousarray(np.concatenate([xp[c % 4], xs[sl].reshape(NSS * TS, D)], axis=0))
        m["ck"] = np.ascontiguousarray(ckk[:, sl])
        m["cv"] = np.ascontiguousarray(cvv[:, sl])
        m["clf"] = np.ascontiguousarray(clf[:, sl])
        m["S0"] = np.ascontiguousarray(np.concatenate([np.zeros((2, 1, 4, 128, 128), np.float32), sS[:, sl]], axis=1))
        m["conv0"] = np.ascontiguousarray(np.concatenate([np.zeros((2, 1, 3, 1536), np.float32), sc[:, sl]], axis=1))
        m["sc0"] = np.ascontiguousarray(np.concatenate([np.zeros((2, 1, 2, 512), np.float32), ss[:, sl]], axis=1))
        maps.append(m)
    return maps


def kernel(**inp):
    nc = build()
    maps = make_in_maps(inp)
    res = run_bass_kernel_spmd(nc, maps, core_ids=list(range(NCORES))).results
    g = lambda k, c: np.asarray(res[c][k], dtype=np.float32)
    y_p = np.stack([g("y_all", c)[:TP] for c in range(4)])
    y_s = np.concatenate([g("y_all", c)[TP:].reshape(NSS, TS, D) for c in range(8)])
    def fox(k, last):
        p = np.stack([g(k, c)[:, :TP] for c in range(4)], axis=1)
        s = np.concatenate([g(k, c)[:, TP:].reshape(2, NSS, TS, last) for c in range(8)], axis=1)
        return p, s
    pk, sk = fox("fox_k", D); pv, sv = fox("fox_v", D); pl, sl_ = fox("fox_logf", 8)
    pk = pk.reshape(2, 4, TP, 8, 128); sk = sk.reshape(2, 32, TS, 8, 128)
    pv = pv.reshape(2, 4, TP, 8, 128); sv = sv.reshape(2, 32, TS, 8, 128)
    def st(k):
        p = np.stack([g(k, c)[:, 0] for c in range(4)], axis=1)
        s = np.concatenate([g(k, c)[:, 1:] for c in range(8)], axis=1)
        return p, s
    pS, sS = st("gdn_S"); pc, scv = st("gdn_conv"); psc, ssc = st("sconv_o")
    return (y_p, y_s, pk, pv, pl, pS, pc, psc, sk, sv, sl_, sS, scv, ssc)
```

```python
import numpy as np
from contextlib import ExitStack
import concourse.bass as bass
import concourse.mybir as mybir
from concourse.bass_utils import run_bass_kernel_spmd

F32 = mybir.dt.float32
BF16 = mybir.dt.bfloat16
AF = mybir.ActivationFunctionType
ALU = mybir.AluOpType

D = 1024
TP = 4096
NSS = 4
TS = 64
R = TP + NSS * TS
PAST = 1024
DFF = 2816
EVEN_IN = 3592
ODD_IN = 3080
EPS = 1e-6
NCORES = 8

TILES = [(r * 128, 128) for r in range(32)] + [(TP + TS * s, TS) for s in range(NSS)]
GROUPS = [(g * 512, 512, [4 * g + i for i in range(4)]) for g in range(8)] + [(TP, 256, [32, 33, 34, 35])]
SEQS = [(0, TP)] + [(TP + TS * s, TS) for s in range(NSS)]


class Tl:
    def __init__(self, t, name=""):
        self.t = t
        self.w = None
        self.rd = {}
        self.sem = None
        self.cnt = 0
        self.name = name
        self.psum = False

    def __getitem__(self, k):
        return self.t[k]


class Eng:
    def __init__(self, name):
        self.name = name
        self.ops = []
        self.cnt = 0
        self.sem = None
        self.waited = {}


class Prog:
    def __init__(self, nc):
        self.nc = nc
        self.stack = ExitStack()
        self.E = {n: Eng(n) for n in ("pe", "act", "dve", "pool", "sp")}
        for e in self.E.values():
            e.sem = self.stack.enter_context(nc.semaphore("s_" + e.name))
        self.pending = []
        self.dsems = []
        self.free = {"sp": [], "pool": []}
        self.phase_tiles = []

    def _deps(self, eng, reads, writes):
        deps = {}

        def add(st, raw):
            if st is None:
                return
            s, v = st
            if s is eng.sem and (eng.name == "pe" or not raw):
                return
            k = id(s)
            if deps.get(k, (None, 0))[1] < v:
                deps[k] = (s, v)

        for t in reads:
            add(t.w, True)
            if t.psum:
                for st in t.rd.values():
                    add(st, False)
        for t in writes:
            add(t.w, False)
            for st in t.rd.values():
                add(st, False)
        out = []
        for k, (s, v) in deps.items():
            if eng.waited.get(k, 0) >= v:
                continue
            eng.waited[k] = v
            out.append((s, v))
        return out

    def op(self, en, fn, reads=(), writes=(), inc=True):
        eng = self.E[en]
        waits = self._deps(eng, reads, writes)
        if inc:
            eng.cnt += 1
            st = (eng.sem, eng.cnt)
        else:
            st = (eng.sem, eng.cnt + 1)
        for t in writes:
            t.w = st
            t.rd = {}
        for t in reads:
            if t in writes:
                continue
            t.rd[id(eng.sem)] = st
        eng.ops.append((waits, fn, eng.sem if inc else None, 1))

    def _dsem(self, t, qn):
        if t.sem is None:
            t.sem = {}
        if qn not in t.sem:
            if self.free[qn]:
                t.sem[qn] = self.free[qn].pop()
            else:
                ds = Eng("dsem")
                ds.sem = self.stack.enter_context(self.nc.semaphore())
                self.dsems.append(ds)
                t.sem[qn] = ds
            self.phase_tiles.append((t, qn))
        return t.sem[qn]

    def dma(self, qn, out_ap, in_ap, reads=(), writes=(), **kw):
        eng = self.E[qn]
        waits = self._deps(eng, reads, writes)
        t = writes[0] if writes else reads[0]
        ds = self._dsem(t, qn)
        ds.cnt += 16
        st = (ds.sem, ds.cnt)
        for x in writes:
            x.w = st
            x.rd = {}
        for x in reads:
            x.rd[id(ds.sem)] = st
        eng.ops.append((waits, lambda e: e.dma_start(out=out_ap, in_=in_ap, **kw), ds.sem, 16))
        self.pending.append(st)

    def flush(self):
        eng = self.E["sp"]
        best = {}
        for s, v in self.pending:
            if best.get(id(s), (s, 0))[1] < v:
                best[id(s)] = (s, v)
        waits = [(s, v) for k, (s, v) in best.items() if eng.waited.get(k, 0) < v]
        eng.ops.append((waits, None, None, 0))
        self.pending = []
        with self.nc.Block() as blk:
            for name, deco in (("pe", blk.tensor), ("act", blk.scalar), ("dve", blk.vector),
                               ("pool", blk.gpsimd), ("sp", blk.sync)):
                ops = self.E[name].ops
                if not ops:
                    continue

                def body(e, ops=ops):
                    for waits, fn, sem, amt in ops:
                        if fn is None:
                            for s, v in waits:
                                e.wait_ge(s, v)
                            continue
                        for s, v in waits[:-1]:
                            e.wait_ge(s, v)
                        ins = fn(e)
                        if waits:
                            ins._wait_ge(*waits[-1])
                        if sem is not None:
                            ins.then_inc(sem, amt)

                deco(body)
        for e in self.E.values():
            e.ops = []
        for e in self.E.values():
            for e2 in self.E.values():
                e.waited[id(e2.sem)] = e2.cnt
            for ds in self.dsems:
                e.waited[id(ds.sem)] = ds.cnt
        for t, qn in self.phase_tiles:
            self.free[qn].append(t.sem.pop(qn))
        self.phase_tiles = []


class Pool_:
    def __init__(self, tiles):
        self.tiles = tiles
        self.i = 0

    def get(self):
        t = self.tiles[self.i % len(self.tiles)]
        self.i += 1
        return t


OPTS = {}


def build(upto=None, dbg=False):
    nc = bass.Bass("TRN2", target_bir_lowering=False)

    def din(name, shape):
        return nc.dram_tensor(name, list(shape), F32, kind="ExternalInput").ap()

    def dout(name, shape):
        return nc.dram_tensor(name, list(shape), F32, kind="ExternalOutput").ap()

    def dscr(name, shape, dt):
        return nc.dram_tensor(name, list(shape), dt, kind=("ExternalOutput" if dbg else "Internal")).ap()

    x_all = din("x_all", [R, D])
    ck = din("ck", [2, NSS, PAST, D])
    cv = din("cv", [2, NSS, PAST, D])
    clf = din("clf", [2, NSS, PAST, 8])
    S0 = din("S0", [2, 5, 4, 128, 128])
    conv0 = din("conv0", [2, 5, 3, 1536])
    sc0 = din("sc0", [2, 5, 2, 512])
    norm_mix_g = din("norm_mix_g", [4, D])
    norm_ffn_g = din("norm_ffn_g", [4, D])
    w_in_even = din("w_in_even", [2, D, EVEN_IN])
    gdn_conv_w = din("gdn_conv_w", [2, 4, 1536])
    gdn_a_log = din("gdn_a_log", [2, 4])
    gdn_dt_bias = din("gdn_dt_bias", [2, 4])
    gdn_norm_g = din("gdn_norm_g", [2, 128])
    sconv_w = din("sconv_w", [2, 3, 512])
    w_out_even = din("w_out_even", [2, D, D])
    w_in_odd = din("w_in_odd", [2, D, ODD_IN])
    fox_b_f = din("fox_b_f", [2, 8])
    fox_q_norm_g = din("fox_q_norm_g", [2, 128])
    fox_k_norm_g = din("fox_k_norm_g", [2, 128])
    w_out_odd = din("w_out_odd", [2, D, D])
    ffn_w_gate = din("ffn_w_gate", [4, D, DFF])
    ffn_w_up = din("ffn_w_up", [4, D, DFF])
    ffn_w_down = din("ffn_w_down", [4, DFF, D])
    c_ident = din("c_ident", [128, 128])
    c_U = din("c_U", [128, 128])
    c_ones = din("c_ones", [128, 128])
    c_negls = din("c_negls", [128, 128])
    c_negu = din("c_negu", [128, 128])

    y_all = dout("y_all", [R, D])
    fox_k = dout("fox_k", [2, R, D])
    fox_v = dout("fox_v", [2, R, D])
    fox_logf = dout("fox_logf", [2, R, 8])
    gdn_S = dout("gdn_S", [2, 5, 4, 128, 128])
    gdn_conv = dout("gdn_conv", [2, 5, 3, 1536])
    sconv_o = dout("sconv_o", [2, 5, 2, 512])

    X = dscr("X", [R, D], F32)
    QKVT = dscr("QKVT", [1536, R], BF16)
    Z = dscr("Z", [R, 512], BF16)
    OT = dscr("OT", [D, R], BF16)
    KT = dscr("KT", [D, R], BF16)
    QT = dscr("QT", [D, R], BF16)
    V = dscr("V", [R, D], BF16)
    ACTS = dscr("ACTS", [DFF, R], BF16)

    P = Prog(nc)
    GS = P.stack

    uid = [0]

    def sb(st, name, shape, dt):
        uid[0] += 1
        name = f"{name}_{uid[0]}"
        return Tl(st.enter_context(nc.sbuf_tensor(name, list(shape), dt)), name)

    def pst(st, name, dt):
        shape = [128, 512] if dt == F32 else [128, 1024]
        uid[0] += 1
        name = f"{name}_{uid[0]}"
        t = Tl(st.enter_context(nc.psum_tensor(name, shape, dt)), name)
        t.psum = True
        return t

    def bcast(ap2d_row, nparts, ncols):
        return bass.AP(ap2d_row.tensor, ap2d_row.offset, [[0, nparts], [1, ncols]])

    def mm(out_t, out_ap, lhsT, rhs, reads, start=True, stop=True, inc=True):
        P.op("pe", lambda e: e.matmul(out_ap, lhsT, rhs, start=start, stop=stop), reads=reads, writes=[out_t], inc=inc)

    def tr(out_t, out_ap, in_ap, ident_ap, reads, inc=True):
        P.op("pe", lambda e: e.transpose(out_ap, in_ap, ident_ap), reads=reads, writes=[out_t], inc=inc)

    def act(out_t, out_ap, in_ap, func, reads, bias=None, scale=None, accum=None):
        kw = {}
        if bias is not None:
            kw["bias"] = bias
        if scale is not None:
            kw["scale"] = scale
        writes = [out_t]
        if accum is not None:
            kw["accum_out"] = accum[1]
            if accum[0] is not out_t:
                writes.append(accum[0])
        P.op("act", lambda e: e.activation(out=out_ap, in_=in_ap, func=func, **kw), reads=reads, writes=writes)

    def ts(en, out_t, out_ap, in0, s1, s2, op0, op1, reads):
        if op1 is None:
            P.op(en, lambda e: e.tensor_scalar(out_ap, in0, s1, None, op0), reads=reads, writes=[out_t])
        else:
            P.op(en, lambda e: e.tensor_scalar(out_ap, in0, s1, s2, op0, op1), reads=reads, writes=[out_t])

    def tt(en, out_t, out_ap, in0, in1, op, reads):
        P.op(en, lambda e: e.tensor_tensor(out_ap, in0, in1, op), reads=reads, writes=[out_t])

    def stt(en, out_t, out_ap, in0, scalar, in1, op0, op1, reads):
        P.op(en, lambda e: e.scalar_tensor_tensor(out_ap, in0, scalar, in1, op0, op1), reads=reads, writes=[out_t])

    def cp(en, out_t, out_ap, in_ap, reads):
        if en == "act":
            P.op("act", lambda e: e.copy(out_ap, in_ap), reads=reads, writes=[out_t])
        else:
            P.op(en, lambda e: e.tensor_copy(out_ap, in_ap), reads=reads, writes=[out_t])

    def rsqrt_(en, t, ap, mult, add, reads=()):
        n = ap.shape[0]
        act(t, ap, ap, AF.Sqrt, list(reads) + [t, epsT], bias=epsT[:n, 0:1], scale=mult)
        P.op("dve", lambda e: e.reciprocal(ap, ap), reads=[t], writes=[t])

    def abs_(t, ap, src_t, src_ap):
        ts("dve", t, ap, src_ap, -1.0, None, ALU.mult, None, [src_t])
        tt("dve", t, ap, ap, src_ap, ALU.max, [t, src_t])

    HT = GS.enter_context(nc.sbuf_tensor("HT", [128, 8, R], BF16))
    HTt = [Tl(HT, f"HT{i}") for i in range(len(TILES))]
    identf = sb(GS, "identf", [128, 128], F32)
    identb = sb(GS, "identb", [128, 128], BF16)
    Uf = sb(GS, "Uf", [128, 128], F32)
    onesf = sb(GS, "onesf", [128, 128], F32)
    onesb = sb(GS, "onesb", [128, 128], BF16)
    negls = sb(GS, "negls", [128, 128], F32)
    negu = sb(GS, "negu", [128, 128], F32)
    maskb = sb(GS, "maskb", [128, 128], BF16)
    epsT = sb(GS, "epsT", [128, 1], F32)
    GB = sb(GS, "GB", [128, 36, 8], F32)
    CCp = sb(GS, "CCp", [128, 32, 8], F32)
    CEp = sb(GS, "CEp", [128, 8, 8], F32)
    CCs = [sb(GS, f"CCs{s}", [128, 9, 8], F32) for s in range(NSS)]
    CEs = [sb(GS, f"CEs{s}", [128, 8], F32) for s in range(NSS)]

    def norm_a(ti, xt, gB, pools):
        r0, n = TILES[ti]
        junk, ssp, hnp, psb = pools
        jk = junk.get()
        ss = ssp.get()
        hn = hnp.get()
        act(jk, jk[:n, :], xt[:n, :], AF.Square, [xt], accum=(ss, ss[:n, 0:1]))
        rsqrt_("dve", ss, ss[:n, 0:1], 1.0 / D, EPS)
        stt("dve", hn, hn[:n, :], xt[:n, :], ss[:n, 0:1], gB[:n, :], ALU.mult, ALU.mult, [xt, ss, gB])
        return hn

    def norm_b(ti, hn, pools):
        r0, n = TILES[ti]
        psb = pools[3]
        pb = psb.get()
        for kc in range(8):
            tr(pb, pb[:, kc * 128:kc * 128 + n], hn[:n, kc * 128:(kc + 1) * 128], identb[:n, :n], [hn, identb], inc=(kc == 7))
        src = pb.t[:, :].rearrange("p (k m) -> p k m", k=8)[:, :, 0:n]
        cp("act", HTt[ti], HT[:, :, r0:r0 + n], src, [pb])

    def norm_tile(ti, xt, gB, pools):
        norm_b(ti, norm_a(ti, xt, gB, pools), pools)

    def norm_pools(st, psb):
        return (Pool_([sb(st, f"njk{i}", [128, D], BF16) for i in range(2)]),
                Pool_([sb(st, f"nss{i}", [128, 1], F32) for i in range(3)]),
                Pool_([sb(st, f"nhn{i}", [128, D], BF16) for i in range(3)]),
                psb)

    def load_gain(st, name, g_ap_row):
        t = sb(st, name, [128, D], F32)
        P.dma("sp", t[:, :], bcast(g_ap_row, 128, D), writes=[t])
        return t

    def phase0():
        with ExitStack() as st:
            psb = Pool_([pst(st, f"p0b{i}", BF16) for i in range(2)])
            P.dma("sp", identf[:, :], c_ident, writes=[identf])
            P.dma("sp", Uf[:, :], c_U, writes=[Uf])
            P.dma("sp", onesf[:, :], c_ones, writes=[onesf])
            P.dma("sp", negls[:, :], c_negls, writes=[negls])
            P.dma("sp", negu[:, :], c_negu, writes=[negu])
            P.op("dve", lambda e: e.memset(epsT[:, :], EPS), writes=[epsT])
            P.dma("pool", identb[:, :], c_ident, writes=[identb])
            P.dma("pool", onesb[:, :], c_ones, writes=[onesb])
            P.dma("pool", maskb[:, :], c_U, writes=[maskb])
            if OPTS.get('p0_consts_only'):
                P.flush()
                return
            gB = load_gain(st, "g0", norm_mix_g[0, :])
            xp = Pool_([sb(st, f"x0_{i}", [128, D], F32) for i in range(3)])
            pools = norm_pools(st, psb)
            for ti, (r0, n) in enumerate(TILES):
                xt = xp.get()
                P.dma("sp", xt[:n, :], x_all[r0:r0 + n, :], writes=[xt])
                norm_tile(ti, xt, gB, pools)
            P.flush()

    def load_w(wb, w2d, c0, ncol, dst0=0, q="pool"):
        src = w2d.rearrange("(k p) c -> p k c", p=128)[:, :, c0:c0 + ncol]
        P.dma(q, wb[:, :, dst0:dst0 + ncol], src, writes=[wb])

    def phase_proj_even(l2):
        W = w_in_even[l2]
        with ExitStack() as st:
            psf = Pool_([pst(st, f"pef{i}", F32) for i in range(8)])
            wbs = Pool_([sb(st, f"wb{i}", [128, 8, 520], BF16) for i in range(3)])
            cw = sb(st, "cw", [128, 12, 4], F32)
            scw = sb(st, "scw", [128, 4, 3], F32)
            for c in range(12):
                P.dma("sp", cw[:, c, :], gdn_conv_w[l2, :, c * 128:(c + 1) * 128].rearrange("t c -> c t"),
                      writes=[cw], allow_slow_non_contiguous=True)
            for c in range(4):
                P.dma("sp", scw[:, c, :], sconv_w[l2, :, c * 128:(c + 1) * 128].rearrange("t c -> c t"),
                      writes=[scw], allow_slow_non_contiguous=True)
            rb = [[sb(st, f"rb{c}_{p}", [128, 516], F32) for p in range(2)] for c in range(4)]
            rbs = Pool_([sb(st, f"rbs{i}", [128, 68], F32) for i in range(3)])
            yp = Pool_([sb(st, f"cy{i}", [128, 512], F32) for i in range(3)])
            sop = Pool_([sb(st, f"so{i}", [128, 512], F32) for i in range(5)])
            sqp = Pool_([sb(st, f"sq{i}", [128, 512], F32) for i in range(3)])
            rsp = Pool_([sb(st, f"rs{i}", [128, 512], F32) for i in range(3)])
            obp = Pool_([sb(st, f"ob{i}", [128, 512], BF16) for i in range(3)])

            o1 = 2056

            def wload(i, wb):
                if i < 3:
                    load_w(wb, W, i * 512, 512)
                elif i < 7:
                    c = i - 3
                    for j in range(3):
                        load_w(wb, W, o1 + j * 512 + c * 128, 128, dst0=j * 128)
                else:
                    load_w(wb, W, 1536, 520)

            wq = [wbs.get()]
            wload(0, wq[0])

            def next_w(i):
                if i + 1 < 8:
                    wq.append(wbs.get())
                    wload(i + 1, wq[i + 1])

            def conv_taps(cur, off0, n, chunk, y, yoff, ntap, wt):
                ts("dve", y, y[:, yoff:yoff + n], cur[:, off0:off0 + n], wt[:, chunk, 0:1], None, ALU.mult, None, [cur, wt])
                for i in range(1, ntap):
                    stt("dve", y, y[:, yoff:yoff + n], cur[:, off0 + i:off0 + i + n], wt[:, chunk, i:i + 1], y[:, yoff:yoff + n],
                        ALU.mult, ALU.add, [cur, wt, y])

            for b in range(3):
                wb = wq[b]
                next_w(b)
                items = [(gi, c) for gi in range(len(GROUPS)) for c in range(4)]
                stt_ = {}

                def stage0(k, wb=wb):
                    gi, c = items[k]
                    g0, gn, gtiles = GROUPS[gi]
                    hts = [HTt[i] for i in gtiles]
                    ps = psf.get()
                    for kc in range(8):
                        mm(ps, ps[:, 0:gn], wb[:, kc, c * 128:(c + 1) * 128], HT[:, kc, g0:g0 + gn], [wb] + hts,
                           start=(kc == 0), stop=(kc == 7), inc=(kc == 7))
                    stt_[k] = {"ps": ps}

                def stage1(k, b=b):
                    gi, c = items[k]
                    g0, gn, gtiles = GROUPS[gi]
                    chunk = b * 4 + c
                    ps = stt_[k]["ps"]
                    y = yp.get()
                    if gi < 8:
                        cur = rb[c][gi % 2]
                        prev = rb[c][(gi - 1) % 2]
                        cp("act", cur, cur[:, 3:3 + gn], ps[:, 0:gn], [ps])
                        if gi == 0:
                            P.dma("sp", cur[:, 0:3], conv0[l2, 0, :, chunk * 128:(chunk + 1) * 128].rearrange("t c -> c t"),
                                  writes=[cur], allow_slow_non_contiguous=True)
                        else:
                            cp("pool", cur, cur[:, 0:3], prev[:, 512:515], [prev])
                        conv_taps(cur, 0, gn, chunk, y, 0, 4, cw)
                        if gi == 7:
                            P.dma("sp", gdn_conv[l2, 0, :, chunk * 128:(chunk + 1) * 128].rearrange("t c -> c t"),
                                  cur[:, 512:515], reads=[cur], allow_slow_non_contiguous=True)
                    else:
                        for s_ in range(NSS):
                            cur = rbs.get()
                            cp("act", cur, cur[:, 3:3 + TS], ps[:, s_ * TS:(s_ + 1) * TS], [ps])
                            P.dma("sp", cur[:, 0:3], conv0[l2, 1 + s_, :, chunk * 128:(chunk + 1) * 128].rearrange("t c -> c t"),
                                  writes=[cur], allow_slow_non_contiguous=True)
                            conv_taps(cur, 0, TS, chunk, y, s_ * TS, 4, cw)
                            P.dma("sp", gdn_conv[l2, 1 + s_, :, chunk * 128:(chunk + 1) * 128].rearrange("t c -> c t"),
                                  cur[:, TS:TS + 3], reads=[cur], allow_slow_non_contiguous=True)
                    stt_[k]["y"] = y

                def stage2(k, b=b):
                    gi, c = items[k]
                    g0, gn, gtiles = GROUPS[gi]
                    chunk = b * 4 + c
                    y = stt_[k]["y"]
                    if b == 2:
                        ob = obp.get()
                        act(ob, ob[:, 0:gn], y[:, 0:gn], AF.Silu, [y])
                        P.dma("sp", QKVT[chunk * 128:(chunk + 1) * 128, g0:g0 + gn], ob[:, 0:gn], reads=[ob])
                    else:
                        so = sop.get()
                        act(so, so[:, 0:gn], y[:, 0:gn], AF.Silu, [y])
                        sq = sqp.get()
                        act(sq, sq[:, 0:gn], so[:, 0:gn], AF.Square, [so])
                        stt_[k]["so"] = so
                        stt_[k]["sq"] = sq

                def stage3(k, b=b):
                    gi, c = items[k]
                    g0, gn, gtiles = GROUPS[gi]
                    if b == 2:
                        return
                    sq = stt_[k]["sq"]
                    ps = psf.get()
                    mm(ps, ps[:, 0:gn], onesf[:, :], sq[:, 0:gn], [onesf, sq])
                    stt_[k]["ps1"] = ps

                def stage4(k, b=b):
                    gi, c = items[k]
                    g0, gn, gtiles = GROUPS[gi]
                    if b == 2:
                        return
                    ps = stt_[k]["ps1"]
                    rs = rsp.get()
                    act(rs, rs[:, 0:gn], ps[:, 0:gn], AF.Sqrt, [ps, epsT], bias=epsT[:, 0:1])
                    stt_[k]["rs"] = rs

                def stage5(k, b=b):
                    gi, c = items[k]
                    g0, gn, gtiles = GROUPS[gi]
                    chunk = b * 4 + c
                    st_ = stt_.pop(k)
                    if b == 2:
                        return
                    so, rs = st_["so"], st_["rs"]
                    P.op("dve", lambda e, rs=rs, gn=gn: e.reciprocal(rs[:, 0:gn], rs[:, 0:gn]), reads=[rs], writes=[rs])
                    sc = (128.0 ** -0.5) if b == 0 else 1.0
                    ob = obp.get()
                    stt("dve", ob, ob[:, 0:gn], so[:, 0:gn], sc, rs[:, 0:gn], ALU.mult, ALU.mult, [so, rs])
                    P.dma("sp", QKVT[chunk * 128:(chunk + 1) * 128, g0:g0 + gn], ob[:, 0:gn], reads=[ob])

                NI = len(items)
                stages = [stage0, stage1, stage2, stage3, stage4, stage5]
                for t in range(NI + 5):
                    for lag in (5, 4, 3, 2, 1, 0):
                        if 0 <= t - lag < NI:
                            stages[lag](t - lag)

            mb = [sb(st, f"mb{p}", [128, 516], F32) for p in range(2)]
            mbs = Pool_([sb(st, f"mbs{i}", [128, 68], F32) for i in range(3)])
            gcp = Pool_([sb(st, f"gc{i}", [128, 512], F32) for i in range(2)])
            for c in range(4):
                wb = wq[3 + c]
                next_w(3 + c)
                stt_ = {}

                def sstage0(gi, wb=wb):
                    g0, gn, gtiles = GROUPS[gi]
                    hts = [HTt[i] for i in gtiles]
                    pss = []
                    for j in range(3):
                        ps = psf.get()
                        for kc in range(8):
                            mm(ps, ps[:, 0:gn], wb[:, kc, j * 128:(j + 1) * 128], HT[:, kc, g0:g0 + gn], [wb] + hts,
                               start=(kc == 0), stop=(kc == 7), inc=(kc == 7))
                        pss.append(ps)
                    stt_[gi] = pss

                def sstage1(gi, c=c):
                    g0, gn, gtiles = GROUPS[gi]
                    pgb, pgc, phin = stt_.pop(gi)
                    gct = gcp.get()
                    cp("act", gct, gct[:, 0:gn], pgc[:, 0:gn], [pgc])
                    y = yp.get()
                    if gi < 8:
                        cur = mb[gi % 2]
                        prev = mb[(gi - 1) % 2]
                        tt("dve", cur, cur[:, 2:2 + gn], phin[:, 0:gn], gct[:, 0:gn], ALU.mult, [phin, gct])
                        if gi == 0:
                            P.dma("sp", cur[:, 0:2], sc0[l2, 0, :, c * 128:(c + 1) * 128].rearrange("t c -> c t"),
                                  writes=[cur], allow_slow_non_contiguous=True)
                        else:
                            cp("pool", cur, cur[:, 0:2], prev[:, 512:514], [prev])
                        conv_taps(cur, 0, gn, c, y, 0, 3, scw)
                        if gi == 7:
                            P.dma("sp", sconv_o[l2, 0, :, c * 128:(c + 1) * 128].rearrange("t c -> c t"),
                                  cur[:, 512:514], reads=[cur], allow_slow_non_contiguous=True)
                    else:
                        for s_ in range(NSS):
                            cur = mbs.get()
                            sl = slice(s_ * TS, (s_ + 1) * TS)
                            tt("dve", cur, cur[:, 2:2 + TS], phin[:, sl], gct[:, sl], ALU.mult, [phin, gct])
                            P.dma("sp", cur[:, 0:2], sc0[l2, 1 + s_, :, c * 128:(c + 1) * 128].rearrange("t c -> c t"),
                                  writes=[cur], allow_slow_non_contiguous=True)
                            conv_taps(cur, 0, TS, c, y, s_ * TS, 3, scw)
                            P.dma("sp", sconv_o[l2, 1 + s_, :, c * 128:(c + 1) * 128].rearrange("t c -> c t"),
                                  cur[:, TS:TS + 2], reads=[cur], allow_slow_non_contiguous=True)
                    ob = obp.get()
                    tt("dve", ob, ob[:, 0:gn], pgb[:, 0:gn], y[:, 0:gn], ALU.mult, [pgb, y])
                    P.dma("sp", OT[512 + c * 128:512 + (c + 1) * 128, g0:g0 + gn], ob[:, 0:gn], reads=[ob])

                NG_ = len(GROUPS)
                for t in range(NG_ + 1):
                    if t < NG_:
                        sstage0(t)
                    if 0 <= t - 1 < NG_:
                        sstage1(t - 1)

            wb = wq[7]
            AB = sb(st, "AB", [128, 36, 8], F32)
            P.op("pool", lambda e: e.memset(AB[:, :, :], 0.0), writes=[AB])
            zsp = Pool_([sb(st, f"zs{i}", [128, 512], BF16) for i in range(3)])
            for ti, (r0, n) in enumerate(TILES):
                ps = psf.get()
                for kc in range(8):
                    mm(ps, ps[:n, 0:512], HT[:, kc, r0:r0 + n], wb[:, kc, 0:512], [wb, HTt[ti]],
                       start=(kc == 0), stop=(kc == 7), inc=(kc == 7))
                ps2 = psf.get()
                for kc in range(8):
                    mm(ps2, ps2[:n, 0:8], HT[:, kc, r0:r0 + n], wb[:, kc, 512:520], [wb, HTt[ti]],
                       start=(kc == 0), stop=(kc == 7), inc=(kc == 7))
                zs = zsp.get()
                act(zs, zs[:n, :], ps[:n, 0:512], AF.Silu, [ps])
                P.dma("sp", Z[r0:r0 + n, :], zs[:n, :], reads=[zs])
                cp("dve", AB, AB[:n, ti, :], ps2[:n, 0:8], [ps2])
            dtb = sb(st, "dtb", [128, 4], F32)
            nex = sb(st, "nex", [128, 4], F32)
            P.dma("sp", dtb[:, :], bcast(gdn_dt_bias[l2, :], 128, 4), writes=[dtb])
            P.dma("sp", nex[:, :], bcast(gdn_a_log[l2, :], 128, 4), writes=[nex])
            act(nex, nex[:, :], nex[:, :], AF.Exp, [nex])
            ts("dve", nex, nex[:, :], nex[:, :], -1.0, None, ALU.mult, None, [nex])
            t1 = sb(st, "gt1", [128, 36], F32)
            t2 = sb(st, "gt2", [128, 36], F32)
            t3 = sb(st, "gt3", [128, 36], F32)
            for h in range(4):
                ts("dve", t1, t1[:, :], AB[:, :, h], dtb[:, h:h + 1], None, ALU.add, None, [AB, dtb])
                abs_(t2, t2[:, :], t1, t1[:, :])
                act(t2, t2[:, :], t2[:, :], AF.Exp, [t2], scale=-1.0)
                act(t2, t2[:, :], t2[:, :], AF.Ln, [t2], bias=1.0)
                ts("dve", t3, t3[:, :], t1[:, :], 0.0, None, ALU.max, None, [t1])
                tt("dve", t3, t3[:, :], t3[:, :], t2[:, :], ALU.add, [t3, t2])
                ts("dve", GB, GB[:, :, h], t3[:, :], nex[:, h:h + 1], None, ALU.mult, None, [t3, nex])
            act(GB, GB[:, :, 4:8], AB[:, :, 4:8], AF.Sigmoid, [AB])
            P.flush()

    def phase_gdn(l2):
        with ExitStack() as st:
            psf = Pool_([pst(st, f"pgf{i}", F32) for i in range(6)])
            psb = Pool_([pst(st, f"pgb{i}", BF16) for i in range(2)])
            NG = sb(st, "NG", [128, 128], F32)
            P.dma("sp", NG[:, :], bcast(gdn_norm_g[l2, :], 128, 128), writes=[NG])

            class HeadCtx:
                pass

            def mk_ctx(tag, Tmax, share):
                c = HeadCtx()
                if share is None:
                    c.qT = sb(st, f"qT{tag}", [128, Tmax], BF16)
                    c.kT = sb(st, f"kT{tag}", [128, Tmax], BF16)
                    c.vT = sb(st, f"vT{tag}", [128, Tmax], BF16)
                    c.zs = sb(st, f"zz{tag}", [128, Tmax // 128 if Tmax >= 128 else 1, 128], BF16)
                    c.OTb = sb(st, f"otb{tag}", [128, Tmax], BF16)
                    c.S = sb(st, f"S{tag}", [128, 128], F32)
                    c.Sb = sb(st, f"Sb{tag}", [128, 128], BF16)
                else:
                    for a_ in ("qT", "kT", "vT", "zs", "OTb", "S", "Sb"):
                        setattr(c, a_, getattr(share, a_))

                def f(name, dt=F32, shape=(128, 128)):
                    return sb(st, f"{name}{tag}", list(shape), dt)

                c.gBt = f("gBt")
                c.gcol = f("gcol", F32, (128, 1))
                c.glast = f("glast", F32, (128, 1))
                c.EG = f("EG")
                c.egcol = f("egcol", F32, (128, 1))
                c.ekd = f("ekd", F32, (128, 1))
                c.egC = f("egC", F32, (128, 1))
                c.bg = f("bg", F32, (128, 1))
                c.d1 = f("d1")
                c.d2 = f("d2")
                c.vb = f("vb", F32)
                c.kbg = f("kbg", F32)
                c.kd = f("kd", BF16)
                c.A = f("A", F32)
                c.PT = f("PT", BF16)
                c.M = [f("M0", F32), f("M1", F32)]
                c.MT = [f("MT0", F32), f("MT1", F32)]
                c.Rm = [f("R0", F32), f("R1", F32)]
                c.RT = [f("RT0", F32), f("RT1", F32)]
                c.nwT = f("nwT", BF16)
                c.ub = f("ub", BF16)
                c.qgT = f("qgT", BF16)
                c.jk = f("jk", BF16)
                c.ss = f("ss", F32, (128, 1))
                c.t1 = f("t1")
                c.t2 = f("t2", BF16)
                return c

            SEQ_ATTRS = ("qT", "kT", "vT", "zs", "OTb", "S", "Sb")

            def mk_pair(tag, Tmax):
                c0 = mk_ctx(tag + "0", Tmax, None)
                c1 = mk_ctx(tag + "1", Tmax, c0)
                return [c0, c1]

            def chunk_head(c, C, n, ti, h):
                t0 = n * C
                cs = slice(t0, t0 + C)
                gsrc = GB[:C, ti, h:h + 1]
                beta = GB[:C, ti, 4 + h:5 + h]
                ts("dve", c.gBt, c.gBt[:C, :], onesf[:C, :], gsrc, None, ALU.mult, None, [onesf, GB])
                psG = psf.get()
                mm(psG, psG[:, 0:C], c.gBt[:C, :], Uf[:C, :C], [c.gBt, Uf])
                mm(psG, psG[:C, 256:257], Uf[:C, :C], gsrc, [Uf, GB])
                cp("act", c.gcol, c.gcol[:C, :], psG[:C, 256:257], [psG])
                cp("act", c.glast, c.glast[:, :], psG[:, C - 1:C], [psG])
                act(c.EG, c.EG[:, :C], psG[:, 0:C], AF.Exp, [psG])
                act(c.egcol, c.egcol[:C, :], c.gcol[:C, :], AF.Exp, [c.gcol])
                act(c.ekd, c.ekd[:C, :], c.gcol[:C, :], AF.Exp, [c.gcol, c.glast], bias=c.glast[:C, 0:1], scale=-1.0)
                act(c.egC, c.egC[:, :], c.glast[:, :], AF.Exp, [c.glast])
                tt("dve", c.bg, c.bg[:C, :], beta, c.egcol[:C, :], ALU.mult, [GB, c.egcol])
                stt("dve", c.d1, c.d1[:C, :C], psG[:C, 0:C], c.gcol[:C, 0:1], negls[:C, :C], ALU.subtract, ALU.subtract, [psG, c.gcol, negls])
                act(c.d1, c.d1[:C, :C], c.d1[:C, :C], AF.Exp, [c.d1], scale=-1.0)
                stt("dve", c.d2, c.d2[:C, :C], psG[:C, 0:C], c.gcol[:C, 0:1], negu[:C, :C], ALU.subtract, ALU.add, [psG, c.gcol, negu])
                act(c.d2, c.d2[:C, :C], c.d2[:C, :C], AF.Exp, [c.d2])
                yield
                pb = psb.get()
                tr(pb, pb[:C, 0:128], c.kT[:, cs], identb[:, :], [c.kT, identb], inc=False)
                tr(pb, pb[:C, 128:256], c.vT[:, cs], identb[:, :], [c.vT, identb])
                ts("dve", c.vb, c.vb[:C, :], pb[:C, 128:256], beta, None, ALU.mult, None, [pb, GB])
                ts("dve", c.kbg, c.kbg[:C, :], pb[:C, 0:128], c.bg[:C, 0:1], None, ALU.mult, None, [pb, c.bg])
                ts("dve", c.kd, c.kd[:C, :], pb[:C, 0:128], c.ekd[:C, 0:1], None, ALU.mult, None, [pb, c.ekd])
                yield
                psK = psf.get()
                mm(psK, psK[:C, 0:C], c.kT[:, cs], c.kT[:, cs], [c.kT], inc=False)
                mm(psK, psK[:C, 128:128 + C], c.kT[:, cs], c.qT[:, cs], [c.kT, c.qT])
                stt("dve", c.A, c.A[:C, :C], psK[:C, 0:C], beta, c.d1[:C, :C], ALU.mult, ALU.mult, [psK, GB, c.d1])
                tt("dve", c.PT, c.PT[:C, :C], psK[:C, 128:128 + C], c.d2[:C, :C], ALU.mult, [psK, c.d2])
                yield
                pb2 = psf.get()
                tr(pb2, pb2[:C, 0:C], c.A[:C, :C], identf[:C, :C], [c.A, identf])
                tt("dve", c.RT[0], c.RT[0][:C, :C], identf[:C, :C], pb2[:C, 0:C], ALU.subtract, [identf, pb2])
                cp("act", c.MT[0], c.MT[0][:C, :C], pb2[:C, 0:C], [pb2])
                yield
                M, MT, Rm, RT = c.A, c.MT[0], c.Rm[0], c.RT[0]
                nl = {128: 6, 64: 5}[C]
                for l in range(nl):
                    last = (l == nl - 1)
                    Mn, MTn, Rn, RTn = c.M[l % 2], c.MT[(l + 1) % 2], c.Rm[(l + 1) % 2], c.RT[(l + 1) % 2]
                    psM = psf.get()
                    mm(psM, psM[:C, 0:C], MT[:C, :C], M[:C, :C], [MT, M], inc=last)
                    if not last:
                        mm(psM, psM[:C, 128:128 + C], M[:C, :C], MT[:C, :C], [M, MT])
                    cp("act", Mn, Mn[:C, :C], psM[:C, 0:C], [psM])
                    if not last:
                        cp("dve", MTn, MTn[:C, :C], psM[:C, 128:128 + C], [psM])
                    yield
                    psR = psf.get()
                    mm(psR, psR[:C, 0:C], Mn[:C, :C], RT[:C, :C], [Mn, RT])
                    tt("dve", RTn, RTn[:C, :C], psR[:C, 0:C], RT[:C, :C], ALU.add, [psR, RT])
                    M, MT, Rm, RT = Mn, MTn, Rn, RTn
                    yield
                TT = RT
                c.TT = TT

            def chunk_tail(c, C, n, ti, h):
                t0 = n * C
                cs = slice(t0, t0 + C)
                TT = c.TT
                psW = psf.get()
                mm(psW, psW[:, 0:C], c.kbg[:C, :], TT[:C, :C], [c.kbg, TT])
                P.op("act", lambda e: e.mul(c.nwT[:, :C], psW[:, 0:C], -1.0), reads=[psW], writes=[c.nwT])
                psU = psf.get()
                mm(psU, psU[:C, 0:128], TT[:C, :C], c.vb[:C, :], [TT, c.vb], start=True, stop=False, inc=False)
                mm(psU, psU[:C, 0:128], c.nwT[:, :C], c.Sb[:, :], [c.nwT, c.Sb], start=False, stop=True)
                cp("dve", c.ub, c.ub[:C, :], psU[:C, 0:128], [psU])
                yield
                tt("pool", c.qgT, c.qgT[:, :C], c.qT[:, cs], c.EG[:, :C], ALU.mult, [c.qT, c.EG])
                psO = psf.get()
                mm(psO, psO[:C, 0:128], c.qgT[:, :C], c.Sb[:, :], [c.qgT, c.Sb], start=True, stop=False, inc=False)
                mm(psO, psO[:C, 0:128], c.PT[:C, :C], c.ub[:C, :], [c.PT, c.ub], start=False, stop=True)
                psS = psf.get()
                mm(psS, psS[:, 0:128], c.kd[:C, :], c.ub[:C, :], [c.kd, c.ub])
                stt("dve", c.S, c.S[:, :], c.S[:, :], c.egC[:, 0:1], psS[:, 0:128], ALU.mult, ALU.add, [c.S, c.egC, psS])
                cp("act", c.Sb, c.Sb[:, :], c.S[:, :], [c.S])
                act(c.jk, c.jk[:C, :], psO[:C, 0:128], AF.Square, [psO], accum=(c.ss, c.ss[:C, 0:1]))
                rsqrt_("dve", c.ss, c.ss[:C, 0:1], 1.0 / 128, EPS)
                stt("dve", c.t1, c.t1[:C, :], psO[:C, 0:128], c.ss[:C, 0:1], NG[:C, :], ALU.mult, ALU.mult, [psO, c.ss, NG])
                tt("pool", c.t2, c.t2[:C, :], c.t1[:C, :], c.zs[:C, n, :], ALU.mult, [c.t1, c.zs])
                pb3 = psb.get()
                tr(pb3, pb3[:, 0:C], c.t2[:C, :], identb[:C, :C], [c.t2, identb])
                cp("act", c.OTb, c.OTb[:, cs], pb3[:, 0:C], [pb3])

            def run_part(pairs, si, heads, C, tp0, Tpart, first, last_part):
                row0, T = SEQS[si]
                r0 = row0 + tp0
                for pr, h in zip(pairs, heads):
                    c = pr[0]
                    P.dma("sp", c.qT[:, 0:Tpart], QKVT[h * 128:(h + 1) * 128, r0:r0 + Tpart], writes=[c.qT])
                    P.dma("sp", c.kT[:, 0:Tpart], QKVT[(4 + h) * 128:(5 + h) * 128, r0:r0 + Tpart], writes=[c.kT])
                    P.dma("sp", c.vT[:, 0:Tpart], QKVT[(8 + h) * 128:(9 + h) * 128, r0:r0 + Tpart], writes=[c.vT])
                    P.dma("sp", c.zs[:C, 0:Tpart // C, :], Z[r0:r0 + Tpart, h * 128:(h + 1) * 128].rearrange("(n p) d -> p n d", p=C),
                          writes=[c.zs])
                    if first:
                        P.dma("sp", c.S[:, :], S0[l2, si, h], writes=[c.S])
                        cp("act", c.Sb, c.Sb[:, :], c.S[:, :], [c.S])
                nch = Tpart // C

                def tix(n):
                    return (tp0 // 128 + n) if si == 0 else 32 + (si - 1)

                def rr(gens):
                    live = list(gens)
                    while live:
                        nxt = []
                        for g in live:
                            try:
                                next(g)
                                nxt.append(g)
                            except StopIteration:
                                pass
                        live = nxt

                rr([chunk_head(pr[0], C, 0, tix(0), h) for pr, h in zip(pairs, heads)])
                for n in range(nch):
                    gens = [chunk_tail(pr[n % 2], C, n, tix(n), h) for pr, h in zip(pairs, heads)]
                    if n + 1 < nch:
                        gens += [chunk_head(pr[(n + 1) % 2], C, n + 1, tix(n + 1), h) for pr, h in zip(pairs, heads)]
                    rr(gens)
                for pr, h in zip(pairs, heads):
                    c = pr[0]
                    P.dma("sp", OT[h * 128:(h + 1) * 128, r0:r0 + Tpart], c.OTb[:, 0:Tpart], reads=[c.OTb])
                    if last_part:
                        P.dma("sp", gdn_S[l2, si, h], c.S[:, :], reads=[c.S])

            QTR = TP // 4
            pairs = [mk_pair(t, QTR) for t in "abcd"]
            if not OPTS.get('gdn_sample_only'):
                for qtr in range(4):
                    run_part(pairs, 0, [0, 1, 2, 3], 128, qtr * QTR, QTR, qtr == 0, qtr == 3)
            for s in range(OPTS.get('gdn_nss', NSS)):
                run_part(pairs, 1 + s, [0, 1, 2, 3], 64, 0, TS, True, True)
            P.flush()

    def phase_out(L, w_out2d):
        with ExitStack() as st:
            psf = Pool_([pst(st, f"pof{i}", F32) for i in range(6)])
            psb = Pool_([pst(st, f"pob{i}", BF16) for i in range(2)])
            wo = sb(st, "wo", [128, 8, D], BF16)
            load_w(wo, w_out2d, 0, 512, 0)
            load_w(wo, w_out2d, 512, 512, 512)
            gB = load_gain(st, "g2", norm_ffn_g[L, :])
            OTr = OT.rearrange("(k p) r -> p k r", p=128)
            for (g0, gn, gtiles) in GROUPS:
                P.dma("sp", HT[:, :, g0:g0 + gn], OTr[:, :, g0:g0 + gn], writes=[HTt[i] for i in gtiles])
            xp = Pool_([sb(st, f"xo{i}", [128, D], F32) for i in range(2)])
            xnp = Pool_([sb(st, f"xn{i}", [128, D], F32) for i in range(3)])
            pools = norm_pools(st, psb)
            xsrc = x_all if L == 0 else X
            pend = None
            for ti, (r0, n) in enumerate(TILES):
                xt = xp.get()
                P.dma("sp", xt[:n, :], xsrc[r0:r0 + n, :], writes=[xt])
                xn = xnp.get()
                for half in range(2):
                    ps = psf.get()
                    for kc in range(8):
                        mm(ps, ps[:n, :], HT[:, kc, r0:r0 + n], wo[:, kc, half * 512:(half + 1) * 512], [HTt[ti], wo],
                           start=(kc == 0), stop=(kc == 7), inc=(kc == 7))
                    tt("dve", xn, xn[:n, half * 512:(half + 1) * 512], ps[:n, :], xt[:n, half * 512:(half + 1) * 512], ALU.add, [ps, xt])
                P.dma("sp", X[r0:r0 + n, :], xn[:n, :], reads=[xn])
                if pend is not None:
                    norm_b(pend[0], pend[1], pools)
                pend = (ti, norm_a(ti, xn, gB, pools))
            norm_b(pend[0], pend[1], pools)
            P.flush()

    def phase_ffn1(L):
        with ExitStack() as st:
            psf = Pool_([pst(st, f"pff{i}", F32) for i in range(8)])
            wgs = Pool_([sb(st, f"wg{i}", [128, 8, 512], BF16) for i in range(2)])
            wus = Pool_([sb(st, f"wu{i}", [128, 8, 512], BF16) for i in range(2)])
            sgp = Pool_([sb(st, f"sg{i}", [128, 512], F32) for i in range(3)])
            abp = Pool_([sb(st, f"ab{i}", [128, 512], BF16) for i in range(4)])
            for fb in range(6):
                f0 = fb * 512
                nf = min(512, DFF - f0)
                wg = wgs.get()
                wu = wus.get()
                load_w(wg, ffn_w_gate[L], f0, nf)
                load_w(wu, ffn_w_up[L], f0, nf)
                for (g0, gn, gtiles) in GROUPS:
                    hts = [HTt[i] for i in gtiles]
                    for c in range(nf // 128):
                        psg = psf.get()
                        for kc in range(8):
                            mm(psg, psg[:, 0:gn], wg[:, kc, c * 128:(c + 1) * 128], HT[:, kc, g0:g0 + gn], [wg] + hts,
                               start=(kc == 0), stop=(kc == 7), inc=(kc == 7))
                        psu = psf.get()
                        for kc in range(8):
                            mm(psu, psu[:, 0:gn], wu[:, kc, c * 128:(c + 1) * 128], HT[:, kc, g0:g0 + gn], [wu] + hts,
                               start=(kc == 0), stop=(kc == 7), inc=(kc == 7))
                        sg = sgp.get()
                        act(sg, sg[:, 0:gn], psg[:, 0:gn], AF.Silu, [psg])
                        ab = abp.get()
                        tt("dve", ab, ab[:, 0:gn], psu[:, 0:gn], sg[:, 0:gn], ALU.mult, [psu, sg])
                        P.dma("sp", ACTS[f0 + c * 128:f0 + (c + 1) * 128, g0:g0 + gn], ab[:, 0:gn], reads=[ab])
            P.flush()

    def phase_ffn2(L):
        with ExitStack() as st:
            psf = Pool_([pst(st, f"pdf{i}", F32) for i in range(6)])
            psb = Pool_([pst(st, f"pdb{i}", BF16) for i in range(2)])
            wd = sb(st, "wd", [128, 22, D], BF16)
            wdr = ffn_w_down[L].rearrange("(c p) d -> p c d", p=128)
            for i in range(0, 22, 4):
                j = min(22, i + 4)
                P.dma("pool", wd[:, i:j, :], wdr[:, i:j, :], writes=[wd])
            last = (L == 3)
            gB = None if last else load_gain(st, "g3", norm_mix_g[L + 1, :])
            abp = Pool_([sb(st, f"a2{i}", [128, 22, 512], BF16) for i in range(2)])
            xp = Pool_([sb(st, f"xd{i}", [128, D], F32) for i in range(2)])
            xnp = Pool_([sb(st, f"xm{i}", [128, D], F32) for i in range(3)])
            pools = norm_pools(st, psb)
            ACr = ACTS.rearrange("(c p) r -> p c r", p=128)
            pend = None
            for (g0, gn, gtiles) in GROUPS:
                ab = abp.get()
                P.dma("sp", ab[:, 0:11, 0:gn], ACr[:, 0:11, g0:g0 + gn], writes=[ab])
                P.dma("sp", ab[:, 11:22, 0:gn], ACr[:, 11:22, g0:g0 + gn], writes=[ab])
                for ti in gtiles:
                    r0, n = TILES[ti]
                    off = r0 - g0
                    xt = xp.get()
                    P.dma("sp", xt[:n, :], X[r0:r0 + n, :], writes=[xt])
                    xn = xnp.get()
                    for half in range(2):
                        ps = psf.get()
                        for c in range(22):
                            mm(ps, ps[:n, :], ab[:, c, off:off + n], wd[:, c, half * 512:(half + 1) * 512], [ab, wd],
                               start=(c == 0), stop=(c == 21), inc=(c == 21))
                        tt("dve", xn, xn[:n, half * 512:(half + 1) * 512], ps[:n, :], xt[:n, half * 512:(half + 1) * 512], ALU.add, [ps, xt])
                    if last:
                        P.dma("sp", y_all[r0:r0 + n, :], xn[:n, :], reads=[xn])
                    else:
                        P.dma("sp", X[r0:r0 + n, :], xn[:n, :], reads=[xn])
                        if pend is not None:
                            norm_b(pend[0], pend[1], pools)
                        pend = (ti, norm_a(ti, xn, gB, pools))
            if pend is not None:
                norm_b(pend[0], pend[1], pools)
            P.flush()

    def phase_proj_odd(l2):
        W = w_in_odd[l2]
        with ExitStack() as st:
            psf = Pool_([pst(st, f"ppf{i}", F32) for i in range(6)])
            psb = Pool_([pst(st, f"ppb{i}", BF16) for i in range(2)])
            wbs = Pool_([sb(st, f"wq{i}", [128, 8, 520], BF16) for i in range(3)])
            QG = sb(st, "QG", [128, 128], F32)
            KG = sb(st, "KG", [128, 128], F32)
            bfB = sb(st, "bfB", [128, 8], F32)
            P.dma("sp", QG[:, :], bcast(fox_q_norm_g[l2, :], 128, 128), writes=[QG])
            P.dma("sp", KG[:, :], bcast(fox_k_norm_g[l2, :], 128, 128), writes=[KG])
            P.dma("sp", bfB[:, :], bcast(fox_b_f[l2, :], 128, 8), writes=[bfB])
            ts("dve", QG, QG[:, :], QG[:, :], 128.0 ** -0.5, None, ALU.mult, None, [QG])
            KTb = sb(st, "KTb", [128, 4, R], BF16)
            KTbt = [Tl(KTb.t, f"KTb{i}") for i in range(len(TILES))]
            LFR = sb(st, "LFR", [128, 36, 8], F32)
            P.op("pool", lambda e: e.memset(LFR[:, :, :], 0.0), writes=[LFR])
            LF = sb(st, "LF", [128, 36, 8], F32)
            jkp = Pool_([sb(st, f"pj{i}", [128, 128], BF16) for i in range(2)])
            ssp = Pool_([sb(st, f"pss{i}", [128, 4], F32) for i in range(3)])
            qnp = Pool_([sb(st, f"qn{i}", [128, 512], BF16) for i in range(4)])
            knp = Pool_([sb(st, f"kn{i}", [128, 512], F32) for i in range(3)])
            wq = [wbs.get()]
            load_w(wq[0], W, 0, 512)
            for blk in range(6):
                kind = blk // 2
                wb = wq[blk]
                if blk + 1 < 6:
                    wq.append(wbs.get())
                    load_w(wq[blk + 1], W, (blk + 1) * 512, 520 if blk + 1 == 5 else 512)
                stt_ = {}

                def stage0(ti, blk=blk, wb=wb):
                    r0, n = TILES[ti]
                    ps = psf.get()
                    for kc in range(8):
                        mm(ps, ps[:n, :], HT[:, kc, r0:r0 + n], wb[:, kc, 0:512], [HTt[ti], wb],
                           start=(kc == 0), stop=(kc == 7), inc=(kc == 7))
                    stt_[ti] = {"ps": ps}
                    if blk == 5:
                        ps2 = psf.get()
                        for kc in range(8):
                            mm(ps2, ps2[:n, 0:8], HT[:, kc, r0:r0 + n], wb[:, kc, 512:520], [HTt[ti], wb],
                               start=(kc == 0), stop=(kc == 7), inc=(kc == 7))
                        stt_[ti]["ps2"] = ps2

                def stage1(ti, blk=blk, kind=kind):
                    r0, n = TILES[ti]
                    ps = stt_[ti]["ps"]
                    if blk == 5:
                        ps2 = stt_[ti]["ps2"]
                        cp("act", LFR, LFR[:n, ti, :], ps2[:n, 0:8], [ps2])
                    if kind < 2:
                        ss = ssp.get()
                        for h in range(4):
                            jk = jkp.get()
                            act(jk, jk[:n, :], ps[:n, h * 128:(h + 1) * 128], AF.Square, [ps], accum=(ss, ss[:n, h:h + 1]))
                        rsqrt_("dve", ss, ss[:n, :], 1.0 / 128, EPS)
                        G = QG if kind == 0 else KG
                        if kind == 0:
                            qn = qnp.get()
                            for h in range(4):
                                stt("dve", qn, qn[:n, h * 128:(h + 1) * 128], ps[:n, h * 128:(h + 1) * 128], ss[:n, h:h + 1], G[:n, :],
                                    ALU.mult, ALU.mult, [ps, ss, G])
                        else:
                            kn = knp.get()
                            for h in range(4):
                                stt("dve", kn, kn[:n, h * 128:(h + 1) * 128], ps[:n, h * 128:(h + 1) * 128], ss[:n, h:h + 1], G[:n, :],
                                    ALU.mult, ALU.mult, [ps, ss, G])
                            P.dma("sp", fox_k[l2, r0:r0 + n, (blk - 2) * 512:(blk - 1) * 512], kn[:n, :], reads=[kn])
                            qn = qnp.get()
                            cp("pool", qn, qn[:n, :], kn[:n, :], [kn])
                        stt_[ti]["qn"] = qn
                    else:
                        vf = knp.get()
                        cp("act", vf, vf[:n, :], ps[:n, :], [ps])
                        P.dma("sp", fox_v[l2, r0:r0 + n, (blk - 4) * 512:(blk - 3) * 512], vf[:n, :], reads=[vf])
                        vb = qnp.get()
                        cp("dve", vb, vb[:n, :], ps[:n, :], [ps])
                        P.dma("sp", V[r0:r0 + n, (blk - 4) * 512:(blk - 3) * 512], vb[:n, :], reads=[vb])

                def stage2(ti, kind=kind):
                    r0, n = TILES[ti]
                    st_ = stt_.pop(ti)
                    if kind < 2:
                        qn = st_["qn"]
                        pb = psb.get()
                        for h in range(4):
                            tr(pb, pb[:, h * 128:h * 128 + n], qn[:n, h * 128:(h + 1) * 128], identb[:n, :n], [qn, identb], inc=(h == 3))
                        src = pb.t[:, 0:512].rearrange("p (k m) -> p k m", k=4)[:, :, 0:n]
                        cp("act", KTbt[ti], KTb[:, :, r0:r0 + n], src, [pb])

                NT = len(TILES)
                for t in range(NT + 2):
                    if t < NT:
                        stage0(t)
                    if 0 <= t - 2 < NT:
                        stage2(t - 2)
                    if 0 <= t - 1 < NT:
                        stage1(t - 1)
                if kind < 2:
                    dst = (QT if kind == 0 else KT)[(blk % 2) * 512:(blk % 2 + 1) * 512, :].rearrange("(h p) r -> p h r", p=128)
                    for hh in range(4):
                        P.dma("sp", dst[:, hh, :], KTb[:, hh, :], reads=KTbt)
            for h in range(8):
                ts("dve", LFR, LFR[:, :, h], LFR[:, :, h], bfB[:, h:h + 1], None, ALU.add, None, [LFR, bfB])
            t2 = sb(st, "lt2", [128, 36, 8], F32)
            abs_(t2, t2[:, :, :], LFR, LFR[:, :, :])
            act(t2, t2[:, :, :], t2[:, :, :], AF.Exp, [t2], scale=-1.0)
            act(t2, t2[:, :, :], t2[:, :, :], AF.Ln, [t2], bias=1.0)
            ts("dve", LF, LF[:, :, :], LFR[:, :, :], 0.0, None, ALU.min, None, [LFR])
            tt("dve", LF, LF[:, :, :], LF[:, :, :], t2[:, :, :], ALU.subtract, [LF, t2])
            for ti, (r0, n) in enumerate(TILES):
                P.dma("sp", fox_logf[l2, r0:r0 + n, :], LF[:n, ti, :], reads=[LF])
            CUM = sb(st, "CUM", [128, 32, 8], F32)
            TOT = sb(st, "TOT", [128, 32, 8], F32)
            carry = sb(st, "carry", [128, 8], F32)
            psc = psf.get()
            for r in range(32):
                mm(psc, psc[:, r * 16:r * 16 + 8], Uf[:, :], LF[:, r, :], [Uf, LF], inc=False)
                mm(psc, psc[:, r * 16 + 8:r * 16 + 16], onesf[:, :], LF[:, r, :], [onesf, LF], inc=(r == 31))
            pv = psc.t[:, :].rearrange("p (r c) -> p r c", c=16)
            cp("act", CUM, CUM[:, :, :], pv[:, :, 0:8], [psc])
            cp("dve", TOT, TOT[:, :, :], pv[:, :, 8:16], [psc])
            P.op("dve", lambda e: e.memset(carry[:, :], 0.0), writes=[carry])
            for r in range(32):
                tt("dve", CCp, CCp[:, r, :], CUM[:, r, :], carry[:, :], ALU.add, [CUM, carry])
                tt("dve", carry, carry[:, :], carry[:, :], TOT[:, r, :], ALU.add, [carry, TOT])
                if r % 4 == 3:
                    cp("dve", CEp, CEp[:, r // 4, :], carry[:, :], [carry])
            CL = sb(st, "CL", [128, 8, 8], F32)
            CUMs = sb(st, "CUMs", [128, 9, 8], F32)
            TOTs = sb(st, "TOTs", [128, 9, 8], F32)
            for s in range(NSS):
                P.dma("sp", CL[:, :, :], clf[l2, s].rearrange("(b p) h -> p b h", p=128), writes=[CL])
                psc = psf.get()
                for b in range(8):
                    mm(psc, psc[:, b * 16:b * 16 + 8], Uf[:, :], CL[:, b, :], [Uf, CL], inc=False)
                    mm(psc, psc[:, b * 16 + 8:b * 16 + 16], onesf[:, :], CL[:, b, :], [onesf, CL], inc=False)
                mm(psc, psc[:TS, 128:136], Uf[:TS, :TS], LF[:TS, 32 + s, :], [Uf, LF], inc=False)
                mm(psc, psc[:, 136:144], onesf[:TS, :], LF[:TS, 32 + s, :], [onesf, LF])
                pv = psc.t[:, 0:144].rearrange("p (r c) -> p r c", c=16)
                cp("act", CUMs, CUMs[:, :, :], pv[:, :, 0:8], [psc])
                cp("dve", TOTs, TOTs[:, :, :], pv[:, :, 8:16], [psc])
                P.op("dve", lambda e: e.memset(carry[:, :], 0.0), writes=[carry])
                for b in range(9):
                    tt("dve", CCs[s], CCs[s][:, b, :], CUMs[:, b, :], carry[:, :], ALU.add, [CUMs, carry])
                    tt("dve", carry, carry[:, :], carry[:, :], TOTs[:, b, :], ALU.add, [carry, TOTs])
                cp("dve", CEs[s], CEs[s][:, :], carry[:, :], [carry])
            P.flush()

    def phase_attn_prompt(l2):
        with ExitStack() as st:
            psS_ = Pool_([pst(st, f"paS{i}", F32) for i in range(4)])
            psA = [[pst(st, f"paO{i}", F32), pst(st, f"paL{i}", F32)] for i in range(2)]
            KTh = Pool_([sb(st, f"KTh{i}", [128, TP], BF16) for i in range(2)])
            QTh = Pool_([sb(st, f"QTh{i}", [128, TP], BF16) for i in range(2)])
            Vh = Pool_([sb(st, f"Vh{i}", [128, 32, 128], BF16) for i in range(2)])
            OTb = Pool_([sb(st, f"aot{i}", [128, TP], BF16) for i in range(2)])
            BIp = Pool_([sb(st, f"BI{i}", [128, 32], F32) for i in range(3)])
            ptp = Pool_([sb(st, f"pt{i}", [128, 512], BF16) for i in range(4)])
            rlp = Pool_([sb(st, f"rl{i}", [128, 512], F32) for i in range(2)])
            qi = 0
            for h in range(8):
                kt, qt, vh, ot = KTh.get(), QTh.get(), Vh.get(), OTb.get()
                P.dma("sp", kt[:, :], KT[h * 128:(h + 1) * 128, 0:TP], writes=[kt])
                P.dma("sp", qt[:, :], QT[h * 128:(h + 1) * 128, 0:TP], writes=[qt])
                P.dma("sp", vh[:, :, :], V[0:TP, h * 128:(h + 1) * 128].rearrange("(b p) d -> p b d", p=128), writes=[vh])
                items = [(Q, j) for Q in range(8) for j in range(4 * (Q + 1))]
                stA = {}
                acc = {}

                def stageA(k, kt=kt, qt=qt, h=h):
                    Q, j = items[k]
                    J = 4 * (Q + 1)
                    if j == 0:
                        BI = BIp.get()
                        ts("dve", BI, BI[:, 0:J], CCp[:, 0:J, h], CEp[:, Q, h:h + 1], -1.0, ALU.subtract, ALU.mult, [CCp, CEp])
                        stA["BI"] = BI
                    BI = stA["BI"]
                    diag = j >= 4 * Q
                    qlo = (j - 4 * Q) * 128 if diag else 0
                    ps = psS_.get()
                    mm(ps, ps[:, qlo:512], kt[:, j * 128:(j + 1) * 128], qt[:, Q * 512 + qlo:(Q + 1) * 512], [kt, qt])
                    pt = ptp.get()
                    act(pt, pt[:, qlo:512], ps[:, qlo:512], AF.Exp, [ps, BI], bias=BI[:, j:j + 1])
                    if diag:
                        tt("pool", pt, pt[:, qlo:qlo + 128], pt[:, qlo:qlo + 128], maskb[:, :], ALU.mult, [pt, maskb])
                    stA[k] = (pt, qlo)

                def stageB(k, vh=vh, ot=ot):
                    nonlocal qi
                    Q, j = items[k]
                    J = 4 * (Q + 1)
                    pt, qlo = stA.pop(k)
                    if j == 0:
                        acc["p"] = psA[qi % 2]
                        qi += 1
                    psO, psL = acc["p"]
                    mm(psO, psO[:, qlo:512], vh[:, j, :], pt[:, qlo:512], [vh, pt], start=(j == 0), stop=(j == J - 1), inc=(j == J - 1))
                    mm(psL, psL[:, qlo:512], onesb[:, :], pt[:, qlo:512], [onesb, pt], start=(j == 0), stop=(j == J - 1), inc=(j == J - 1))
                    if j == J - 1:
                        rl = rlp.get()
                        P.op("dve", lambda e, rl=rl, psL=psL: e.reciprocal(rl[:, :], psL[:, :]), reads=[psL], writes=[rl])
                        tt("dve", ot, ot[:, Q * 512:(Q + 1) * 512], psO[:, :], rl[:, :], ALU.mult, [psO, rl])

                LA = 2
                for k in range(len(items) + LA):
                    if k < len(items):
                        stageA(k)
                    if k >= LA:
                        stageB(k - LA)
                P.dma("sp", OT[h * 128:(h + 1) * 128, 0:TP], ot[:, :], reads=[ot])
            P.flush()

    def phase_attn_sample(l2):
        with ExitStack() as st:
            psf = Pool_([pst(st, f"psf{i}", F32) for i in range(6)])
            psb = Pool_([pst(st, f"psb{i}", BF16) for i in range(2)])
            KCp = Pool_([sb(st, f"KC{i}", [128, 8, D], BF16) for i in range(2)])
            VCp = Pool_([sb(st, f"VC{i}", [128, 8, D], BF16) for i in range(2)])
            KTc = Pool_([sb(st, f"KTc{i}", [128, 1024], BF16) for i in range(3)])
            KNp = Pool_([sb(st, f"KN{i}", [128, 8, TS], BF16) for i in range(2)])
            QNp = Pool_([sb(st, f"QN{i}", [128, 8, TS], BF16) for i in range(2)])
            VNp = Pool_([sb(st, f"VN{i}", [128, D], BF16) for i in range(2)])
            OTs = Pool_([sb(st, f"OTs{i}", [128, 8, TS], BF16) for i in range(2)])
            BIp = Pool_([sb(st, f"BIs{i}", [128, 9], F32) for i in range(3)])
            ptp = Pool_([sb(st, f"pts{i}", [128, 9 * TS], BF16) for i in range(3)])
            rlp = Pool_([sb(st, f"rls{i}", [128, TS], F32) for i in range(2)])
            KTr = KT.rearrange("(h p) r -> p h r", p=128)
            QTr = QT.rearrange("(h p) r -> p h r", p=128)
            for s in range(NSS):
                row0 = TP + TS * s
                kc_, vc_ = KCp.get(), VCp.get()
                for b0 in range(0, 8, 2):
                    P.dma("pool", kc_[:, b0:b0 + 2, :], ck[l2, s].rearrange("(b p) f -> p b f", p=128)[:, b0:b0 + 2, :], writes=[kc_])
                    P.dma("pool", vc_[:, b0:b0 + 2, :], cv[l2, s].rearrange("(b p) f -> p b f", p=128)[:, b0:b0 + 2, :], writes=[vc_])
                kn, qn, vn, ots = KNp.get(), QNp.get(), VNp.get(), OTs.get()
                P.dma("sp", kn[:, :, :], KTr[:, :, row0:row0 + TS], writes=[kn])
                P.dma("sp", qn[:, :, :], QTr[:, :, row0:row0 + TS], writes=[qn])
                P.dma("sp", vn[:TS, :], V[row0:row0 + TS, :], writes=[vn])
                for h in range(8):
                    pb = psb.get()
                    for b in range(8):
                        tr(pb, pb[:, b * 128:(b + 1) * 128], kc_[:, b, h * 128:(h + 1) * 128], identb[:, :], [kc_, identb], inc=(b == 7))
                    ktc = KTc.get()
                    cp("dve", ktc, ktc[:, :], pb[:, :], [pb])
                    BI = BIp.get()
                    ts("dve", BI, BI[:, 0:9], CCs[s][:, 0:9, h], CEs[s][:, h:h + 1], -1.0, ALU.subtract, ALU.mult, [CCs[s], CEs[s]])
                    ps = psf.get()
                    for b in range(8):
                        mm(ps, ps[:, b * TS:(b + 1) * TS], ktc[:, b * 128:(b + 1) * 128], qn[:, h, :], [ktc, qn], inc=(b == 7))
                    ps2 = psf.get()
                    mm(ps2, ps2[:TS, 0:TS], kn[:, h, :], qn[:, h, :], [kn, qn])
                    pt = ptp.get()
                    for b in range(8):
                        act(pt, pt[:, b * TS:(b + 1) * TS], ps[:, b * TS:(b + 1) * TS], AF.Exp, [ps, BI], bias=BI[:, b:b + 1])
                    act(pt, pt[:TS, 8 * TS:9 * TS], ps2[:TS, 0:TS], AF.Exp, [ps2, BI], bias=BI[:TS, 8:9])
                    tt("pool", pt, pt[:TS, 8 * TS:9 * TS], pt[:TS, 8 * TS:9 * TS], maskb[:TS, :TS], ALU.mult, [pt, maskb])
                    psO = psf.get()
                    psL = psf.get()
                    for b in range(8):
                        mm(psO, psO[:, 0:TS], vc_[:, b, h * 128:(h + 1) * 128], pt[:, b * TS:(b + 1) * TS], [vc_, pt],
                           start=(b == 0), stop=False, inc=False)
                    mm(psO, psO[:, 0:TS], vn[:TS, h * 128:(h + 1) * 128], pt[:TS, 8 * TS:9 * TS], [vn, pt], start=False, stop=True)
                    for b in range(8):
                        mm(psL, psL[:, 0:TS], onesb[:, :], pt[:, b * TS:(b + 1) * TS], [onesb, pt], start=(b == 0), stop=False, inc=False)
                    mm(psL, psL[:, 0:TS], onesb[:TS, :], pt[:TS, 8 * TS:9 * TS], [onesb, pt], start=False, stop=True)
                    rl = rlp.get()
                    P.op("dve", lambda e, rl=rl, psL=psL: e.reciprocal(rl[:, :], psL[:, 0:TS]), reads=[psL], writes=[rl])
                    tt("dve", ots, ots[:, h, :], psO[:, 0:TS], rl[:, :], ALU.mult, [psO, rl])
                P.dma("sp", OT.rearrange("(h p) r -> p h r", p=128)[:, :, row0:row0 + TS], ots[:, :, :], reads=[ots])
            P.flush()

    plan = [("p0", phase0)]
    for L in range(4):
        l2 = L // 2
        if L % 2 == 0:
            plan.append((f"pe{L}", lambda l2=l2: phase_proj_even(l2)))
            plan.append((f"gdn{L}", lambda l2=l2: phase_gdn(l2)))
            plan.append((f"out{L}", lambda L=L, l2=l2: phase_out(L, w_out_even[l2])))
        else:
            plan.append((f"po{L}", lambda l2=l2: phase_proj_odd(l2)))
            plan.append((f"ap{L}", lambda l2=l2: phase_attn_prompt(l2)))
            plan.append((f"as{L}", lambda l2=l2: phase_attn_sample(l2)))
            plan.append((f"out{L}", lambda L=L, l2=l2: phase_out(L, w_out_odd[l2])))
        plan.append((f"f1_{L}", lambda L=L: phase_ffn1(L)))
        plan.append((f"f2_{L}", lambda L=L: phase_ffn2(L)))
    for name, fn in plan:
        if OPTS.get('only') and name not in OPTS['only']:
            continue
        fn()
        if upto is not None and name == upto:
            break
    if dbg:
        HTd = nc.dram_tensor("HTd", [128, 8, R], BF16, kind="ExternalOutput").ap()
        P.dma("sp", HTd, HT[:, :, :], reads=HTt)
        P.flush()
    P.stack.close()
    return nc


def make_consts():
    i = np.arange(128)
    ident = np.eye(128, dtype=np.float32)
    U = (i[:, None] <= i[None, :]).astype(np.float32)
    ones = np.ones((128, 128), np.float32)
    negls = np.where(i[:, None] > i[None, :], 0.0, -30000.0).astype(np.float32)
    negu = np.where(i[None, :] >= i[:, None], 0.0, -30000.0).astype(np.float32)
    return dict(c_ident=ident, c_U=U, c_ones=ones, c_negls=negls, c_negu=negu)


def make_in_maps(inp):
    f = lambda a: np.ascontiguousarray(np.asarray(a, dtype=np.float32))
    consts = make_consts()
    shared = {k: f(inp[k]) for k in (
        "norm_mix_g", "norm_ffn_g", "w_in_even", "gdn_conv_w", "gdn_a_log", "gdn_dt_bias", "gdn_norm_g",
        "sconv_w", "w_out_even", "w_in_odd", "fox_b_f", "fox_q_norm_g", "fox_k_norm_g", "w_out_odd",
        "ffn_w_gate", "ffn_w_up", "ffn_w_down")}
    shared.update(consts)
    xp = f(inp["x_prompt"]); xs = f(inp["x_sample"])
    ckk = f(inp["cache_fox_k"]).reshape(2, 32, PAST, D)
    cvv = f(inp["cache_fox_v"]).reshape(2, 32, PAST, D)
    clf = f(inp["cache_fox_logf"])
    sS = f(inp["state_gdn_S"]); sc = f(inp["state_gdn_conv"]); ss = f(inp["state_sconv"])
    maps = []
    for c in range(NCORES):
        sl = slice(4 * c, 4 * c + 4)
        m = dict(shared)
        m["x_all"] = np.ascontiguousarray(np.concatenate([xp[c % 4], xs[sl].reshape(NSS * TS, D)], axis=0))
        m["ck"] = np.ascontiguousarray(ckk[:, sl])
        m["cv"] = np.ascontiguousarray(cvv[:, sl])
        m["clf"] = np.ascontiguousarray(clf[:, sl])
        m["S0"] = np.ascontiguousarray(np.concatenate([np.zeros((2, 1, 4, 128, 128), np.float32), sS[:, sl]], axis=1))
        m["conv0"] = np.ascontiguousarray(np.concatenate([np.zeros((2, 1, 3, 1536), np.float32), sc[:, sl]], axis=1))
        m["sc0"] = np.ascontiguousarray(np.concatenate([np.zeros((2, 1, 2, 512), np.float32), ss[:, sl]], axis=1))
        maps.append(m)
    return maps


def kernel(**inp):
    nc = build()
    maps = make_in_maps(inp)
    res = run_bass_kernel_spmd(nc, maps, core_ids=list(range(NCORES))).results
    g = lambda k, c: np.asarray(res[c][k], dtype=np.float32)
    y_p = np.stack([g("y_all", c)[:TP] for c in range(4)])
    y_s = np.concatenate([g("y_all", c)[TP:].reshape(NSS, TS, D) for c in range(8)])
    def fox(k, last):
        p = np.stack([g(k, c)[:, :TP] for c in range(4)], axis=1)
        s = np.concatenate([g(k, c)[:, TP:].reshape(2, NSS, TS, last) for c in range(8)], axis=1)
        return p, s
    pk, sk = fox("fox_k", D); pv, sv = fox("fox_v", D); pl, sl_ = fox("fox_logf", 8)
    pk = pk.reshape(2, 4, TP, 8, 128); sk = sk.reshape(2, 32, TS, 8, 128)
    pv = pv.reshape(2, 4, TP, 8, 128); sv = sv.reshape(2, 32, TS, 8, 128)
    def st(k):
        p = np.stack([g(k, c)[:, 0] for c in range(4)], axis=1)
        s = np.concatenate([g(k, c)[:, 1:] for c in range(8)], axis=1)
        return p, s
    pS, sS = st("gdn_S"); pc, scv = st("gdn_conv"); psc, ssc = st("sconv_o")
    return (y_p, y_s, pk, pv, pl, pS, pc, psc, sk, sv, sl_, sS, scv, ssc)
```

```python
import numpy as np
from contextlib import ExitStack
import concourse.bass as bass
import concourse.mybir as mybir
from concourse.bass_utils import run_bass_kernel_spmd

F32 = mybir.dt.float32
BF16 = mybir.dt.bfloat16
AF = mybir.ActivationFunctionType
ALU = mybir.AluOpType

D = 1024
TP = 4096
NSS = 4
TS = 64
R = TP + NSS * TS
PAST = 1024
DFF = 2816
EVEN_IN = 3592
ODD_IN = 3080
EPS = 1e-6
NCORES = 8

TILES = [(r * 128, 128) for r in range(32)] + [(TP + TS * s, TS) for s in range(NSS)]
GROUPS = [(g * 512, 512, [4 * g + i for i in range(4)]) for g in range(8)] + [(TP, 256, [32, 33, 34, 35])]
SEQS = [(0, TP)] + [(TP + TS * s, TS) for s in range(NSS)]


class Tl:
    def __init__(self, t, name=""):
        self.t = t
        self.w = None
        self.rd = {}
        self.sem = None
        self.cnt = 0
        self.name = name
        self.psum = False

    def __getitem__(self, k):
        return self.t[k]


class Eng:
    def __init__(self, name):
        self.name = name
        self.ops = []
        self.cnt = 0
        self.sem = None
        self.waited = {}


class Prog:
    def __init__(self, nc):
        self.nc = nc
        self.stack = ExitStack()
        self.E = {n: Eng(n) for n in ("pe", "act", "dve", "pool", "sp")}
        for e in self.E.values():
            e.sem = self.stack.enter_context(nc.semaphore("s_" + e.name))
        self.pending = []
        self.dsems = []
        self.free = {"sp": [], "pool": []}
        self.phase_tiles = []

    def _deps(self, eng, reads, writes):
        deps = {}

        def add(st, raw):
            if st is None:
                return
            s, v = st
            if s is eng.sem and (eng.name == "pe" or not raw):
                return
            k = id(s)
            if deps.get(k, (None, 0))[1] < v:
                deps[k] = (s, v)

        for t in reads:
            add(t.w, True)
            if t.psum:
                for st in t.rd.values():
                    add(st, False)
        for t in writes:
            add(t.w, False)
            for st in t.rd.values():
                add(st, False)
        out = []
        for k, (s, v) in deps.items():
            if eng.waited.get(k, 0) >= v:
                continue
            eng.waited[k] = v
            out.append((s, v))
        return out

    def op(self, en, fn, reads=(), writes=(), inc=True):
        eng = self.E[en]
        waits = self._deps(eng, reads, writes)
        if inc:
            eng.cnt += 1
            st = (eng.sem, eng.cnt)
        else:
            st = (eng.sem, eng.cnt + 1)
        for t in writes:
            t.w = st
            t.rd = {}
        for t in reads:
            if t in writes:
                continue
            t.rd[id(eng.sem)] = st
        eng.ops.append((waits, fn, eng.sem if inc else None, 1))

    def _dsem(self, t, qn):
        if t.sem is None:
            t.sem = {}
        if qn not in t.sem:
            if self.free[qn]:
                t.sem[qn] = self.free[qn].pop()
            else:
                ds = Eng("dsem")
                ds.sem = self.stack.enter_context(self.nc.semaphore())
                self.dsems.append(ds)
                t.sem[qn] = ds
            self.phase_tiles.append((t, qn))
        return t.sem[qn]

    def dma(self, qn, out_ap, in_ap, reads=(), writes=(), **kw):
        eng = self.E[qn]
        waits = self._deps(eng, reads, writes)
        t = writes[0] if writes else reads[0]
        ds = self._dsem(t, qn)
        ds.cnt += 16
        st = (ds.sem, ds.cnt)
        for x in writes:
            x.w = st
            x.rd = {}
        for x in reads:
            x.rd[id(ds.sem)] = st
        eng.ops.append((waits, lambda e: e.dma_start(out=out_ap, in_=in_ap, **kw), ds.sem, 16))
        self.pending.append(st)

    def flush(self):
        eng = self.E["sp"]
        best = {}
        for s, v in self.pending:
            if best.get(id(s), (s, 0))[1] < v:
                best[id(s)] = (s, v)
        waits = [(s, v) for k, (s, v) in best.items() if eng.waited.get(k, 0) < v]
        eng.ops.append((waits, None, None, 0))
        self.pending = []
        with self.nc.Block() as blk:
            for name, deco in (("pe", blk.tensor), ("act", blk.scalar), ("dve", blk.vector),
                               ("pool", blk.gpsimd), ("sp", blk.sync)):
                ops = self.E[name].ops
                if not ops:
                    continue

                def body(e, ops=ops):
                    for waits, fn, sem, amt in ops:
                        if fn is None:
                            for s, v in waits:
                                e.wait_ge(s, v)
                            continue
                        for s, v in waits[:-1]:
                            e.wait_ge(s, v)
                        ins = fn(e)
                        if waits:
                            ins._wait_ge(*waits[-1])
                        if sem is not None:
                            ins.then_inc(sem, amt)

                deco(body)
        for e in self.E.values():
            e.ops = []
        for e in self.E.values():
            for e2 in self.E.values():
                e.waited[id(e2.sem)] = e2.cnt
            for ds in self.dsems:
                e.waited[id(ds.sem)] = ds.cnt
        for t, qn in self.phase_tiles:
            self.free[qn].append(t.sem.pop(qn))
        self.phase_tiles = []


class Pool_:
    def __init__(self, tiles):
        self.tiles = tiles
        self.i = 0

    def get(self):
        t = self.tiles[self.i % len(self.tiles)]
        self.i += 1
        return t


OPTS = {}


def build(upto=None, dbg=False):
    nc = bass.Bass("TRN2", target_bir_lowering=False)

    def din(name, shape):
        return nc.dram_tensor(name, list(shape), F32, kind="ExternalInput").ap()

    def dout(name, shape):
        return nc.dram_tensor(name, list(shape), F32, kind="ExternalOutput").ap()

    def dscr(name, shape, dt):
        return nc.dram_tensor(name, list(shape), dt, kind=("ExternalOutput" if dbg else "Internal")).ap()

    x_all = din("x_all", [R, D])
    ck = din("ck", [2, NSS, PAST, D])
    cv = din("cv", [2, NSS, PAST, D])
    clf = din("clf", [2, NSS, PAST, 8])
    S0 = din("S0", [2, 5, 4, 128, 128])
    conv0 = din("conv0", [2, 5, 3, 1536])
    sc0 = din("sc0", [2, 5, 2, 512])
    norm_mix_g = din("norm_mix_g", [4, D])
    norm_ffn_g = din("norm_ffn_g", [4, D])
    w_in_even = din("w_in_even", [2, D, EVEN_IN])
    gdn_conv_w = din("gdn_conv_w", [2, 4, 1536])
    gdn_a_log = din("gdn_a_log", [2, 4])
    gdn_dt_bias = din("gdn_dt_bias", [2, 4])
    gdn_norm_g = din("gdn_norm_g", [2, 128])
    sconv_w = din("sconv_w", [2, 3, 512])
    w_out_even = din("w_out_even", [2, D, D])
    w_in_odd = din("w_in_odd", [2, D, ODD_IN])
    fox_b_f = din("fox_b_f", [2, 8])
    fox_q_norm_g = din("fox_q_norm_g", [2, 128])
    fox_k_norm_g = din("fox_k_norm_g", [2, 128])
    w_out_odd = din("w_out_odd", [2, D, D])
    ffn_w_gate = din("ffn_w_gate", [4, D, DFF])
    ffn_w_up = din("ffn_w_up", [4, D, DFF])
    ffn_w_down = din("ffn_w_down", [4, DFF, D])
    c_ident = din("c_ident", [128, 128])
    c_U = din("c_U", [128, 128])
    c_ones = din("c_ones", [128, 128])
    c_negls = din("c_negls", [128, 128])
    c_negu = din("c_negu", [128, 128])

    y_all = dout("y_all", [R, D])
    fox_k = dout("fox_k", [2, R, D])
    fox_v = dout("fox_v", [2, R, D])
    fox_logf = dout("fox_logf", [2, R, 8])
    gdn_S = dout("gdn_S", [2, 5, 4, 128, 128])
    gdn_conv = dout("gdn_conv", [2, 5, 3, 1536])
    sconv_o = dout("sconv_o", [2, 5, 2, 512])

    X = dscr("X", [R, D], F32)
    QKVT = dscr("QKVT", [1536, R], BF16)
    Z = dscr("Z", [R, 512], BF16)
    OT = dscr("OT", [D, R], BF16)
    KT = dscr("KT", [D, R], BF16)
    QT = dscr("QT", [D, R], BF16)
    V = dscr("V", [R, D], BF16)
    ACTS = dscr("ACTS", [DFF, R], BF16)

    P = Prog(nc)
    GS = P.stack

    uid = [0]

    def sb(st, name, shape, dt):
        uid[0] += 1
        name = f"{name}_{uid[0]}"
        return Tl(st.enter_context(nc.sbuf_tensor(name, list(shape), dt)), name)

    def pst(st, name, dt):
        shape = [128, 512] if dt == F32 else [128, 1024]
        uid[0] += 1
        name = f"{name}_{uid[0]}"
        t = Tl(st.enter_context(nc.psum_tensor(name, shape, dt)), name)
        t.psum = True
        return t

    def bcast(ap2d_row, nparts, ncols):
        return bass.AP(ap2d_row.tensor, ap2d_row.offset, [[0, nparts], [1, ncols]])

    def mm(out_t, out_ap, lhsT, rhs, reads, start=True, stop=True, inc=True):
        P.op("pe", lambda e: e.matmul(out_ap, lhsT, rhs, start=start, stop=stop), reads=reads, writes=[out_t], inc=inc)

    def tr(out_t, out_ap, in_ap, ident_ap, reads, inc=True):
        P.op("pe", lambda e: e.transpose(out_ap, in_ap, ident_ap), reads=reads, writes=[out_t], inc=inc)

    def act(out_t, out_ap, in_ap, func, reads, bias=None, scale=None, accum=None):
        kw = {}
        if bias is not None:
            kw["bias"] = bias
        if scale is not None:
            kw["scale"] = scale
        writes = [out_t]
        if accum is not None:
            kw["accum_out"] = accum[1]
            if accum[0] is not out_t:
                writes.append(accum[0])
        P.op("act", lambda e: e.activation(out=out_ap, in_=in_ap, func=func, **kw), reads=reads, writes=writes)

    def ts(en, out_t, out_ap, in0, s1, s2, op0, op1, reads):
        if op1 is None:
            P.op(en, lambda e: e.tensor_scalar(out_ap, in0, s1, None, op0), reads=reads, writes=[out_t])
        else:
            P.op(en, lambda e: e.tensor_scalar(out_ap, in0, s1, s2, op0, op1), reads=reads, writes=[out_t])

    def tt(en, out_t, out_ap, in0, in1, op, reads):
        P.op(en, lambda e: e.tensor_tensor(out_ap, in0, in1, op), reads=reads, writes=[out_t])

    def stt(en, out_t, out_ap, in0, scalar, in1, op0, op1, reads):
        P.op(en, lambda e: e.scalar_tensor_tensor(out_ap, in0, scalar, in1, op0, op1), reads=reads, writes=[out_t])

    def cp(en, out_t, out_ap, in_ap, reads):
        if en == "act":
            P.op("act", lambda e: e.copy(out_ap, in_ap), reads=reads, writes=[out_t])
        else:
            P.op(en, lambda e: e.tensor_copy(out_ap, in_ap), reads=reads, writes=[out_t])

    def rsqrt_(en, t, ap, mult, add, reads=()):
        n = ap.shape[0]
        act(t, ap, ap, AF.Sqrt, list(reads) + [t, epsT], bias=epsT[:n, 0:1], scale=mult)
        P.op("dve", lambda e: e.reciprocal(ap, ap), reads=[t], writes=[t])

    def abs_(t, ap, src_t, src_ap):
        ts("dve", t, ap, src_ap, -1.0, None, ALU.mult, None, [src_t])
        tt("dve", t, ap, ap, src_ap, ALU.max, [t, src_t])

    HT = GS.enter_context(nc.sbuf_tensor("HT", [128, 8, R], BF16))
    HTt = [Tl(HT, f"HT{i}") for i in range(len(TILES))]
    identf = sb(GS, "identf", [128, 128], F32)
    identb = sb(GS, "identb", [128, 128], BF16)
    Uf = sb(GS, "Uf", [128, 128], F32)
    onesf = sb(GS, "onesf", [128, 128], F32)
    onesb = sb(GS, "onesb", [128, 128], BF16)
    negls = sb(GS, "negls", [128, 128], F32)
    negu = sb(GS, "negu", [128, 128], F32)
    maskb = sb(GS, "maskb", [128, 128], BF16)
    epsT = sb(GS, "epsT", [128, 1], F32)
    GB = sb(GS, "GB", [128, 36, 8], F32)
    CCp = sb(GS, "CCp", [128, 32, 8], F32)
    CEp = sb(GS, "CEp", [128, 8, 8], F32)
    CCs = [sb(GS, f"CCs{s}", [128, 9, 8], F32) for s in range(NSS)]
    CEs = [sb(GS, f"CEs{s}", [128, 8], F32) for s in range(NSS)]

    def norm_a(ti, xt, gB, pools):
        r0, n = TILES[ti]
        junk, ssp, hnp, psb = pools
        jk = junk.get()
        ss = ssp.get()
        hn = hnp.get()
        act(jk, jk[:n, :], xt[:n, :], AF.Square, [xt], accum=(ss, ss[:n, 0:1]))
        rsqrt_("dve", ss, ss[:n, 0:1], 1.0 / D, EPS)
        stt("dve", hn, hn[:n, :], xt[:n, :], ss[:n, 0:1], gB[:n, :], ALU.mult, ALU.mult, [xt, ss, gB])
        return hn

    def norm_b(ti, hn, pools):
        r0, n = TILES[ti]
        psb = pools[3]
        pb = psb.get()
        for kc in range(8):
            tr(pb, pb[:, kc * 128:kc * 128 + n], hn[:n, kc * 128:(kc + 1) * 128], identb[:n, :n], [hn, identb], inc=(kc == 7))
        src = pb.t[:, :].rearrange("p (k m) -> p k m", k=8)[:, :, 0:n]
        cp("act", HTt[ti], HT[:, :, r0:r0 + n], src, [pb])

    def norm_tile(ti, xt, gB, pools):
        norm_b(ti, norm_a(ti, xt, gB, pools), pools)

    def norm_pools(st, psb):
        return (Pool_([sb(st, f"njk{i}", [128, D], BF16) for i in range(2)]),
                Pool_([sb(st, f"nss{i}", [128, 1], F32) for i in range(3)]),
                Pool_([sb(st, f"nhn{i}", [128, D], BF16) for i in range(3)]),
                psb)

    def load_gain(st, name, g_ap_row):
        t = sb(st, name, [128, D], F32)
        P.dma("sp", t[:, :], bcast(g_ap_row, 128, D), writes=[t])
        return t

    def phase0():
        with ExitStack() as st:
            psb = Pool_([pst(st, f"p0b{i}", BF16) for i in range(2)])
            P.dma("sp", identf[:, :], c_ident, writes=[identf])
            P.dma("sp", Uf[:, :], c_U, writes=[Uf])
            P.dma("sp", onesf[:, :], c_ones, writes=[onesf])
            P.dma("sp", negls[:, :], c_negls, writes=[negls])
            P.dma("sp", negu[:, :], c_negu, writes=[negu])
            P.op("dve", lambda e: e.memset(epsT[:, :], EPS), writes=[epsT])
            P.dma("pool", identb[:, :], c_ident, writes=[identb])
            P.dma("pool", onesb[:, :], c_ones, writes=[onesb])
            P.dma("pool", maskb[:, :], c_U, writes=[maskb])
            if OPTS.get('p0_consts_only'):
                P.flush()
                return
            gB = load_gain(st, "g0", norm_mix_g[0, :])
            xp = Pool_([sb(st, f"x0_{i}", [128, D], F32) for i in range(3)])
            pools = norm_pools(st, psb)
            for ti, (r0, n) in enumerate(TILES):
                xt = xp.get()
                P.dma("sp", xt[:n, :], x_all[r0:r0 + n, :], writes=[xt])
                norm_tile(ti, xt, gB, pools)
            P.flush()

    def load_w(wb, w2d, c0, ncol, dst0=0, q="pool"):
        src = w2d.rearrange("(k p) c -> p k c", p=128)[:, :, c0:c0 + ncol]
        P.dma(q, wb[:, :, dst0:dst0 + ncol], src, writes=[wb])

    def phase_proj_even(l2):
        W = w_in_even[l2]
        with ExitStack() as st:
            psf = Pool_([pst(st, f"pef{i}", F32) for i in range(8)])
            wbs = Pool_([sb(st, f"wb{i}", [128, 8, 520], BF16) for i in range(3)])
            cw = sb(st, "cw", [128, 12, 4], F32)
            scw = sb(st, "scw", [128, 4, 3], F32)
            for c in range(12):
                P.dma("sp", cw[:, c, :], gdn_conv_w[l2, :, c * 128:(c + 1) * 128].rearrange("t c -> c t"),
                      writes=[cw], allow_slow_non_contiguous=True)
            for c in range(4):
                P.dma("sp", scw[:, c, :], sconv_w[l2, :, c * 128:(c + 1) * 128].rearrange("t c -> c t"),
                      writes=[scw], allow_slow_non_contiguous=True)
            rb = [[sb(st, f"rb{c}_{p}", [128, 516], F32) for p in range(2)] for c in range(4)]
            rbs = Pool_([sb(st, f"rbs{i}", [128, 68], F32) for i in range(12)])
            yp = Pool_([sb(st, f"cy{i}", [128, 512], F32) for i in range(3)])
            sop = Pool_([sb(st, f"so{i}", [128, 512], F32) for i in range(6)])
            sqp = Pool_([sb(st, f"sq{i}", [128, 512], F32) for i in range(3)])
            rsp = Pool_([sb(st, f"rs{i}", [128, 512], F32) for i in range(3)])
            obp = Pool_([sb(st, f"ob{i}", [128, 512], BF16) for i in range(3)])

            o1 = 2056

            def wload(i, wb):
                if i < 3:
                    load_w(wb, W, i * 512, 512)
                elif i < 7:
                    c = i - 3
                    for j in range(3):
                        load_w(wb, W, o1 + j * 512 + c * 128, 128, dst0=j * 128)
                else:
                    load_w(wb, W, 1536, 520)

            wq = [wbs.get()]
            wload(0, wq[0])

            def next_w(i):
                if i + 1 < 8:
                    wq.append(wbs.get())
                    wload(i + 1, wq[i + 1])

            def conv_taps(cur, off0, n, chunk, y, yoff, ntap, wt):
                ts("dve", y, y[:, yoff:yoff + n], cur[:, off0:off0 + n], wt[:, chunk, 0:1], None, ALU.mult, None, [cur, wt])
                for i in range(1, ntap):
                    stt("dve", y, y[:, yoff:yoff + n], cur[:, off0 + i:off0 + i + n], wt[:, chunk, i:i + 1], y[:, yoff:yoff + n],
                        ALU.mult, ALU.add, [cur, wt, y])

            for b in range(3):
                wb = wq[b]
                next_w(b)
                items = [(gi, c) for gi in range(len(GROUPS)) for c in range(4)]
                stt_ = {}

                def stage0(k, wb=wb):
                    gi, c = items[k]
                    g0, gn, gtiles = GROUPS[gi]
                    hts = [HTt[i] for i in gtiles]
                    ps = psf.get()
                    for kc in range(8):
                        mm(ps, ps[:, 0:gn], wb[:, kc, c * 128:(c + 1) * 128], HT[:, kc, g0:g0 + gn], [wb] + hts,
                           start=(kc == 0), stop=(kc == 7), inc=(kc == 7))
                    stt_[k] = {"ps": ps}

                def stage1(k, b=b):
                    gi, c = items[k]
                    g0, gn, gtiles = GROUPS[gi]
                    chunk = b * 4 + c
                    ps = stt_[k]["ps"]
                    if gi < 8:
                        cur = rb[c][gi % 2]
                        prev = rb[c][(gi - 1) % 2]
                        cp("act", cur, cur[:, 3:3 + gn], ps[:, 0:gn], [ps])
                        if gi == 0:
                            P.dma("sp", cur[:, 0:3], conv0[l2, 0, :, chunk * 128:(chunk + 1) * 128].rearrange("t c -> c t"),
                                  writes=[cur], allow_slow_non_contiguous=True)
                        else:
                            cp("pool", cur, cur[:, 0:3], prev[:, 512:515], [prev])
                        if gi == 7:
                            P.dma("sp", gdn_conv[l2, 0, :, chunk * 128:(chunk + 1) * 128].rearrange("t c -> c t"),
                                  cur[:, 512:515], reads=[cur], allow_slow_non_contiguous=True)
                        stt_[k]["cur"] = [cur]
                    else:
                        curs = []
                        for s_ in range(NSS):
                            cur = rbs.get()
                            cp("act", cur, cur[:, 3:3 + TS], ps[:, s_ * TS:(s_ + 1) * TS], [ps])
                            P.dma("sp", cur[:, 0:3], conv0[l2, 1 + s_, :, chunk * 128:(chunk + 1) * 128].rearrange("t c -> c t"),
                                  writes=[cur], allow_slow_non_contiguous=True)
                            P.dma("sp", gdn_conv[l2, 1 + s_, :, chunk * 128:(chunk + 1) * 128].rearrange("t c -> c t"),
                                  cur[:, TS:TS + 3], reads=[cur], allow_slow_non_contiguous=True)
                            curs.append(cur)
                        stt_[k]["cur"] = curs

                def stage1b(k, b=b):
                    gi, c = items[k]
                    g0, gn, gtiles = GROUPS[gi]
                    chunk = b * 4 + c
                    curs = stt_[k]["cur"]
                    y = yp.get()
                    if gi < 8:
                        conv_taps(curs[0], 0, gn, chunk, y, 0, 4, cw)
                    else:
                        for s_ in range(NSS):
                            conv_taps(curs[s_], 0, TS, chunk, y, s_ * TS, 4, cw)
                    stt_[k]["y"] = y

                def stage2(k, b=b):
                    gi, c = items[k]
                    g0, gn, gtiles = GROUPS[gi]
                    chunk = b * 4 + c
                    y = stt_[k]["y"]
                    if b == 2:
                        ob = obp.get()
                        act(ob, ob[:, 0:gn], y[:, 0:gn], AF.Silu, [y])
                        P.dma("sp", QKVT[chunk * 128:(chunk + 1) * 128, g0:g0 + gn], ob[:, 0:gn], reads=[ob])
                    else:
                        so = sop.get()
                        act(so, so[:, 0:gn], y[:, 0:gn], AF.Silu, [y])
                        sq = sqp.get()
                        act(sq, sq[:, 0:gn], so[:, 0:gn], AF.Square, [so])
                        stt_[k]["so"] = so
                        stt_[k]["sq"] = sq

                def stage3(k, b=b):
                    gi, c = items[k]
                    g0, gn, gtiles = GROUPS[gi]
                    if b == 2:
                        return
                    sq = stt_[k]["sq"]
                    ps = psf.get()
                    mm(ps, ps[:, 0:gn], onesf[:, :], sq[:, 0:gn], [onesf, sq])
                    stt_[k]["ps1"] = ps

                def stage4(k, b=b):
                    gi, c = items[k]
                    g0, gn, gtiles = GROUPS[gi]
                    if b == 2:
                        return
                    ps = stt_[k]["ps1"]
                    rs = rsp.get()
                    act(rs, rs[:, 0:gn], ps[:, 0:gn], AF.Sqrt, [ps, epsT], bias=epsT[:, 0:1])
                    stt_[k]["rs"] = rs

                def stage5(k, b=b):
                    gi, c = items[k]
                    g0, gn, gtiles = GROUPS[gi]
                    chunk = b * 4 + c
                    st_ = stt_.pop(k)
                    if b == 2:
                        return
                    so, rs = st_["so"], st_["rs"]
                    P.op("dve", lambda e, rs=rs, gn=gn: e.reciprocal(rs[:, 0:gn], rs[:, 0:gn]), reads=[rs], writes=[rs])
                    sc = (128.0 ** -0.5) if b == 0 else 1.0
                    ob = obp.get()
                    stt("dve", ob, ob[:, 0:gn], so[:, 0:gn], sc, rs[:, 0:gn], ALU.mult, ALU.mult, [so, rs])
                    P.dma("sp", QKVT[chunk * 128:(chunk + 1) * 128, g0:g0 + gn], ob[:, 0:gn], reads=[ob])

                NI = len(items)
                stages = [stage0, stage1, stage1b, stage2, stage3, stage4, stage5]
                for t in range(NI + 6):
                    for lag in (6, 5, 4, 3, 2, 1, 0):
                        if 0 <= t - lag < NI:
                            stages[lag](t - lag)

            mb = [sb(st, f"mb{p}", [128, 516], F32) for p in range(2)]
            mbs = Pool_([sb(st, f"mbs{i}", [128, 68], F32) for i in range(3)])
            gcp = Pool_([sb(st, f"gc{i}", [128, 512], F32) for i in range(2)])
            for c in range(4):
                wb = wq[3 + c]
                next_w(3 + c)
                stt_ = {}

                def sstage0(gi, wb=wb):
                    g0, gn, gtiles = GROUPS[gi]
                    hts = [HTt[i] for i in gtiles]
                    pss = []
                    for j in range(3):
                        ps = psf.get()
                        for kc in range(8):
                            mm(ps, ps[:, 0:gn], wb[:, kc, j * 128:(j + 1) * 128], HT[:, kc, g0:g0 + gn], [wb] + hts,
                               start=(kc == 0), stop=(kc == 7), inc=(kc == 7))
                        pss.append(ps)
                    stt_[gi] = pss

                def sstage1(gi, c=c):
                    g0, gn, gtiles = GROUPS[gi]
                    pgb, pgc, phin = stt_.pop(gi)
                    gct = gcp.get()
                    cp("act", gct, gct[:, 0:gn], pgc[:, 0:gn], [pgc])
                    y = yp.get()
                    if gi < 8:
                        cur = mb[gi % 2]
                        prev = mb[(gi - 1) % 2]
                        tt("dve", cur, cur[:, 2:2 + gn], phin[:, 0:gn], gct[:, 0:gn], ALU.mult, [phin, gct])
                        if gi == 0:
                            P.dma("sp", cur[:, 0:2], sc0[l2, 0, :, c * 128:(c + 1) * 128].rearrange("t c -> c t"),
                                  writes=[cur], allow_slow_non_contiguous=True)
                        else:
                            cp("pool", cur, cur[:, 0:2], prev[:, 512:514], [prev])
                        conv_taps(cur, 0, gn, c, y, 0, 3, scw)
                        if gi == 7:
                            P.dma("sp", sconv_o[l2, 0, :, c * 128:(c + 1) * 128].rearrange("t c -> c t"),
                                  cur[:, 512:514], reads=[cur], allow_slow_non_contiguous=True)
                    else:
                        for s_ in range(NSS):
                            cur = mbs.get()
                            sl = slice(s_ * TS, (s_ + 1) * TS)
                            tt("dve", cur, cur[:, 2:2 + TS], phin[:, sl], gct[:, sl], ALU.mult, [phin, gct])
                            P.dma("sp", cur[:, 0:2], sc0[l2, 1 + s_, :, c * 128:(c + 1) * 128].rearrange("t c -> c t"),
                                  writes=[cur], allow_slow_non_contiguous=True)
                            conv_taps(cur, 0, TS, c, y, s_ * TS, 3, scw)
                            P.dma("sp", sconv_o[l2, 1 + s_, :, c * 128:(c + 1) * 128].rearrange("t c -> c t"),
                                  cur[:, TS:TS + 2], reads=[cur], allow_slow_non_contiguous=True)
                    ob = obp.get()
                    tt("dve", ob, ob[:, 0:gn], pgb[:, 0:gn], y[:, 0:gn], ALU.mult, [pgb, y])
                    P.dma("sp", OT[512 + c * 128:512 + (c + 1) * 128, g0:g0 + gn], ob[:, 0:gn], reads=[ob])

                NG_ = len(GROUPS)
                for t in range(NG_ + 1):
                    if t < NG_:
                        sstage0(t)
                    if 0 <= t - 1 < NG_:
                        sstage1(t - 1)

            wb = wq[7]
            AB = sb(st, "AB", [128, 36, 8], F32)
            P.op("pool", lambda e: e.memset(AB[:, :, :], 0.0), writes=[AB])
            zsp = Pool_([sb(st, f"zs{i}", [128, 512], BF16) for i in range(3)])
            for ti, (r0, n) in enumerate(TILES):
                ps = psf.get()
                for kc in range(8):
                    mm(ps, ps[:n, 0:512], HT[:, kc, r0:r0 + n], wb[:, kc, 0:512], [wb, HTt[ti]],
                       start=(kc == 0), stop=(kc == 7), inc=(kc == 7))
                ps2 = psf.get()
                for kc in range(8):
                    mm(ps2, ps2[:n, 0:8], HT[:, kc, r0:r0 + n], wb[:, kc, 512:520], [wb, HTt[ti]],
                       start=(kc == 0), stop=(kc == 7), inc=(kc == 7))
                zs = zsp.get()
                act(zs, zs[:n, :], ps[:n, 0:512], AF.Silu, [ps])
                P.dma("sp", Z[r0:r0 + n, :], zs[:n, :], reads=[zs])
                cp("dve", AB, AB[:n, ti, :], ps2[:n, 0:8], [ps2])
            dtb = sb(st, "dtb", [128, 4], F32)
            nex = sb(st, "nex", [128, 4], F32)
            P.dma("sp", dtb[:, :], bcast(gdn_dt_bias[l2, :], 128, 4), writes=[dtb])
            P.dma("sp", nex[:, :], bcast(gdn_a_log[l2, :], 128, 4), writes=[nex])
            act(nex, nex[:, :], nex[:, :], AF.Exp, [nex])
            ts("dve", nex, nex[:, :], nex[:, :], -1.0, None, ALU.mult, None, [nex])
            t1 = sb(st, "gt1", [128, 36], F32)
            t2 = sb(st, "gt2", [128, 36], F32)
            t3 = sb(st, "gt3", [128, 36], F32)
            for h in range(4):
                ts("dve", t1, t1[:, :], AB[:, :, h], dtb[:, h:h + 1], None, ALU.add, None, [AB, dtb])
                abs_(t2, t2[:, :], t1, t1[:, :])
                act(t2, t2[:, :], t2[:, :], AF.Exp, [t2], scale=-1.0)
                act(t2, t2[:, :], t2[:, :], AF.Ln, [t2], bias=1.0)
                ts("dve", t3, t3[:, :], t1[:, :], 0.0, None, ALU.max, None, [t1])
                tt("dve", t3, t3[:, :], t3[:, :], t2[:, :], ALU.add, [t3, t2])
                ts("dve", GB, GB[:, :, h], t3[:, :], nex[:, h:h + 1], None, ALU.mult, None, [t3, nex])
            act(GB, GB[:, :, 4:8], AB[:, :, 4:8], AF.Sigmoid, [AB])
            P.flush()

    def phase_gdn(l2):
        with ExitStack() as st:
            psf = Pool_([pst(st, f"pgf{i}", F32) for i in range(6)])
            psb = Pool_([pst(st, f"pgb{i}", BF16) for i in range(2)])
            NG = sb(st, "NG", [128, 128], F32)
            P.dma("sp", NG[:, :], bcast(gdn_norm_g[l2, :], 128, 128), writes=[NG])

            class HeadCtx:
                pass

            def mk_ctx(tag, Tmax, share):
                c = HeadCtx()
                if share is None:
                    c.qT = sb(st, f"qT{tag}", [128, Tmax], BF16)
                    c.kT = sb(st, f"kT{tag}", [128, Tmax], BF16)
                    c.vT = sb(st, f"vT{tag}", [128, Tmax], BF16)
                    c.zs = sb(st, f"zz{tag}", [128, Tmax // 128 if Tmax >= 128 else 1, 128], BF16)
                    c.OTb = sb(st, f"otb{tag}", [128, Tmax], BF16)
                    c.S = sb(st, f"S{tag}", [128, 128], F32)
                    c.Sb = sb(st, f"Sb{tag}", [128, 128], BF16)
                else:
                    for a_ in ("qT", "kT", "vT", "zs", "OTb", "S", "Sb"):
                        setattr(c, a_, getattr(share, a_))

                def f(name, dt=F32, shape=(128, 128)):
                    return sb(st, f"{name}{tag}", list(shape), dt)

                c.gBt = f("gBt")
                c.gcol = f("gcol", F32, (128, 1))
                c.glast = f("glast", F32, (128, 1))
                c.EG = f("EG")
                c.egcol = f("egcol", F32, (128, 1))
                c.ekd = f("ekd", F32, (128, 1))
                c.egC = f("egC", F32, (128, 1))
                c.bg = f("bg", F32, (128, 1))
                c.d1 = f("d1")
                c.d2 = f("d2")
                c.vb = f("vb", F32)
                c.kbg = f("kbg", F32)
                c.kd = f("kd", BF16)
                c.A = f("A", F32)
                c.PT = f("PT", BF16)
                c.M = [f("M0", F32), f("M1", F32)]
                c.MT = [f("MT0", F32), f("MT1", F32)]
                c.Rm = [f("R0", F32), f("R1", F32)]
                c.RT = [f("RT0", F32), f("RT1", F32)]
                c.nwT = f("nwT", BF16)
                c.ub = f("ub", BF16)
                c.qgT = f("qgT", BF16)
                c.jk = f("jk", BF16)
                c.ss = f("ss", F32, (128, 1))
                c.t1 = f("t1")
                c.t2 = f("t2", BF16)
                return c

            SEQ_ATTRS = ("qT", "kT", "vT", "zs", "OTb", "S", "Sb")

            def mk_pair(tag, Tmax):
                c0 = mk_ctx(tag + "0", Tmax, None)
                c1 = mk_ctx(tag + "1", Tmax, c0)
                return [c0, c1]

            def chunk_head(c, C, n, ti, h):
                t0 = n * C
                cs = slice(t0, t0 + C)
                gsrc = GB[:C, ti, h:h + 1]
                beta = GB[:C, ti, 4 + h:5 + h]
                ts("dve", c.gBt, c.gBt[:C, :], onesf[:C, :], gsrc, None, ALU.mult, None, [onesf, GB])
                psG = psf.get()
                mm(psG, psG[:, 0:C], c.gBt[:C, :], Uf[:C, :C], [c.gBt, Uf])
                mm(psG, psG[:C, 256:257], Uf[:C, :C], gsrc, [Uf, GB])
                cp("act", c.gcol, c.gcol[:C, :], psG[:C, 256:257], [psG])
                cp("act", c.glast, c.glast[:, :], psG[:, C - 1:C], [psG])
                act(c.EG, c.EG[:, :C], psG[:, 0:C], AF.Exp, [psG])
                act(c.egcol, c.egcol[:C, :], c.gcol[:C, :], AF.Exp, [c.gcol])
                act(c.ekd, c.ekd[:C, :], c.gcol[:C, :], AF.Exp, [c.gcol, c.glast], bias=c.glast[:C, 0:1], scale=-1.0)
                act(c.egC, c.egC[:, :], c.glast[:, :], AF.Exp, [c.glast])
                tt("dve", c.bg, c.bg[:C, :], beta, c.egcol[:C, :], ALU.mult, [GB, c.egcol])
                stt("dve", c.d1, c.d1[:C, :C], psG[:C, 0:C], c.gcol[:C, 0:1], negls[:C, :C], ALU.subtract, ALU.subtract, [psG, c.gcol, negls])
                act(c.d1, c.d1[:C, :C], c.d1[:C, :C], AF.Exp, [c.d1], scale=-1.0)
                stt("dve", c.d2, c.d2[:C, :C], psG[:C, 0:C], c.gcol[:C, 0:1], negu[:C, :C], ALU.subtract, ALU.add, [psG, c.gcol, negu])
                act(c.d2, c.d2[:C, :C], c.d2[:C, :C], AF.Exp, [c.d2])
                yield
                pb = psb.get()
                tr(pb, pb[:C, 0:128], c.kT[:, cs], identb[:, :], [c.kT, identb], inc=False)
                tr(pb, pb[:C, 128:256], c.vT[:, cs], identb[:, :], [c.vT, identb])
                ts("dve", c.vb, c.vb[:C, :], pb[:C, 128:256], beta, None, ALU.mult, None, [pb, GB])
                ts("dve", c.kbg, c.kbg[:C, :], pb[:C, 0:128], c.bg[:C, 0:1], None, ALU.mult, None, [pb, c.bg])
                ts("dve", c.kd, c.kd[:C, :], pb[:C, 0:128], c.ekd[:C, 0:1], None, ALU.mult, None, [pb, c.ekd])
                yield
                psK = psf.get()
                mm(psK, psK[:C, 0:C], c.kT[:, cs], c.kT[:, cs], [c.kT], inc=False)
                mm(psK, psK[:C, 128:128 + C], c.kT[:, cs], c.qT[:, cs], [c.kT, c.qT])
                stt("dve", c.A, c.A[:C, :C], psK[:C, 0:C], beta, c.d1[:C, :C], ALU.mult, ALU.mult, [psK, GB, c.d1])
                tt("dve", c.PT, c.PT[:C, :C], psK[:C, 128:128 + C], c.d2[:C, :C], ALU.mult, [psK, c.d2])
                yield
                pb2 = psf.get()
                tr(pb2, pb2[:C, 0:C], c.A[:C, :C], identf[:C, :C], [c.A, identf])
                tt("dve", c.RT[0], c.RT[0][:C, :C], identf[:C, :C], pb2[:C, 0:C], ALU.subtract, [identf, pb2])
                cp("act", c.MT[0], c.MT[0][:C, :C], pb2[:C, 0:C], [pb2])
                yield
                M, MT, Rm, RT = c.A, c.MT[0], c.Rm[0], c.RT[0]
                nl = {128: 6, 64: 5}[C]
                for l in range(nl):
                    last = (l == nl - 1)
                    Mn, MTn, Rn, RTn = c.M[l % 2], c.MT[(l + 1) % 2], c.Rm[(l + 1) % 2], c.RT[(l + 1) % 2]
                    psM = psf.get()
                    mm(psM, psM[:C, 0:C], MT[:C, :C], M[:C, :C], [MT, M], inc=last)
                    if not last:
                        mm(psM, psM[:C, 128:128 + C], M[:C, :C], MT[:C, :C], [M, MT])
                    cp("act", Mn, Mn[:C, :C], psM[:C, 0:C], [psM])
                    if not last:
                        cp("dve", MTn, MTn[:C, :C], psM[:C, 128:128 + C], [psM])
                    yield
                    psR = psf.get()
                    mm(psR, psR[:C, 0:C], Mn[:C, :C], RT[:C, :C], [Mn, RT])
                    tt("dve", RTn, RTn[:C, :C], psR[:C, 0:C], RT[:C, :C], ALU.add, [psR, RT])
                    M, MT, Rm, RT = Mn, MTn, Rn, RTn
                    yield
                TT = RT
                c.TT = TT

            def chunk_tail(c, C, n, ti, h):
                t0 = n * C
                cs = slice(t0, t0 + C)
                TT = c.TT
                psW = psf.get()
                mm(psW, psW[:, 0:C], c.kbg[:C, :], TT[:C, :C], [c.kbg, TT])
                P.op("act", lambda e: e.mul(c.nwT[:, :C], psW[:, 0:C], -1.0), reads=[psW], writes=[c.nwT])
                psU = psf.get()
                mm(psU, psU[:C, 0:128], TT[:C, :C], c.vb[:C, :], [TT, c.vb], start=True, stop=False, inc=False)
                mm(psU, psU[:C, 0:128], c.nwT[:, :C], c.Sb[:, :], [c.nwT, c.Sb], start=False, stop=True)
                cp("dve", c.ub, c.ub[:C, :], psU[:C, 0:128], [psU])
                yield
                tt("pool", c.qgT, c.qgT[:, :C], c.qT[:, cs], c.EG[:, :C], ALU.mult, [c.qT, c.EG])
                psO = psf.get()
                mm(psO, psO[:C, 0:128], c.qgT[:, :C], c.Sb[:, :], [c.qgT, c.Sb], start=True, stop=False, inc=False)
                mm(psO, psO[:C, 0:128], c.PT[:C, :C], c.ub[:C, :], [c.PT, c.ub], start=False, stop=True)
                psS = psf.get()
                mm(psS, psS[:, 0:128], c.kd[:C, :], c.ub[:C, :], [c.kd, c.ub])
                stt("dve", c.S, c.S[:, :], c.S[:, :], c.egC[:, 0:1], psS[:, 0:128], ALU.mult, ALU.add, [c.S, c.egC, psS])
                cp("act", c.Sb, c.Sb[:, :], c.S[:, :], [c.S])
                act(c.jk, c.jk[:C, :], psO[:C, 0:128], AF.Square, [psO], accum=(c.ss, c.ss[:C, 0:1]))
                rsqrt_("dve", c.ss, c.ss[:C, 0:1], 1.0 / 128, EPS)
                stt("dve", c.t1, c.t1[:C, :], psO[:C, 0:128], c.ss[:C, 0:1], NG[:C, :], ALU.mult, ALU.mult, [psO, c.ss, NG])
                tt("pool", c.t2, c.t2[:C, :], c.t1[:C, :], c.zs[:C, n, :], ALU.mult, [c.t1, c.zs])
                pb3 = psb.get()
                tr(pb3, pb3[:, 0:C], c.t2[:C, :], identb[:C, :C], [c.t2, identb])
                cp("act", c.OTb, c.OTb[:, cs], pb3[:, 0:C], [pb3])

            def run_part(pairs, si, heads, C, tp0, Tpart, first, last_part):
                row0, T = SEQS[si]
                r0 = row0 + tp0
                for pr, h in zip(pairs, heads):
                    c = pr[0]
                    P.dma("sp", c.qT[:, 0:Tpart], QKVT[h * 128:(h + 1) * 128, r0:r0 + Tpart], writes=[c.qT])
                    P.dma("sp", c.kT[:, 0:Tpart], QKVT[(4 + h) * 128:(5 + h) * 128, r0:r0 + Tpart], writes=[c.kT])
                    P.dma("sp", c.vT[:, 0:Tpart], QKVT[(8 + h) * 128:(9 + h) * 128, r0:r0 + Tpart], writes=[c.vT])
                    P.dma("sp", c.zs[:C, 0:Tpart // C, :], Z[r0:r0 + Tpart, h * 128:(h + 1) * 128].rearrange("(n p) d -> p n d", p=C),
                          writes=[c.zs])
                    if first:
                        P.dma("sp", c.S[:, :], S0[l2, si, h], writes=[c.S])
                        cp("act", c.Sb, c.Sb[:, :], c.S[:, :], [c.S])
                nch = Tpart // C

                def tix(n):
                    return (tp0 // 128 + n) if si == 0 else 32 + (si - 1)

                def rr(gens):
                    live = list(gens)
                    while live:
                        nxt = []
                        for g in live:
                            try:
                                next(g)
                                nxt.append(g)
                            except StopIteration:
                                pass
                        live = nxt

                rr([chunk_head(pr[0], C, 0, tix(0), h) for pr, h in zip(pairs, heads)])
                for n in range(nch):
                    gens = [chunk_tail(pr[n % 2], C, n, tix(n), h) for pr, h in zip(pairs, heads)]
                    if n + 1 < nch:
                        gens += [chunk_head(pr[(n + 1) % 2], C, n + 1, tix(n + 1), h) for pr, h in zip(pairs, heads)]
                    rr(gens)
                for pr, h in zip(pairs, heads):
                    c = pr[0]
                    P.dma("sp", OT[h * 128:(h + 1) * 128, r0:r0 + Tpart], c.OTb[:, 0:Tpart], reads=[c.OTb])
                    if last_part:
                        P.dma("sp", gdn_S[l2, si, h], c.S[:, :], reads=[c.S])

            QTR = TP // 4
            pairs = [mk_pair(t, QTR) for t in "abcd"]
            if not OPTS.get('gdn_sample_only'):
                for qtr in range(4):
                    run_part(pairs, 0, [0, 1, 2, 3], 128, qtr * QTR, QTR, qtr == 0, qtr == 3)
            for s in range(OPTS.get('gdn_nss', NSS)):
                run_part(pairs, 1 + s, [0, 1, 2, 3], 64, 0, TS, True, True)
            P.flush()

    def phase_out(L, w_out2d):
        with ExitStack() as st:
            psf = Pool_([pst(st, f"pof{i}", F32) for i in range(6)])
            psb = Pool_([pst(st, f"pob{i}", BF16) for i in range(2)])
            wo = sb(st, "wo", [128, 8, D], BF16)
            load_w(wo, w_out2d, 0, 512, 0)
            load_w(wo, w_out2d, 512, 512, 512)
            gB = load_gain(st, "g2", norm_ffn_g[L, :])
            OTr = OT.rearrange("(k p) r -> p k r", p=128)
            for (g0, gn, gtiles) in GROUPS:
                P.dma("sp", HT[:, :, g0:g0 + gn], OTr[:, :, g0:g0 + gn], writes=[HTt[i] for i in gtiles])
            xp = Pool_([sb(st, f"xo{i}", [128, D], F32) for i in range(2)])
            xnp = Pool_([sb(st, f"xn{i}", [128, D], F32) for i in range(3)])
            pools = norm_pools(st, psb)
            xsrc = x_all if L == 0 else X
            pend = None
            for ti, (r0, n) in enumerate(TILES):
                xt = xp.get()
                P.dma("sp", xt[:n, :], xsrc[r0:r0 + n, :], writes=[xt])
                xn = xnp.get()
                for half in range(2):
                    ps = psf.get()
                    for kc in range(8):
                        mm(ps, ps[:n, :], HT[:, kc, r0:r0 + n], wo[:, kc, half * 512:(half + 1) * 512], [HTt[ti], wo],
                           start=(kc == 0), stop=(kc == 7), inc=(kc == 7))
                    tt("dve", xn, xn[:n, half * 512:(half + 1) * 512], ps[:n, :], xt[:n, half * 512:(half + 1) * 512], ALU.add, [ps, xt])
                P.dma("sp", X[r0:r0 + n, :], xn[:n, :], reads=[xn])
                if pend is not None:
                    norm_b(pend[0], pend[1], pools)
                pend = (ti, norm_a(ti, xn, gB, pools))
            norm_b(pend[0], pend[1], pools)
            P.flush()

    def phase_ffn1(L):
        with ExitStack() as st:
            psf = Pool_([pst(st, f"pff{i}", F32) for i in range(8)])
            wgs = Pool_([sb(st, f"wg{i}", [128, 8, 512], BF16) for i in range(2)])
            wus = Pool_([sb(st, f"wu{i}", [128, 8, 512], BF16) for i in range(2)])
            sgp = Pool_([sb(st, f"sg{i}", [128, 512], F32) for i in range(3)])
            abp = Pool_([sb(st, f"ab{i}", [128, 512], BF16) for i in range(4)])
            for fb in range(6):
                f0 = fb * 512
                nf = min(512, DFF - f0)
                wg = wgs.get()
                wu = wus.get()
                load_w(wg, ffn_w_gate[L], f0, nf)
                load_w(wu, ffn_w_up[L], f0, nf)
                for (g0, gn, gtiles) in GROUPS:
                    hts = [HTt[i] for i in gtiles]
                    for c in range(nf // 128):
                        psg = psf.get()
                        for kc in range(8):
                            mm(psg, psg[:, 0:gn], wg[:, kc, c * 128:(c + 1) * 128], HT[:, kc, g0:g0 + gn], [wg] + hts,
                               start=(kc == 0), stop=(kc == 7), inc=(kc == 7))
                        psu = psf.get()
                        for kc in range(8):
                            mm(psu, psu[:, 0:gn], wu[:, kc, c * 128:(c + 1) * 128], HT[:, kc, g0:g0 + gn], [wu] + hts,
                               start=(kc == 0), stop=(kc == 7), inc=(kc == 7))
                        sg = sgp.get()
                        act(sg, sg[:, 0:gn], psg[:, 0:gn], AF.Silu, [psg])
                        ab = abp.get()
                        tt("dve", ab, ab[:, 0:gn], psu[:, 0:gn], sg[:, 0:gn], ALU.mult, [psu, sg])
                        P.dma("sp", ACTS[f0 + c * 128:f0 + (c + 1) * 128, g0:g0 + gn], ab[:, 0:gn], reads=[ab])
            P.flush()

    def phase_ffn2(L):
        with ExitStack() as st:
            psf = Pool_([pst(st, f"pdf{i}", F32) for i in range(6)])
            psb = Pool_([pst(st, f"pdb{i}", BF16) for i in range(2)])
            wd = sb(st, "wd", [128, 22, D], BF16)
            wdr = ffn_w_down[L].rearrange("(c p) d -> p c d", p=128)
            for i in range(0, 22, 4):
                j = min(22, i + 4)
                P.dma("pool", wd[:, i:j, :], wdr[:, i:j, :], writes=[wd])
            last = (L == 3)
            gB = None if last else load_gain(st, "g3", norm_mix_g[L + 1, :])
            abp = Pool_([sb(st, f"a2{i}", [128, 22, 512], BF16) for i in range(2)])
            xp = Pool_([sb(st, f"xd{i}", [128, D], F32) for i in range(2)])
            xnp = Pool_([sb(st, f"xm{i}", [128, D], F32) for i in range(3)])
            pools = norm_pools(st, psb)
            ACr = ACTS.rearrange("(c p) r -> p c r", p=128)
            pend = None
            for (g0, gn, gtiles) in GROUPS:
                ab = abp.get()
                P.dma("sp", ab[:, 0:11, 0:gn], ACr[:, 0:11, g0:g0 + gn], writes=[ab])
                P.dma("sp", ab[:, 11:22, 0:gn], ACr[:, 11:22, g0:g0 + gn], writes=[ab])
                for ti in gtiles:
                    r0, n = TILES[ti]
                    off = r0 - g0
                    xt = xp.get()
                    P.dma("sp", xt[:n, :], X[r0:r0 + n, :], writes=[xt])
                    xn = xnp.get()
                    for half in range(2):
                        ps = psf.get()
                        for c in range(22):
                            mm(ps, ps[:n, :], ab[:, c, off:off + n], wd[:, c, half * 512:(half + 1) * 512], [ab, wd],
                               start=(c == 0), stop=(c == 21), inc=(c == 21))
                        tt("dve", xn, xn[:n, half * 512:(half + 1) * 512], ps[:n, :], xt[:n, half * 512:(half + 1) * 512], ALU.add, [ps, xt])
                    if last:
                        P.dma("sp", y_all[r0:r0 + n, :], xn[:n, :], reads=[xn])
                    else:
                        P.dma("sp", X[r0:r0 + n, :], xn[:n, :], reads=[xn])
                        if pend is not None:
                            norm_b(pend[0], pend[1], pools)
                        pend = (ti, norm_a(ti, xn, gB, pools))
            if pend is not None:
                norm_b(pend[0], pend[1], pools)
            P.flush()

    def phase_proj_odd(l2):
        W = w_in_odd[l2]
        with ExitStack() as st:
            psf = Pool_([pst(st, f"ppf{i}", F32) for i in range(6)])
            psb = Pool_([pst(st, f"ppb{i}", BF16) for i in range(2)])
            wbs = Pool_([sb(st, f"wq{i}", [128, 8, 520], BF16) for i in range(3)])
            QG = sb(st, "QG", [128, 128], F32)
            KG = sb(st, "KG", [128, 128], F32)
            bfB = sb(st, "bfB", [128, 8], F32)
            P.dma("sp", QG[:, :], bcast(fox_q_norm_g[l2, :], 128, 128), writes=[QG])
            P.dma("sp", KG[:, :], bcast(fox_k_norm_g[l2, :], 128, 128), writes=[KG])
            P.dma("sp", bfB[:, :], bcast(fox_b_f[l2, :], 128, 8), writes=[bfB])
            ts("dve", QG, QG[:, :], QG[:, :], 128.0 ** -0.5, None, ALU.mult, None, [QG])
            KTb = sb(st, "KTb", [128, 4, R], BF16)
            KTbt = [Tl(KTb.t, f"KTb{i}") for i in range(len(TILES))]
            LFR = sb(st, "LFR", [128, 36, 8], F32)
            P.op("pool", lambda e: e.memset(LFR[:, :, :], 0.0), writes=[LFR])
            LF = sb(st, "LF", [128, 36, 8], F32)
            jkp = Pool_([sb(st, f"pj{i}", [128, 128], BF16) for i in range(2)])
            ssp = Pool_([sb(st, f"pss{i}", [128, 4], F32) for i in range(3)])
            qnp = Pool_([sb(st, f"qn{i}", [128, 512], BF16) for i in range(4)])
            knp = Pool_([sb(st, f"kn{i}", [128, 512], F32) for i in range(3)])
            wq = [wbs.get()]
            load_w(wq[0], W, 0, 512)
            for blk in range(6):
                kind = blk // 2
                wb = wq[blk]
                if blk + 1 < 6:
                    wq.append(wbs.get())
                    load_w(wq[blk + 1], W, (blk + 1) * 512, 520 if blk + 1 == 5 else 512)
                stt_ = {}

                def stage0(ti, blk=blk, wb=wb):
                    r0, n = TILES[ti]
                    ps = psf.get()
                    for kc in range(8):
                        mm(ps, ps[:n, :], HT[:, kc, r0:r0 + n], wb[:, kc, 0:512], [HTt[ti], wb],
                           start=(kc == 0), stop=(kc == 7), inc=(kc == 7))
                    stt_[ti] = {"ps": ps}
                    if blk == 5:
                        ps2 = psf.get()
                        for kc in range(8):
                            mm(ps2, ps2[:n, 0:8], HT[:, kc, r0:r0 + n], wb[:, kc, 512:520], [HTt[ti], wb],
                               start=(kc == 0), stop=(kc == 7), inc=(kc == 7))
                        stt_[ti]["ps2"] = ps2

                def stage1(ti, blk=blk, kind=kind):
                    r0, n = TILES[ti]
                    ps = stt_[ti]["ps"]
                    if blk == 5:
                        ps2 = stt_[ti]["ps2"]
                        cp("act", LFR, LFR[:n, ti, :], ps2[:n, 0:8], [ps2])
                    if kind < 2:
                        ss = ssp.get()
                        for h in range(4):
                            jk = jkp.get()
                            act(jk, jk[:n, :], ps[:n, h * 128:(h + 1) * 128], AF.Square, [ps], accum=(ss, ss[:n, h:h + 1]))
                        rsqrt_("dve", ss, ss[:n, :], 1.0 / 128, EPS)
                        G = QG if kind == 0 else KG
                        if kind == 0:
                            qn = qnp.get()
                            for h in range(4):
                                stt("dve", qn, qn[:n, h * 128:(h + 1) * 128], ps[:n, h * 128:(h + 1) * 128], ss[:n, h:h + 1], G[:n, :],
                                    ALU.mult, ALU.mult, [ps, ss, G])
                        else:
                            kn = knp.get()
                            for h in range(4):
                                stt("dve", kn, kn[:n, h * 128:(h + 1) * 128], ps[:n, h * 128:(h + 1) * 128], ss[:n, h:h + 1], G[:n, :],
                                    ALU.mult, ALU.mult, [ps, ss, G])
                            P.dma("sp", fox_k[l2, r0:r0 + n, (blk - 2) * 512:(blk - 1) * 512], kn[:n, :], reads=[kn])
                            qn = qnp.get()
                            cp("pool", qn, qn[:n, :], kn[:n, :], [kn])
                        stt_[ti]["qn"] = qn
                    else:
                        vf = knp.get()
                        cp("act", vf, vf[:n, :], ps[:n, :], [ps])
                        P.dma("sp", fox_v[l2, r0:r0 + n, (blk - 4) * 512:(blk - 3) * 512], vf[:n, :], reads=[vf])
                        vb = qnp.get()
                        cp("dve", vb, vb[:n, :], ps[:n, :], [ps])
                        P.dma("sp", V[r0:r0 + n, (blk - 4) * 512:(blk - 3) * 512], vb[:n, :], reads=[vb])

                def stage2(ti, kind=kind):
                    r0, n = TILES[ti]
                    st_ = stt_.pop(ti)
                    if kind < 2:
                        qn = st_["qn"]
                        pb = psb.get()
                        for h in range(4):
                            tr(pb, pb[:, h * 128:h * 128 + n], qn[:n, h * 128:(h + 1) * 128], identb[:n, :n], [qn, identb], inc=(h == 3))
                        src = pb.t[:, 0:512].rearrange("p (k m) -> p k m", k=4)[:, :, 0:n]
                        cp("act", KTbt[ti], KTb[:, :, r0:r0 + n], src, [pb])

                NT = len(TILES)
                for t in range(NT + 2):
                    if t < NT:
                        stage0(t)
                    if 0 <= t - 2 < NT:
                        stage2(t - 2)
                    if 0 <= t - 1 < NT:
                        stage1(t - 1)
                if kind < 2:
                    dst = (QT if kind == 0 else KT)[(blk % 2) * 512:(blk % 2 + 1) * 512, :].rearrange("(h p) r -> p h r", p=128)
                    for hh in range(4):
                        P.dma("sp", dst[:, hh, :], KTb[:, hh, :], reads=KTbt)
            for h in range(8):
                ts("dve", LFR, LFR[:, :, h], LFR[:, :, h], bfB[:, h:h + 1], None, ALU.add, None, [LFR, bfB])
            t2 = sb(st, "lt2", [128, 36, 8], F32)
            abs_(t2, t2[:, :, :], LFR, LFR[:, :, :])
            act(t2, t2[:, :, :], t2[:, :, :], AF.Exp, [t2], scale=-1.0)
            act(t2, t2[:, :, :], t2[:, :, :], AF.Ln, [t2], bias=1.0)
            ts("dve", LF, LF[:, :, :], LFR[:, :, :], 0.0, None, ALU.min, None, [LFR])
            tt("dve", LF, LF[:, :, :], LF[:, :, :], t2[:, :, :], ALU.subtract, [LF, t2])
            for ti, (r0, n) in enumerate(TILES):
                P.dma("sp", fox_logf[l2, r0:r0 + n, :], LF[:n, ti, :], reads=[LF])
            CUM = sb(st, "CUM", [128, 32, 8], F32)
            TOT = sb(st, "TOT", [128, 32, 8], F32)
            carry = sb(st, "carry", [128, 8], F32)
            psc = psf.get()
            for r in range(32):
                mm(psc, psc[:, r * 16:r * 16 + 8], Uf[:, :], LF[:, r, :], [Uf, LF], inc=False)
                mm(psc, psc[:, r * 16 + 8:r * 16 + 16], onesf[:, :], LF[:, r, :], [onesf, LF], inc=(r == 31))
            pv = psc.t[:, :].rearrange("p (r c) -> p r c", c=16)
            cp("act", CUM, CUM[:, :, :], pv[:, :, 0:8], [psc])
            cp("dve", TOT, TOT[:, :, :], pv[:, :, 8:16], [psc])
            P.op("dve", lambda e: e.memset(carry[:, :], 0.0), writes=[carry])
            for r in range(32):
                tt("dve", CCp, CCp[:, r, :], CUM[:, r, :], carry[:, :], ALU.add, [CUM, carry])
                tt("dve", carry, carry[:, :], carry[:, :], TOT[:, r, :], ALU.add, [carry, TOT])
                if r % 4 == 3:
                    cp("dve", CEp, CEp[:, r // 4, :], carry[:, :], [carry])
            CL = sb(st, "CL", [128, 8, 8], F32)
            CUMs = sb(st, "CUMs", [128, 9, 8], F32)
            TOTs = sb(st, "TOTs", [128, 9, 8], F32)
            for s in range(NSS):
                P.dma("sp", CL[:, :, :], clf[l2, s].rearrange("(b p) h -> p b h", p=128), writes=[CL])
                psc = psf.get()
                for b in range(8):
                    mm(psc, psc[:, b * 16:b * 16 + 8], Uf[:, :], CL[:, b, :], [Uf, CL], inc=False)
                    mm(psc, psc[:, b * 16 + 8:b * 16 + 16], onesf[:, :], CL[:, b, :], [onesf, CL], inc=False)
                mm(psc, psc[:TS, 128:136], Uf[:TS, :TS], LF[:TS, 32 + s, :], [Uf, LF], inc=False)
                mm(psc, psc[:, 136:144], onesf[:TS, :], LF[:TS, 32 + s, :], [onesf, LF])
                pv = psc.t[:, 0:144].rearrange("p (r c) -> p r c", c=16)
                cp("act", CUMs, CUMs[:, :, :], pv[:, :, 0:8], [psc])
                cp("dve", TOTs, TOTs[:, :, :], pv[:, :, 8:16], [psc])
                P.op("dve", lambda e: e.memset(carry[:, :], 0.0), writes=[carry])
                for b in range(9):
                    tt("dve", CCs[s], CCs[s][:, b, :], CUMs[:, b, :], carry[:, :], ALU.add, [CUMs, carry])
                    tt("dve", carry, carry[:, :], carry[:, :], TOTs[:, b, :], ALU.add, [carry, TOTs])
                cp("dve", CEs[s], CEs[s][:, :], carry[:, :], [carry])
            P.flush()

    def phase_attn_prompt(l2):
        with ExitStack() as st:
            psS_ = Pool_([pst(st, f"paS{i}", F32) for i in range(4)])
            psA = [[pst(st, f"paO{i}", F32), pst(st, f"paL{i}", F32)] for i in range(2)]
            KTh = Pool_([sb(st, f"KTh{i}", [128, TP], BF16) for i in range(2)])
            QTh = Pool_([sb(st, f"QTh{i}", [128, TP], BF16) for i in range(2)])
            Vh = Pool_([sb(st, f"Vh{i}", [128, 32, 128], BF16) for i in range(2)])
            OTb = Pool_([sb(st, f"aot{i}", [128, TP], BF16) for i in range(2)])
            BIp = Pool_([sb(st, f"BI{i}", [128, 32], F32) for i in range(3)])
            ptp = Pool_([sb(st, f"pt{i}", [128, 512], BF16) for i in range(4)])
            rlp = Pool_([sb(st, f"rl{i}", [128, 512], F32) for i in range(2)])
            qi = 0
            for h in range(8):
                kt, qt, vh, ot = KTh.get(), QTh.get(), Vh.get(), OTb.get()
                P.dma("sp", kt[:, :], KT[h * 128:(h + 1) * 128, 0:TP], writes=[kt])
                P.dma("sp", qt[:, :], QT[h * 128:(h + 1) * 128, 0:TP], writes=[qt])
                P.dma("sp", vh[:, :, :], V[0:TP, h * 128:(h + 1) * 128].rearrange("(b p) d -> p b d", p=128), writes=[vh])
                items = [(Q, j) for Q in range(8) for j in range(4 * (Q + 1))]
                stA = {}
                acc = {}

                def stageA(k, kt=kt, qt=qt, h=h):
                    Q, j = items[k]
                    J = 4 * (Q + 1)
                    if j == 0:
                        BI = BIp.get()
                        ts("dve", BI, BI[:, 0:J], CCp[:, 0:J, h], CEp[:, Q, h:h + 1], -1.0, ALU.subtract, ALU.mult, [CCp, CEp])
                        stA["BI"] = BI
                    BI = stA["BI"]
                    diag = j >= 4 * Q
                    qlo = (j - 4 * Q) * 128 if diag else 0
                    ps = psS_.get()
                    mm(ps, ps[:, qlo:512], kt[:, j * 128:(j + 1) * 128], qt[:, Q * 512 + qlo:(Q + 1) * 512], [kt, qt])
                    pt = ptp.get()
                    act(pt, pt[:, qlo:512], ps[:, qlo:512], AF.Exp, [ps, BI], bias=BI[:, j:j + 1])
                    if diag:
                        tt("pool", pt, pt[:, qlo:qlo + 128], pt[:, qlo:qlo + 128], maskb[:, :], ALU.mult, [pt, maskb])
                    stA[k] = (pt, qlo)

                def stageB(k, vh=vh, ot=ot):
                    nonlocal qi
                    Q, j = items[k]
                    J = 4 * (Q + 1)
                    pt, qlo = stA.pop(k)
                    if j == 0:
                        acc["p"] = psA[qi % 2]
                        qi += 1
                    psO, psL = acc["p"]
                    mm(psO, psO[:, qlo:512], vh[:, j, :], pt[:, qlo:512], [vh, pt], start=(j == 0), stop=(j == J - 1), inc=(j == J - 1))
                    mm(psL, psL[:, qlo:512], onesb[:, :], pt[:, qlo:512], [onesb, pt], start=(j == 0), stop=(j == J - 1), inc=(j == J - 1))
                    if j == J - 1:
                        rl = rlp.get()
                        P.op("dve", lambda e, rl=rl, psL=psL: e.reciprocal(rl[:, :], psL[:, :]), reads=[psL], writes=[rl])
                        tt("dve", ot, ot[:, Q * 512:(Q + 1) * 512], psO[:, :], rl[:, :], ALU.mult, [psO, rl])

                LA = 2
                for k in range(len(items) + LA):
                    if k < len(items):
                        stageA(k)
                    if k >= LA:
                        stageB(k - LA)
                P.dma("sp", OT[h * 128:(h + 1) * 128, 0:TP], ot[:, :], reads=[ot])
            P.flush()

    def phase_attn_sample(l2):
        with ExitStack() as st:
            psf = Pool_([pst(st, f"psf{i}", F32) for i in range(6)])
            psb = Pool_([pst(st, f"psb{i}", BF16) for i in range(2)])
            KCp = Pool_([sb(st, f"KC{i}", [128, 8, D], BF16) for i in range(2)])
            VCp = Pool_([sb(st, f"VC{i}", [128, 8, D], BF16) for i in range(2)])
            KTc = Pool_([sb(st, f"KTc{i}", [128, 1024], BF16) for i in range(3)])
            KNp = Pool_([sb(st, f"KN{i}", [128, 8, TS], BF16) for i in range(2)])
            QNp = Pool_([sb(st, f"QN{i}", [128, 8, TS], BF16) for i in range(2)])
            VNp = Pool_([sb(st, f"VN{i}", [128, D], BF16) for i in range(2)])
            OTs = Pool_([sb(st, f"OTs{i}", [128, 8, TS], BF16) for i in range(2)])
            BIp = Pool_([sb(st, f"BIs{i}", [128, 9], F32) for i in range(3)])
            ptp = Pool_([sb(st, f"pts{i}", [128, 9 * TS], BF16) for i in range(3)])
            rlp = Pool_([sb(st, f"rls{i}", [128, TS], F32) for i in range(2)])
            KTr = KT.rearrange("(h p) r -> p h r", p=128)
            QTr = QT.rearrange("(h p) r -> p h r", p=128)
            for s in range(NSS):
                row0 = TP + TS * s
                kc_, vc_ = KCp.get(), VCp.get()
                for b0 in range(0, 8, 2):
                    P.dma("pool", kc_[:, b0:b0 + 2, :], ck[l2, s].rearrange("(b p) f -> p b f", p=128)[:, b0:b0 + 2, :], writes=[kc_])
                    P.dma("pool", vc_[:, b0:b0 + 2, :], cv[l2, s].rearrange("(b p) f -> p b f", p=128)[:, b0:b0 + 2, :], writes=[vc_])
                kn, qn, vn, ots = KNp.get(), QNp.get(), VNp.get(), OTs.get()
                P.dma("sp", kn[:, :, :], KTr[:, :, row0:row0 + TS], writes=[kn])
                P.dma("sp", qn[:, :, :], QTr[:, :, row0:row0 + TS], writes=[qn])
                P.dma("sp", vn[:TS, :], V[row0:row0 + TS, :], writes=[vn])
                for h in range(8):
                    pb = psb.get()
                    for b in range(8):
                        tr(pb, pb[:, b * 128:(b + 1) * 128], kc_[:, b, h * 128:(h + 1) * 128], identb[:, :], [kc_, identb], inc=(b == 7))
                    ktc = KTc.get()
                    cp("dve", ktc, ktc[:, :], pb[:, :], [pb])
                    BI = BIp.get()
                    ts("dve", BI, BI[:, 0:9], CCs[s][:, 0:9, h], CEs[s][:, h:h + 1], -1.0, ALU.subtract, ALU.mult, [CCs[s], CEs[s]])
                    ps = psf.get()
                    for b in range(8):
                        mm(ps, ps[:, b * TS:(b + 1) * TS], ktc[:, b * 128:(b + 1) * 128], qn[:, h, :], [ktc, qn], inc=(b == 7))
                    ps2 = psf.get()
                    mm(ps2, ps2[:TS, 0:TS], kn[:, h, :], qn[:, h, :], [kn, qn])
                    pt = ptp.get()
                    for b in range(8):
                        act(pt, pt[:, b * TS:(b + 1) * TS], ps[:, b * TS:(b + 1) * TS], AF.Exp, [ps, BI], bias=BI[:, b:b + 1])
                    act(pt, pt[:TS, 8 * TS:9 * TS], ps2[:TS, 0:TS], AF.Exp, [ps2, BI], bias=BI[:TS, 8:9])
                    tt("pool", pt, pt[:TS, 8 * TS:9 * TS], pt[:TS, 8 * TS:9 * TS], maskb[:TS, :TS], ALU.mult, [pt, maskb])
                    psO = psf.get()
                    psL = psf.get()
                    for b in range(8):
                        mm(psO, psO[:, 0:TS], vc_[:, b, h * 128:(h + 1) * 128], pt[:, b * TS:(b + 1) * TS], [vc_, pt],
                           start=(b == 0), stop=False, inc=False)
                    mm(psO, psO[:, 0:TS], vn[:TS, h * 128:(h + 1) * 128], pt[:TS, 8 * TS:9 * TS], [vn, pt], start=False, stop=True)
                    for b in range(8):
                        mm(psL, psL[:, 0:TS], onesb[:, :], pt[:, b * TS:(b + 1) * TS], [onesb, pt], start=(b == 0), stop=False, inc=False)
                    mm(psL, psL[:, 0:TS], onesb[:TS, :], pt[:TS, 8 * TS:9 * TS], [onesb, pt], start=False, stop=True)
                    rl = rlp.get()
                    P.op("dve", lambda e, rl=rl, psL=psL: e.reciprocal(rl[:, :], psL[:, 0:TS]), reads=[psL], writes=[rl])
                    tt("dve", ots, ots[:, h, :], psO[:, 0:TS], rl[:, :], ALU.mult, [psO, rl])
                P.dma("sp", OT.rearrange("(h p) r -> p h r", p=128)[:, :, row0:row0 + TS], ots[:, :, :], reads=[ots])
            P.flush()

    plan = [("p0", phase0)]
    for L in range(4):
        l2 = L // 2
        if L % 2 == 0:
            plan.append((f"pe{L}", lambda l2=l2: phase_proj_even(l2)))
            plan.append((f"gdn{L}", lambda l2=l2: phase_gdn(l2)))
            plan.append((f"out{L}", lambda L=L, l2=l2: phase_out(L, w_out_even[l2])))
        else:
            plan.append((f"po{L}", lambda l2=l2: phase_proj_odd(l2)))
            plan.append((f"ap{L}", lambda l2=l2: phase_attn_prompt(l2)))
            plan.append((f"as{L}", lambda l2=l2: phase_attn_sample(l2)))
            plan.append((f"out{L}", lambda L=L, l2=l2: phase_out(L, w_out_odd[l2])))
        plan.append((f"f1_{L}", lambda L=L: phase_ffn1(L)))
        plan.append((f"f2_{L}", lambda L=L: phase_ffn2(L)))
    for name, fn in plan:
        if OPTS.get('only') and name not in OPTS['only']:
            continue
        fn()
        if upto is not None and name == upto:
            break
    if dbg:
        HTd = nc.dram_tensor("HTd", [128, 8, R], BF16, kind="ExternalOutput").ap()
        P.dma("sp", HTd, HT[:, :, :], reads=HTt)
        P.flush()
    P.stack.close()
    return nc


def make_consts():
    i = np.arange(128)
    ident = np.eye(128, dtype=np.float32)
    U = (i[:, None] <= i[None, :]).astype(np.float32)
    ones = np.ones((128, 128), np.float32)
    negls = np.where(i[:, None] > i[None, :], 0.0, -30000.0).astype(np.float32)
    negu = np.where(i[None, :] >= i[:, None], 0.0, -30000.0).astype(np.float32)
    return dict(c_ident=ident, c_U=U, c_ones=ones, c_negls=negls, c_negu=negu)


def make_in_maps(inp):
    f = lambda a: np.ascontiguousarray(np.asarray(a, dtype=np.float32))
    consts = make_consts()
    shared = {k: f(inp[k]) for k in (
        "norm_mix_g", "norm_ffn_g", "w_in_even", "gdn_conv_w", "gdn_a_log", "gdn_dt_bias", "gdn_norm_g",
        "sconv_w", "w_out_even", "w_in_odd", "fox_b_f", "fox_q_norm_g", "fox_k_norm_g", "w_out_odd",
        "ffn_w_gate", "ffn_w_up", "ffn_w_down")}
    shared.update(consts)
    xp = f(inp["x_prompt"]); xs = f(inp["x_sample"])
    ckk = f(inp["cache_fox_k"]).reshape(2, 32, PAST, D)
    cvv = f(inp["cache_fox_v"]).reshape(2, 32, PAST, D)
    clf = f(inp["cache_fox_logf"])
    sS = f(inp["state_gdn_S"]); sc = f(inp["state_gdn_conv"]); ss = f(inp["state_sconv"])
    maps = []
    for c in range(NCORES):
        sl = slice(4 * c, 4 * c + 4)
        m = dict(shared)
        m["x_all"] = np.ascontiguousarray(np.concatenate([xp[c % 4], xs[sl].reshape(NSS * TS, D)], axis=0))
        m["ck"] = np.ascontiguousarray(ckk[:, sl])
        m["cv"] = np.ascontiguousarray(cvv[:, sl])
        m["clf"] = np.ascontiguousarray(clf[:, sl])
        m["S0"] = np.ascontiguousarray(np.concatenate([np.zeros((2, 1, 4, 128, 128), np.float32), sS[:, sl]], axis=1))
        m["conv0"] = np.ascontiguousarray(np.concatenate([np.zeros((2, 1, 3, 1536), np.float32), sc[:, sl]], axis=1))
        m["sc0"] = np.ascontiguousarray(np.concatenate([np.zeros((2, 1, 2, 512), np.float32), ss[:, sl]], axis=1))
        maps.append(m)
    return maps


def kernel(**inp):
    nc = build()
    maps = make_in_maps(inp)
    res = run_bass_kernel_spmd(nc, maps, core_ids=list(range(NCORES))).results
    g = lambda k, c: np.asarray(res[c][k], dtype=np.float32)
    y_p = np.stack([g("y_all", c)[:TP] for c in range(4)])
    y_s = np.concatenate([g("y_all", c)[TP:].reshape(NSS, TS, D) for c in range(8)])
    def fox(k, last):
        p = np.stack([g(k, c)[:, :TP] for c in range(4)], axis=1)
        s = np.concatenate([g(k, c)[:, TP:].reshape(2, NSS, TS, last) for c in range(8)], axis=1)
        return p, s
    pk, sk = fox("fox_k", D); pv, sv = fox("fox_v", D); pl, sl_ = fox("fox_logf", 8)
    pk = pk.reshape(2, 4, TP, 8, 128); sk = sk.reshape(2, 32, TS, 8, 128)
    pv = pv.reshape(2, 4, TP, 8, 128); sv = sv.reshape(2, 32, TS, 8, 128)
    def st(k):
        p = np.stack([g(k, c)[:, 0] for c in range(4)], axis=1)
        s = np.concatenate([g(k, c)[:, 1:] for c in range(8)], axis=1)
        return p, s
    pS, sS = st("gdn_S"); pc, scv = st("gdn_conv"); psc, ssc = st("sconv_o")
    return (y_p, y_s, pk, pv, pl, pS, pc, psc, sk, sv, sl_, sS, scv, ssc)
```
